# Optimizing a Trainium2 kernel written in Bass

```python
import math
import jax
import jax.numpy as jnp
from jax import lax
import numpy as np

D_MODEL = 1024
BATCH = 32
SEQ = 256
DEPTH = 2
DEC_BATCH = 2
DEC_SEQ = 4096
PAST_LEN = 512

GRID_W = 64
MIX_W = 1024
GROUP_W = 256
DIFF_HEADS = 4
DIFF_QK = 32
DIFF_V = 64
DIFF_REP = 2
FNET_GROUPS = 4
FNET_CH = 64
MLA_HEADS = 4
MLA_Q_RANK = 192
MLA_KV_RANK = 128
MLA_NOPE = 64
MLA_ROPE = 32
MLA_V = 64
GQA_HEADS = 4
GQA_KV_HEADS = 2
GQA_DIM = 64
GQA_REP = GQA_HEADS // GQA_KV_HEADS
D_FF = 4 * D_MODEL
N_MOD = 6
Q_BLOCK = 128
ROPE_THETA = 10000.0
EPS = 1e-6
DIFF_SCALE = DIFF_QK ** -0.5
MLA_SCALE = (MLA_NOPE + MLA_ROPE) ** -0.5
GQA_SCALE = GQA_DIM ** -0.5
SPLITS = (DIFF_HEADS * 2 * DIFF_QK, DIFF_HEADS * 2 * DIFF_QK, DIFF_HEADS * DIFF_V,
          FNET_GROUPS * FNET_CH, MLA_Q_RANK, MLA_KV_RANK, MLA_ROPE,
          GQA_HEADS * GQA_DIM, GQA_KV_HEADS * GQA_DIM, GQA_KV_HEADS * GQA_DIM)
IN_W = 1888

kernel_name = "hybrid_diffusion_parallel_groups_step"


def rmsnorm(x, g):
    xf = x.astype(jnp.float32)
    y = xf * lax.rsqrt(jnp.mean(xf * xf, axis=-1, keepdims=True) + EPS)
    return (y * g.astype(jnp.float32)).astype(x.dtype)


def adaln(c, w, b):
    mods = jax.nn.silu(c) @ w + b
    return jnp.split(mods, N_MOD, axis=-1)


def modulate(x, g, shift, scale):
    return rmsnorm(x, g) * (1.0 + scale) + shift


def rope_2d(x, row, col):
    d = x.shape[-1]
    a = d // 2
    inv = ROPE_THETA ** (-jnp.arange(0, a, 2, dtype=jnp.float32) / a)

    def rot(xa, pos):
        ang = pos.astype(jnp.float32)[:, None] * inv[None, :]
        cos = jnp.cos(ang)[None, :, None, :]
        sin = jnp.sin(ang)[None, :, None, :]
        x1 = xa[..., : a // 2].astype(jnp.float32)
        x2 = xa[..., a // 2:].astype(jnp.float32)
        return jnp.concatenate([x1 * cos - x2 * sin, x1 * sin + x2 * cos], axis=-1)

    out = jnp.concatenate([rot(x[..., :a], row), rot(x[..., a:], col)], axis=-1)
    return out.astype(x.dtype)


def blocked_attention(parts, scale):
    b, s, h, _ = parts[0][0].shape
    dv = parts[0][2].shape[-1]
    nb = s // Q_BLOCK
    q_blocks = tuple(jnp.moveaxis(q.reshape(b, nb, Q_BLOCK, h, q.shape[-1]), 1, 0) for q, _, _ in parts)
    keys = [k for _, k, _ in parts]
    vals = [v for _, _, v in parts]
    sizes = [k.shape[1] for k in keys]

    def one_block(qb):
        logits = jnp.concatenate(
            [jnp.einsum("bqhd,bkhd->bhqk", q_i, k_i, preferred_element_type=jnp.float32)
             for q_i, k_i in zip(qb, keys)], axis=-1) * scale
        probs = jax.nn.softmax(logits, axis=-1)
        out = None
        start = 0
        for v_i, n_i in zip(vals, sizes):
            o_i = jnp.einsum("bhqk,bkhd->bqhd", probs[..., start:start + n_i].astype(v_i.dtype), v_i)
            out = o_i if out is None else out + o_i
            start += n_i
        return out

    o = lax.map(one_block, q_blocks)
    return jnp.moveaxis(o, 0, 1).reshape(b, s, h, dv)


def split_cols(z):
    offs = []
    acc = 0
    for n in SPLITS[:-1]:
        acc += n
        offs.append(acc)
    return jnp.split(z, offs, axis=-1)


def mixer_inputs(h, lp):
    b, s, _ = h.shape
    a_q, a_k, a_v, b_u, c_q, c_kv, c_kr, d_q, d_k, d_v = split_cols(h @ lp["w_in"])
    q_mla = (rmsnorm(c_q, lp["mla_q_norm_g"]) @ lp["mla_w_uq"]).reshape(b, s, MLA_HEADS, MLA_NOPE + MLA_ROPE)
    return {
        "diff_q": a_q.reshape(b, s, DIFF_HEADS * 2, DIFF_QK),
        "diff_k": a_k.reshape(b, s, DIFF_HEADS, 2 * DIFF_QK),
        "diff_v": a_v.reshape(b, s, DIFF_HEADS, DIFF_V),
        "fnet_u": b_u,
        "mla_q_nope": q_mla[..., :MLA_NOPE],
        "mla_q_rope": q_mla[..., MLA_NOPE:],
        "mla_ckv": rmsnorm(c_kv, lp["mla_kv_norm_g"]),
        "mla_krope": c_kr,
        "gqa_q": rmsnorm(d_q.reshape(b, s, GQA_HEADS, GQA_DIM), lp["gqa_q_norm_g"]),
        "gqa_k": rmsnorm(d_k.reshape(b, s, GQA_KV_HEADS, GQA_DIM), lp["gqa_k_norm_g"]),
        "gqa_v": d_v.reshape(b, s, GQA_KV_HEADS, GQA_DIM),
    }


def diff_maps(k):
    b, s = k.shape[:2]
    return k.reshape(b, s, DIFF_HEADS * 2, DIFF_QK)


def mla_expand(ckv, w_ukv):
    b, s = ckv.shape[:2]
    kv = (ckv @ w_ukv).reshape(b, s, MLA_HEADS, MLA_NOPE + MLA_V)
    return kv[..., :MLA_NOPE], kv[..., MLA_NOPE:]


def bcast_heads(kr):
    return jnp.broadcast_to(kr, kr.shape[:2] + (MLA_HEADS, kr.shape[-1]))


def fourier_mix(u):
    b, s, _ = u.shape
    uf = u.astype(jnp.float32).reshape(b, s, FNET_GROUPS, FNET_CH)
    y = jnp.fft.fft2(uf, axes=(1, 3), norm="ortho").real
    return y.reshape(b, s, GROUP_W).astype(u.dtype)


def diff_combine(o, lam, g, lam_init):
    b, s = o.shape[:2]
    o = o.reshape(b, s, DIFF_HEADS, 2, DIFF_V)
    y = o[..., 0, :] - lam.astype(o.dtype) * o[..., 1, :]
    y = rmsnorm(y, g) * (1.0 - lam_init)
    return y.reshape(b, s, GROUP_W)


def mixer_output(o_diff, u, o_mla, o_gqa, lp, lam, lam_init):
    b, s = u.shape[:2]
    y = jnp.concatenate([
        diff_combine(o_diff, lam, lp["diff_subln_g"], lam_init),
        fourier_mix(u),
        o_mla.reshape(b, s, MLA_HEADS * MLA_V),
        o_gqa.reshape(b, s, GQA_HEADS * GQA_DIM)], axis=-1)
    return y @ lp["w_out"]


def sqrelu_mlp(h, lp):
    return jnp.square(jax.nn.relu(h @ lp["mlp_w1"])) @ lp["mlp_w2"]


def context_layer(x, mod, lp, lam, lam_init):
    sh1, sc1, g1, sh2, sc2, g2 = mod
    t = mixer_inputs(modulate(x, lp["norm_mix_g"], sh1, sc1), lp)
    o_a = blocked_attention([(t["diff_q"], diff_maps(t["diff_k"]), jnp.repeat(t["diff_v"], DIFF_REP, axis=2))], DIFF_SCALE)
    k_nope, v_mla = mla_expand(t["mla_ckv"], lp["mla_w_ukv"])
    q_mla = jnp.concatenate([t["mla_q_nope"], t["mla_q_rope"]], axis=-1)
    k_mla = jnp.concatenate([k_nope, bcast_heads(t["mla_krope"][:, :, None, :])], axis=-1)
    o_c = blocked_attention([(q_mla, k_mla, v_mla)], MLA_SCALE)
    o_d = blocked_attention([(t["gqa_q"], jnp.repeat(t["gqa_k"], GQA_REP, axis=2),
                              jnp.repeat(t["gqa_v"], GQA_REP, axis=2))], GQA_SCALE)
    x = x + g1 * mixer_output(o_a, t["fnet_u"], o_c, o_d, lp, lam, lam_init)
    x = x + g2 * sqrelu_mlp(modulate(x, lp["norm_mlp_g"], sh2, sc2), lp)
    state = (t["diff_k"], t["diff_v"], t["mla_ckv"], t["mla_krope"], t["gqa_k"], t["gqa_v"])
    return x, state


def latent_layer(x, mod, cache, lp, lam, lam_init, row, col):
    sh1, sc1, g1, sh2, sc2, g2 = mod
    ck_diff, cv_diff, c_ckv, c_kr, ck_gqa, cv_gqa = cache
    t = mixer_inputs(modulate(x, lp["norm_mix_g"], sh1, sc1), lp)
    o_a = blocked_attention([
        (rope_2d(t["diff_q"], row, col), rope_2d(diff_maps(t["diff_k"]), row, col),
         jnp.repeat(t["diff_v"], DIFF_REP, axis=2)),
        (t["diff_q"], diff_maps(ck_diff), jnp.repeat(cv_diff, DIFF_REP, axis=2))], DIFF_SCALE)
    k_nope, v_mla = mla_expand(t["mla_ckv"], lp["mla_w_ukv"])
    k_nope_c, v_mla_c = mla_expand(c_ckv, lp["mla_w_ukv"])
    q_rot = jnp.concatenate([t["mla_q_nope"], rope_2d(t["mla_q_rope"], row, col)], axis=-1)
    q_plain = jnp.concatenate([t["mla_q_nope"], t["mla_q_rope"]], axis=-1)
    k_lat = jnp.concatenate([k_nope, bcast_heads(rope_2d(t["mla_krope"][:, :, None, :], row, col))], axis=-1)
    k_ctx = jnp.concatenate([k_nope_c, bcast_heads(c_kr[:, :, None, :])], axis=-1)
    o_c = blocked_attention([(q_rot, k_lat, v_mla), (q_plain, k_ctx, v_mla_c)], MLA_SCALE)
    o_d = blocked_attention([
        (rope_2d(t["gqa_q"], row, col), jnp.repeat(rope_2d(t["gqa_k"], row, col), GQA_REP, axis=2),
         jnp.repeat(t["gqa_v"], GQA_REP, axis=2)),
        (t["gqa_q"], jnp.repeat(ck_gqa, GQA_REP, axis=2), jnp.repeat(cv_gqa, GQA_REP, axis=2))], GQA_SCALE)
    x = x + g1 * mixer_output(o_a, t["fnet_u"], o_c, o_d, lp, lam, lam_init)
    x = x + g2 * sqrelu_mlp(modulate(x, lp["norm_mlp_g"], sh2, sc2), lp)
    return x


def setup_inputs(seed: int = 0) -> dict:
    key = jax.random.key(seed)
    ks = jax.random.split(key, 32)
    f32 = jnp.float32

    def nrm(k, shape, scale=1.0):
        return jax.random.normal(k, shape, f32) * scale

    def gain(k, shape):
        return 1.0 + 0.05 * jax.random.normal(k, shape, f32)

    return {
        "x_prompt": nrm(ks[0], (BATCH, SEQ, D_MODEL)),
        "x_sample": nrm(ks[1], (DEC_BATCH, DEC_SEQ, D_MODEL)),
        "cache_diff_k": nrm(ks[2], (DEC_BATCH, DEPTH, PAST_LEN, DIFF_HEADS, 2 * DIFF_QK)),
        "cache_diff_v": nrm(ks[3], (DEC_BATCH, DEPTH, PAST_LEN, DIFF_HEADS, DIFF_V)),
        "cache_mla_ckv": nrm(ks[4], (DEC_BATCH, DEPTH, PAST_LEN, MLA_KV_RANK)),
        "cache_mla_krope": nrm(ks[5], (DEC_BATCH, DEPTH, PAST_LEN, MLA_ROPE)),
        "cache_gqa_k": nrm(ks[6], (DEC_BATCH, DEPTH, PAST_LEN, GQA_KV_HEADS, GQA_DIM)),
        "cache_gqa_v": nrm(ks[7], (DEC_BATCH, DEPTH, PAST_LEN, GQA_KV_HEADS, GQA_DIM)),
        "c": nrm(ks[8], (DEC_BATCH, D_MODEL)),
        "c_ctx": nrm(ks[9], (D_MODEL,)),
        "norm_mix_g": gain(ks[10], (DEPTH, D_MODEL)),
        "norm_mlp_g": gain(ks[11], (DEPTH, D_MODEL)),
        "ada_w": nrm(ks[12], (DEPTH, D_MODEL, N_MOD * D_MODEL), D_MODEL ** -0.5),
        "ada_b": nrm(ks[13], (DEPTH, N_MOD * D_MODEL), 0.02),
        "w_in": nrm(ks[14], (DEPTH, D_MODEL, IN_W), D_MODEL ** -0.5),
        "diff_lambda": nrm(ks[15], (DEPTH, 4, DIFF_QK), 0.1),
        "diff_subln_g": gain(ks[16], (DEPTH, DIFF_V)),
        "mla_q_norm_g": gain(ks[17], (DEPTH, MLA_Q_RANK)),
        "mla_w_uq": nrm(ks[18], (DEPTH, MLA_Q_RANK, MLA_HEADS * (MLA_NOPE + MLA_ROPE)), MLA_Q_RANK ** -0.5),
        "mla_kv_norm_g": gain(ks[19], (DEPTH, MLA_KV_RANK)),
        "mla_w_ukv": nrm(ks[20], (DEPTH, MLA_KV_RANK, MLA_HEADS * (MLA_NOPE + MLA_V)), MLA_KV_RANK ** -0.5),
        "gqa_q_norm_g": gain(ks[21], (DEPTH, GQA_DIM)),
        "gqa_k_norm_g": gain(ks[22], (DEPTH, GQA_DIM)),
        "w_out": nrm(ks[23], (DEPTH, MIX_W, D_MODEL), MIX_W ** -0.5),
        "mlp_w1": nrm(ks[24], (DEPTH, D_MODEL, D_FF), D_MODEL ** -0.5),
        "mlp_w2": nrm(ks[25], (DEPTH, D_FF, D_MODEL), D_FF ** -0.5),
        "final_norm_g": gain(ks[26], (D_MODEL,)),
    }


def reference(x_prompt, x_sample, cache_diff_k, cache_diff_v, cache_mla_ckv, cache_mla_krope,
              cache_gqa_k, cache_gqa_v, c, c_ctx, norm_mix_g, norm_mlp_g, ada_w, ada_b, w_in,
              diff_lambda, diff_subln_g, mla_q_norm_g, mla_w_uq, mla_kv_norm_g, mla_w_ukv,
              gqa_q_norm_g, gqa_k_norm_g, w_out, mlp_w1, mlp_w2, final_norm_g):
    n_lat = x_sample.shape[1]
    rows = n_lat // GRID_W
    row = jnp.repeat(jnp.arange(rows, dtype=jnp.int32), GRID_W)
    col = jnp.tile(jnp.arange(GRID_W, dtype=jnp.int32), rows)

    xp = x_prompt
    xs = x_sample
    st_diff_k, st_diff_v, st_ckv, st_kr, st_gk, st_gv = [], [], [], [], [], []
    for l in range(DEPTH):
        lp = {
            "norm_mix_g": norm_mix_g[l], "norm_mlp_g": norm_mlp_g[l], "w_in": w_in[l],
            "diff_subln_g": diff_subln_g[l], "mla_q_norm_g": mla_q_norm_g[l], "mla_w_uq": mla_w_uq[l],
            "mla_kv_norm_g": mla_kv_norm_g[l], "mla_w_ukv": mla_w_ukv[l],
            "gqa_q_norm_g": gqa_q_norm_g[l], "gqa_k_norm_g": gqa_k_norm_g[l],
            "w_out": w_out[l], "mlp_w1": mlp_w1[l], "mlp_w2": mlp_w2[l],
        }
        lam_init = 0.8 - 0.6 * math.exp(-0.3 * l)
        lamp = diff_lambda[l].astype(jnp.float32)
        lam = jnp.exp(jnp.sum(lamp[0] * lamp[1])) - jnp.exp(jnp.sum(lamp[2] * lamp[3])) + lam_init
        mod_ctx = adaln(c_ctx[None, None, :], ada_w[l], ada_b[l])
        mod_lat = adaln(c[:, None, :], ada_w[l], ada_b[l])

        xp, st = context_layer(xp, mod_ctx, lp, lam, lam_init)
        st_diff_k.append(st[0]); st_diff_v.append(st[1]); st_ckv.append(st[2])
        st_kr.append(st[3]); st_gk.append(st[4]); st_gv.append(st[5])

        cache_l = (cache_diff_k[:, l], cache_diff_v[:, l], cache_mla_ckv[:, l], cache_mla_krope[:, l],
                   cache_gqa_k[:, l], cache_gqa_v[:, l])
        xs = latent_layer(xs, mod_lat, cache_l, lp, lam, lam_init, row, col)

    y_prompt = rmsnorm(xp, final_norm_g)
    y_sample = rmsnorm(xs, final_norm_g)
    new_diff_k = jnp.stack(st_diff_k, axis=1)
    new_diff_v = jnp.stack(st_diff_v, axis=1)
    new_mla_ckv = jnp.stack(st_ckv, axis=1)
    new_mla_krope = jnp.stack(st_kr, axis=1)
    new_gqa_k = jnp.stack(st_gk, axis=1)
    new_gqa_v = jnp.stack(st_gv, axis=1)
    return (y_prompt, y_sample, new_diff_k, new_diff_v, new_mla_ckv, new_mla_krope, new_gqa_k, new_gqa_v)
```

```python
import contextlib
import math
import numpy as np
import ml_dtypes
import concourse.bass as bass
import concourse.mybir as mybir
from concourse.bass_utils import run_bass_kernel_spmd

F32 = mybir.dt.float32
BF16 = mybir.dt.bfloat16
AF = mybir.ActivationFunctionType
ALU = mybir.AluOpType

NL = 2
EPS = 1e-6
TP = 1024
TS = 1024
T = TP + TS
XW = 10240
X_KD, X_KG, X_CKV, X_KR, X_VD, X_VG, X_U = 0, 2048, 3072, 4096, 5120, 7168, 8192
SAME_ENG_SYNC = True
DEBUG = {}
NDMASEM = 12


class Op:
    __slots__ = ("eng", "fn", "deps", "dma", "sem", "cnt", "need", "idx", "cc")

    def __init__(self, eng, fn, dma):
        self.eng, self.fn, self.dma = eng, fn, dma
        self.cc = False
        self.deps = set()
        self.sem = None
        self.cnt = 0
        self.need = False


class StopBuild(Exception):
    pass


def ck(n):
    if DEBUG.get('stop') == n:
        raise StopBuild()


class Builder:
    ENGS = ("pe", "act", "dve", "pool", "sp")

    def __init__(self, nc):
        self.nc = nc
        self.ops = {e: [] for e in self.ENGS}
        self.lastw = {}
        self.readers = {}
        self.pending = {e: set() for e in self.ENGS}
        self.dma_ring = {"sp": [None] * NDMASEM, "pool": [None] * NDMASEM}
        self.dma_rr = {"sp": 0, "pool": 0}
        self.dmas_since_bar = []

    def op(self, eng, fn, r=(), w=(), dma=False, slot=None, cc=False):
        o = Op(eng, fn, dma)
        o.cc = cc
        deps = set()
        if dma:
            i = self.dma_rr[eng]
            self.dma_rr[eng] = (i + 1) % NDMASEM
            prev = self.dma_ring[eng][i]
            if prev is not None:
                deps.add(prev)
            self.dma_ring[eng][i] = o
            o.sem = (eng, i)
        slot = o.sem if dma else (("cc", len(self.ops[eng])) if cc else eng)
        for k in r:
            deps.update(self.lastw.get(k, {}).values())
        for k in w:
            deps.update(self.lastw.get(k, {}).values())
            deps.update(self.readers.get(k, {}).values())
        for k in r:
            self.readers.setdefault(k, {})[slot] = o
        for k in w:
            self.lastw.setdefault(k, {})[slot] = o
            self.readers[k] = {}
        deps.update(self.pending[eng])
        self.pending[eng] = set()
        deps.discard(o)
        o.deps = deps
        self.ops[eng].append(o)
        return o

    def dma(self, eng, out, in_, r=(), w=()):
        return self.op(eng, lambda e: e.dma_start(out=out, in_=in_), r=r, w=w, dma=True)

    def barrier(self):
        lasts = set()
        for e in self.ENGS:
            for o in reversed(self.ops[e]):
                if not o.dma and not o.cc:
                    lasts.add(o)
                    break
        for e in ("sp", "pool"):
            for o in self.dma_ring[e]:
                if o is not None:
                    lasts.add(o)
        for e in self.ENGS:
            self.pending[e] |= lasts

    def emit(self, block, sems, dsems):
        nc = self.nc
        for e in self.ENGS:
            for o in self.ops[e]:
                for d in o.deps:
                    if d.dma or d.cc:
                        continue
                    if d.eng == o.eng and (d.eng == "pe" or not SAME_ENG_SYNC):
                        continue
                    d.need = True
        for e in self.ENGS:
            c = 0
            cd = {}
            for o in self.ops[e]:
                if o.dma:
                    cd[o.sem] = cd.get(o.sem, 0) + 16
                    o.cnt = cd[o.sem]
                elif o.cc:
                    cd["cc"] = cd.get("cc", 0) + 1
                    o.cnt = cd["cc"]
                elif o.need:
                    c += 1
                    o.cnt = c

        def run(engname, engobj):
            waited = {}
            for o in self.ops[engname]:
                for d in sorted(o.deps, key=lambda x: (x.eng, x.cnt)):
                    if d.dma:
                        key, s, v = d.sem, dsems[d.sem], d.cnt
                    elif d.cc:
                        key, s, v = "cc", dsems["cc"], d.cnt
                    else:
                        if d.eng == o.eng and (d.eng == "pe" or not SAME_ENG_SYNC):
                            continue
                        key, s, v = d.eng, sems[d.eng], d.cnt
                    if waited.get(key, 0) >= v:
                        continue
                    waited[key] = v
                    engobj.wait_ge(s, v)
                ins = o.fn(engobj)
                if o.dma:
                    ins.then_inc(dsems[o.sem], 16)
                elif o.cc:
                    ins.then_inc(dsems["cc"], 1)
                elif o.need:
                    ins.then_inc(sems[o.eng], 1)

        @block.tensor
        def _(t):
            run("pe", t)

        @block.scalar
        def _(t):
            run("act", t)

        @block.vector
        def _(t):
            run("dve", t)

        @block.gpsimd
        def _(t):
            run("pool", t)

        @block.sync
        def _(t):
            run("sp", t)
            for q in ("sp", "pool"):
                for o in self.dma_ring[q]:
                    if o is not None:
                        t.wait_ge(dsems[o.sem], o.cnt)


class Arena:
    def __init__(self, tensor, nwords):
        self.t = tensor
        self.n = nwords
        self.top = 0
        self.peak = 0

    def alloc(self, shape, dt):
        free = 1
        for s in shape[1:]:
            free *= s
        words = free if dt == F32 else (free + 1) // 2
        words = (words + 7) // 8 * 8
        off = self.top
        self.top += words
        self.peak = max(self.peak, self.top)
        assert self.top <= self.n, f"arena overflow {self.top} > {self.n}"
        ap = self.t[0:shape[0], off:off + words]
        if dt != F32:
            ap = ap.bitcast(dt)
        ap = ap[:, 0:free]
        if len(shape) == 3:
            ap = ap.rearrange("p (a b) -> p a b", b=shape[2])
        elif len(shape) == 4:
            ap = ap.rearrange("p (a b c) -> p a b c", b=shape[2], c=shape[3])
        return ap

    def mark(self):
        return self.top

    def release(self, m):
        self.top = m


def build_program():
    nc = bass.Bass("TRN2", target_bir_lowering=False)
    D = {}

    def din(name, shape, dt=F32):
        D[name] = nc.dram_tensor(name, list(shape), dt, kind="ExternalInput").ap()
        return D[name]

    def dout(name, shape, dt=F32):
        D[name] = nc.dram_tensor(name, list(shape), dt, kind="ExternalOutput").ap()
        return D[name]

    din("xT", [128, 8 * T])
    din("cv", [128, 16])
    din("adaw", [NL, 128, 8 * 6144])
    din("adab", [NL, 128, 48])
    din("gmix", [NL, 128, 8])
    din("gmlp", [NL, 128, 8])
    din("gfin", [128, 8])
    din("win", [NL, 128, 8 * 1888])
    din("wuq", [NL, 128, 2 * 384])
    din("wuk", [NL, 128, 256])
    din("wuv", [NL, 128, 256])
    din("wout", [NL, 128, 8 * 1024])
    din("w1", [NL, 128, 8 * 4096])
    din("w2", [NL, 128, 32 * 1024])
    din("gt", [NL, 128, 448])
    din("gsub", [NL, 64, 1])
    din("lamp", [NL, 128, 128])
    for nm in ("rc32", "rs32", "rc64", "rs64"):
        din(nm, [128, 8 * 128])
    din("c256", [128, 2 * 256], BF16)
    din("s256", [128, 2 * 256], BF16)
    din("cbig", [128, 32 * 1024], BF16)
    din("sbig", [128, 32 * 1024], BF16)
    din("c64", [128, 128], BF16)
    din("s64n", [128, 128], BF16)
    din("ident", [128, 128])
    din("ckdT", [NL, 128, 2 * 512])
    din("cvd", [NL, 128, 4 * 256])
    din("cckvT", [NL, 128, 512])
    din("ckrT", [NL, 128, 512])
    din("ckgT", [NL, 128, 512])
    din("cvg", [NL, 128, 4 * 128])
    dout("yT", [128, 8 * T])
    dout("o_dk", [4, NL, 256, 256])
    dout("o_dv", [4, NL, 256, 256])
    dout("o_ckv", [4, NL, 256, 128])
    dout("o_kr", [4, NL, 256, 32])
    dout("o_gk", [4, NL, 256, 128])
    dout("o_gv", [4, NL, 256, 128])
    SEGW = (4096, 4096, 2048)
    xin = [[nc.dram_tensor(f"xin{l}_{j}", [128, SEGW[j]], BF16) for j in range(3)] for l in range(NL)]
    xout = [[nc.dram_tensor(f"xout{l}_{j}", [4 * 128, SEGW[j]], BF16) for j in range(3)] for l in range(NL)]

    es = contextlib.ExitStack()
    with es:
        NW = 52000
        arena_t = es.enter_context(nc.sbuf_tensor("arena", [128, NW], F32))
        A = Arena(arena_t, NW)
        psall = es.enter_context(nc.psum_tensor("psall", [128, 4096], F32))
        psum = [psall[:, i * 512:(i + 1) * 512] for i in range(8)]
        sems = {e: es.enter_context(nc.semaphore(f"s_{e}")) for e in Builder.ENGS}
        dsems = {(q, i): es.enter_context(nc.semaphore(f"d_{q}{i}")) for q in ("sp", "pool") for i in range(NDMASEM)}
        s_cc = es.enter_context(nc.semaphore("s_cc"))
        dsems["cc"] = s_cc
        B = Builder(nc)

        ps_state = {"rr": 0, "held": set(), "lo": 0}

        def ps_get(hold=False):
            for _ in range(8):
                i = ps_state["rr"]
                ps_state["rr"] = (i + 1) % 8
                if i < ps_state["lo"]:
                    continue
                if i not in ps_state["held"]:
                    if hold:
                        ps_state["held"].add(i)
                    return i
            raise RuntimeError("no psum bank")

        def ps_rel(i):
            ps_state["held"].discard(i)

        def PK(i):
            return ("ps", i)

        uid = [0]

        def U():
            uid[0] += 1
            return ("u", uid[0])

        def mm(out, lhsT, rhs, start, stop, r, w):
            try:
                bp = lhsT.base_partition()
            except AssertionError:
                bp = 96
            kw = {"tile_position": (96, 0)} if bp == 96 else {}
            return B.op("pe", lambda e: e.matmul(out, lhsT, rhs, start=start, stop=stop, **kw), r=r, w=w)

        def transpose(out, in_, ident, r, w):
            return B.op("pe", lambda e: e.transpose(out, in_, ident), r=r, w=w)

        def act(out, in_, func, r, w, bias=None, scale=None, accum_out=None):
            kw = {}
            if bias is not None:
                kw["bias"] = bias
            if scale is not None:
                kw["scale"] = scale
            if accum_out is not None:
                kw["accum_out"] = accum_out
            return B.op("act", lambda e: e.activation(out=out, in_=in_, func=func, **kw), r=r, w=w)

        def tt(eng, out, in0, in1, op, r, w):
            return B.op(eng, lambda e: e.tensor_tensor(out=out, in0=in0, in1=in1, op=op), r=r, w=w)

        def stt(eng, out, in0, scalar, in1, op0, op1, r, w):
            return B.op(eng, lambda e: e.scalar_tensor_tensor(out=out, in0=in0, scalar=scalar, in1=in1,
                                                              op0=op0, op1=op1), r=r, w=w)

        def ts(eng, out, in0, s1, s2, op0, op1, r, w):
            return B.op(eng, lambda e: e.tensor_scalar(out=out, in0=in0, scalar1=s1, scalar2=s2,
                                                       op0=op0, op1=op1), r=r, w=w)

        def cp(eng, out, in_, r, w):
            if eng == "act":
                return act(out, in_, AF.Copy, r, w)
            return B.op(eng, lambda e: e.tensor_copy(out=out, in_=in_), r=r, w=w)

        def recip(out, in_, r, w):
            return B.op("dve", lambda e: e.reciprocal(out=out, in_=in_), r=r, w=w)

        def memset(eng, ap, val, w):
            return B.op(eng, lambda e: e.memset(ap, val), w=w)

        xT = A.alloc([128, 8, T], F32)
        ident = A.alloc([128, 128], F32)
        ones_bf = A.alloc([128, 128], BF16)
        epsc = A.alloc([128, 1], F32)
        mods = A.alloc([128, NL * 48 * 2], F32)
        modA = A.alloc([128, NL * 2 * 2 * 8], F32)
        gains = A.alloc([128, NL * 2 * 8 + 8], F32)
        gtab = A.alloc([128, NL * 448], F32)
        gsub = A.alloc([128, NL], F32)
        onesbd = A.alloc([128, 128], BF16)
        lam = A.alloc([128, NL * 4], F32)
        c64 = A.alloc([128, 128], BF16)
        s64n = A.alloc([128, 128], BF16)
        c256 = A.alloc([128, 2, 256], BF16)
        s256 = A.alloc([128, 2, 256], BF16)
        wuq = A.alloc([128, NL * 2, 384], BF16)
        wuk = A.alloc([128, NL, 256], BF16)
        wuv = A.alloc([128, NL, 256], BF16)
        cvb = A.alloc([128, 8, 2], BF16)
        adab = A.alloc([128, NL, 48], F32)
        mrow = A.alloc([128, 1024], F32)

        def mods_ap(l, m, k, v):
            i = (l * 48 + m * 8 + k) * 2 + v
            return mods[:, i:i + 1]

        def modA_ap(l, v, which, k):
            i = ((l * 2 + v) * 2 + which) * 8 + k
            return modA[:, i:i + 1]

        try:
            B.dma("sp", xT, D["xT"].rearrange("p (c t) -> p c t", t=T), w=["xT"])
            B.dma("sp", ident, D["ident"], w=["ident"])
            memset("dve", ones_bf, 1.0, w=["ones"])
            memset("dve", epsc, EPS, w=["eps"])
            memset("dve", onesbd, 0.0, w=["onesbd"])
            memset("dve", onesbd[0:64, 0:64], 1.0, w=["onesbd"])
            memset("dve", onesbd[64:128, 64:128], 1.0, w=["onesbd"])
            B.dma("sp", gains[:, 0:16].rearrange("p (l k) -> p l k", k=8), D["gmix"].rearrange("l p k -> p l k"), w=["gains"])
            B.dma("sp", gains[:, 16:32].rearrange("p (l k) -> p l k", k=8), D["gmlp"].rearrange("l p k -> p l k"), w=["gains"])
            B.dma("sp", gains[:, 32:40], D["gfin"], w=["gains"])
            B.dma("sp", gtab.rearrange("p (l c) -> p l c", c=448), D["gt"].rearrange("l p c -> p l c"), w=["gtab"])
            for l_ in range(NL):
                B.dma("sp", gsub[0:64, l_:l_ + 1], D["gsub"][l_], w=["gsub"])
                B.dma("sp", gsub[64:128, l_:l_ + 1], D["gsub"][l_], w=["gsub"])
            B.dma("pool", c64, D["c64"], w=["c64"])
            B.dma("pool", s64n, D["s64n"], w=["c64"])
            B.dma("pool", c256, D["c256"].rearrange("p (a b) -> p a b", b=256), w=["c256"])
            B.dma("pool", s256, D["s256"].rearrange("p (a b) -> p a b", b=256), w=["c256"])
            for l_ in range(NL):
                B.dma("pool", wuq[:, l_ * 2:l_ * 2 + 2, :], D["wuq"][l_].rearrange("p (a b) -> p a b", b=384), w=["wuq"])
            B.dma("pool", wuk, D["wuk"].rearrange("l p c -> p l c"), w=["wuk"])
            B.dma("pool", wuv, D["wuv"].rearrange("l p c -> p l c"), w=["wuv"])

            m0 = A.mark()
            cvf = A.alloc([128, 16], F32)
            lamp = A.alloc([128, NL, 128], F32)
            lprod = A.alloc([128, NL, 64], F32)
            lsum = A.alloc([128, NL * 2], F32)
            adw = [A.alloc([128, 8, 1024], BF16) for _ in range(2)]
            B.dma("sp", cvf, D["cv"], w=["cvf"])
            B.dma("sp", adab, D["adab"].rearrange("l p j -> p l j"), w=["adab"])
            B.dma("sp", lamp, D["lamp"].rearrange("l p c -> p l c"), w=["lamp"])
            act(cvb.rearrange("p a b -> p (a b)"), cvf, AF.Silu, r=["cvf"], w=["cvb"])

            def mods_load(l, m, buf, key):
                B.dma("pool", buf, D["adaw"][l].rearrange("p (k c) -> p k c", c=6144)[:, :, m * 1024:(m + 1) * 1024],
                      w=[key])

            def mods_piece(l, m, buf, key):
                for hf in range(2):
                    pb = ps_get()
                    for k in range(8):
                        mm(psum[pb][0:2, :], cvb[:, k, :], buf[:, k, hf * 512:(hf + 1) * 512], start=(k == 0), stop=(k == 7),
                           r=[key, "cvb"], w=[PK(pb)])
                    cp("dve", mrow[0:2, hf * 512:(hf + 1) * 512], psum[pb][0:2, :], r=[PK(pb)], w=[("mrow", hf)])
                pb = ps_get()
                for j in range(8):
                    transpose(psum[pb][:, 2 * j:2 * j + 2], mrow[0:2, j * 128:(j + 1) * 128], ident[0:2, 0:2],
                              r=[("mrow", j // 4), "ident"], w=[PK(pb)])
                base = (l * 48 + m * 8) * 2
                for v in range(2):
                    dst = mods[:, base:base + 16].rearrange("p (j v) -> p j v", v=2)[:, :, v]
                    src = psum[pb][:, 0:16].rearrange("p (j v) -> p j v", v=2)[:, :, v]
                    tt("dve", dst, src, adab[:, l, m * 8:(m + 1) * 8], ALU.add, r=[PK(pb), "adab"], w=["mods"])

            def mods_finish(l):
                for v in range(2):
                    for which, (msc, goff) in enumerate(((1, 0), (4, 16))):
                        for k in range(8):
                            ts("dve", modA_ap(l, v, which, k), mods_ap(l, msc, k, v), 1.0,
                               gains[:, goff + l * 8 + k:goff + l * 8 + k + 1], ALU.add, ALU.mult,
                               r=["mods", "gains"], w=["modA"])

            for m in range(6):
                mods_load(0, m, adw[m % 2], ("adw", m % 2))
                mods_piece(0, m, adw[m % 2], ("adw", m % 2))
            mods_finish(0)
            for l in range(NL):
                tt("dve", lprod[:, l, :].rearrange("p (a b) -> p a b", b=32),
                   lamp[:, l, :].rearrange("p (a t b) -> p a t b", t=2, b=32)[:, :, 0, :],
                   lamp[:, l, :].rearrange("p (a t b) -> p a t b", t=2, b=32)[:, :, 1, :], ALU.mult,
                   r=["lamp"], w=["lprod"])
                for j in range(2):
                    B.op("dve", lambda e, l=l, j=j: e.reduce_sum(out=lsum[:, l * 2 + j:l * 2 + j + 1],
                                                                 in_=lprod[:, l, j * 32:(j + 1) * 32],
                                                                 axis=mybir.AxisListType.X), r=["lprod"], w=["lsum"])
                act(lsum[:, l * 2:l * 2 + 2], lsum[:, l * 2:l * 2 + 2], AF.Exp, r=["lsum"], w=["lsum"])
                lam_init = 0.8 - 0.6 * math.exp(-0.3 * l)
                tt("dve", lam[:, l * 4:l * 4 + 1], lsum[:, l * 2 + 1:l * 2 + 2], lsum[:, l * 2:l * 2 + 1], ALU.subtract,
                   r=["lsum"], w=["lam"])
                B.op("dve", lambda e, l=l, li=lam_init: e.tensor_scalar_add(out=lam[:, l * 4:l * 4 + 1], in0=lam[:, l * 4:l * 4 + 1],
                                                                      scalar1=-li), r=["lam"], w=["lam"])
            B.barrier()
            A.release(m0)
            ck(1)

            def fm_norm(t0, ntok, hT_dst, scale_ap_fn, bias_ap_fn, tmp, sq, rstd, out_f32=None):
                tmps = tmp if isinstance(tmp, list) else [tmp]
                sqs = sq if isinstance(sq, list) else [sq]
                rstds = rstd if isinstance(rstd, list) else [rstd]
                for it, tt0 in enumerate(range(t0, t0 + ntok, 512)):
                    p = it % len(sqs)
                    tmp_, sq_, rstd_ = tmps[p], sqs[p], rstds[p]
                    sl = slice(tt0, tt0 + 512)
                    pb = ps_get()
                    for k in range(8):
                        if k % 2 == 0:
                            act(sq_[:, k, :], xT[:, k, sl], AF.Square, r=["xT"], w=[("sq", p, k)])
                        else:
                            tt("dve", sq_[:, k, :], xT[:, k, sl], xT[:, k, sl], ALU.mult, r=["xT"], w=[("sq", p, k)])
                    for k in range(8):
                        mm(psum[pb][:, :], ones_bf, sq_[:, k, :], start=(k == 0), stop=(k == 7),
                           r=[("sq", p, k), "ones"], w=[PK(pb)])
                    act(rstd_, psum[pb][:, :], AF.Sqrt, r=[PK(pb), "eps"], w=[("rstd", p)], bias=epsc[:, 0:1], scale=1.0 / 1024)
                    recip(rstd_, rstd_, r=[("rstd", p)], w=[("rstd", p)])
                    for k in range(8):
                        tt("dve", tmp_[:, k % 2, :], xT[:, k, sl], rstd_, ALU.mult, r=["xT", ("rstd", p)], w=[("ntmp", p, k % 2)])
                        b = bias_ap_fn(k)
                        act(hT_dst(k, tt0 - t0), tmp_[:, k % 2, :], AF.Identity, r=[("ntmp", p, k % 2), "mods", "modA", "gains"],
                            w=[("hT", k)], scale=scale_ap_fn(k), **({"bias": b} if b is not None else {}))

            def attention2(pairs, scale, ptb, nq):
                ps_state["lo"] = 4
                merge = (nq == 512)
                pre_done = set()

                def run_pre(ix):
                    if ix < len(pairs) and ix not in pre_done:
                        pre_done.add(ix)
                        for job in pairs[ix]:
                            if "pre" in job:
                                job["pre"]()

                for pidx, pair in enumerate(pairs):
                    nk = pair[0]["nk"]
                    run_pre(pidx)
                    obs = [ps_get(hold=True), ps_get(hold=True)]
                    rr0 = pair[0]["r"] + pair[1]["r"]

                    def rkeys(kc):
                        out = list(rr0)
                        for job in pair:
                            if "rk" in job:
                                out += job["rk"](kc)
                        return out

                    def qk(kc):
                        sbase = (kc % 2) * 2
                        for i, job in enumerate(pair):
                            spec = job["qk"](kc)
                            for j, (kT, qT) in enumerate(spec):
                                mm(psum[sbase + i][:, 0:nq], kT, qT, start=(j == 0), stop=(j == len(spec) - 1),
                                   r=rkeys(kc), w=[PK(sbase + i)])

                    def pv_exp(kc):
                        sbase = (kc % 2) * 2
                        pi_ = kc % len(ptb)
                        pt = ptb[pi_]
                        if merge:
                            act(pt[:, 0:1024], psall[:, sbase * 512:(sbase + 2) * 512], AF.Exp,
                                r=[PK(sbase), PK(sbase + 1)], w=[("pt", pi_)], scale=scale)
                        else:
                            for i in range(2):
                                act(pt[:, i * 512:i * 512 + nq], psum[sbase + i][:, 0:nq], AF.Exp,
                                    r=[PK(sbase + i)], w=[("pt", pi_)], scale=scale)

                    def pv_mm(kc):
                        pi_ = kc % len(ptb)
                        pt = ptb[pi_]
                        for i, job in enumerate(pair):
                            mm(psum[obs[i]][:, 0:nq], job["v"](kc), pt[:, i * 512:i * 512 + nq], start=(kc == 0),
                               stop=(kc == nk - 1), r=rkeys(kc) + [("pt", pi_)], w=[PK(obs[i])])

                    qk(0)
                    if nk > 1:
                        qk(1)
                    run_pre(pidx + 1)
                    for kc in range(nk):
                        pv_exp(kc)
                        if kc + 2 < nk:
                            qk(kc + 2)
                        pv_mm(kc)
                    for i, job in enumerate(pair):
                        job["epi"](obs[i])
                        ps_rel(obs[i])
                ps_state["lo"] = 0

            for l in range(NL):
                lam_init = 0.8 - 0.6 * math.exp(-0.3 * l)
                neglam = lam[0:64, l * 4:l * 4 + 1]
                mL = A.mark()
                QdS = A.alloc([128, 2, 2, TS], BF16)
                QnS = A.alloc([128, 2, TS], BF16)
                QrS = A.alloc([128, 2, TS], BF16)
                QgS = A.alloc([128, 2, 2, TS], BF16)

                def drive(gens):
                    live = list(gens)
                    first = True
                    while live:
                        for g in list(live):
                            try:
                                next(g)
                                if first:
                                    next(g)
                                    next(g)
                                    first = False
                            except StopIteration:
                                live.remove(g)

                def inproj(grp):
                    t0 = TP if grp == "S" else 0
                    v = 1 if grp == "S" else 0
                    G = gtab[:, l * 448:(l + 1) * 448]
                    winv = D["win"][l].rearrange("p (k c) -> p k c", c=1888)
                    mI = A.mark()
                    hT = A.alloc([128, 8, 512], BF16)
                    win0 = A.alloc([128, 8, 512], BF16)
                    win1 = A.alloc([128, 8, 512], BF16)

                    def load_winA():
                        B.dma("pool", win0, winv[:, :, 0:512], w=["win0"])
                        B.dma("pool", win1, winv[:, :, 512:1024], w=["win1"])

                    def load_winB():
                        B.dma("pool", win0[:, :, 0:352], winv[:, :, 1024:1376], w=["win0"])
                        B.dma("pool", win1, winv[:, :, 1376:1888], w=["win1"])
                    load_winA()
                    for half in range(2):
                        m1 = A.mark()
                        sq = A.alloc([128, 8, 512], BF16)
                        ntmp = A.alloc([128, 2, 512], F32)
                        rstd = A.alloc([128, 512], F32)
                        fm_norm(t0 + half * 512, 512, lambda k, o: hT[:, k, :], lambda k: modA_ap(l, v, 0, k),
                                lambda k: mods_ap(l, 0, k, v), ntmp, sq, rstd)
                        B.barrier()
                        A.release(m1)
                        ck(20)

                        def tr(dst, src, ncols, rk, wk, eng="dve"):
                            pb = ps_get()
                            transpose(psum[pb][0:ncols, 0:128], src, ident, r=rk + ["ident"], w=[PK(pb)])
                            cp(eng, dst, psum[pb][0:ncols, 0:128], r=[PK(pb)], w=wk)

                        def rope(dst, src, nh, d, ci, si, rk, wk, rp, rtab, p):
                            q = d // 4
                            n = nh * d
                            sv = src.rearrange("p (a t q) -> p a t q", t=2, q=q)
                            dv = dst.rearrange("p (a t q) -> p a t q", t=2, q=q)
                            cs = rtab[:, ci, 0:n // 2].rearrange("p (a q) -> p a q", q=q)
                            sn = rtab[:, si, 0:n // 2].rearrange("p (a q) -> p a q", q=q)
                            t = [rp[:, i, 0:n // 2].rearrange("p (a q) -> p a q", q=q) for i in range(4)]
                            RK = rk + [("rope", p)]
                            tt("pool", t[0], sv[:, :, 0, :], cs, ALU.mult, r=RK, w=[("rp", p, 0)])
                            tt("pool", t[1], sv[:, :, 1, :], sn, ALU.mult, r=RK, w=[("rp", p, 1)])
                            tt("dve", t[2], sv[:, :, 0, :], sn, ALU.mult, r=RK, w=[("rp", p, 2)])
                            tt("dve", t[3], sv[:, :, 1, :], cs, ALU.mult, r=RK, w=[("rp", p, 3)])
                            tt("pool", dv[:, :, 0, :], t[0], t[1], ALU.subtract, r=[("rp", p, 0), ("rp", p, 1)], w=wk)
                            tt("dve", dv[:, :, 1, :], t[2], t[3], ALU.add, r=[("rp", p, 2), ("rp", p, 3)], w=wk)

                        def load_rope(ti, rtab, p):
                            for j, nm in enumerate(("rc32", "rs32", "rc64", "rs64")):
                                B.dma("sp", rtab[:, j, :], D[nm][:, ti * 128:(ti + 1) * 128], w=[("rope", p)])

                        mA = A.mark()
                        ztA = [A.alloc([128, 1024], F32) for _ in range(2)]
                        zrA = [A.alloc([128, 512], F32) for _ in range(2)] if grp == "S" else [None, None]
                        rpA = [A.alloc([128, 4, 128], F32) for _ in range(2)] if grp == "S" else [None, None]
                        rtA = [A.alloc([128, 4, 128], F32) for _ in range(2)] if grp == "S" else [None, None]
                        def projA(tl):
                            ti = half * 4 + tl
                            p = tl % 2
                            zt, zr, rp, rtab = ztA[p], zrA[p], rpA[p], rtA[p]
                            tsl = slice(ti * 128, (ti + 1) * 128)
                            for (c0, c1, eng, wb, wk) in ((0, 512, "act", win0, "win0"), (512, 1024, "dve", win1, "win1")):
                                pb = ps_get()
                                for k in range(8):
                                    mm(psum[pb][:, 0:c1 - c0], hT[:, k, tl * 128:(tl + 1) * 128], wb[:, k, 0:c1 - c0],
                                       start=(k == 0), stop=(k == 7), r=[("hT", k), wk], w=[PK(pb)])
                                cp(eng, zt[:, c0:c1], psum[pb][:, 0:c1 - c0], r=[PK(pb)], w=[("zt", p, c0)])

                        def postA(tl):
                            ti = half * 4 + tl
                            p = tl % 2
                            zt, zr, rp, rtab = ztA[p], zrA[p], rpA[p], rtA[p]
                            tsl = slice(ti * 128, (ti + 1) * 128)
                            ZA, ZB = [("zt", p, 0)], [("zt", p, 512)]
                            if grp == "S":
                                load_rope(ti, rtab, p)
                                rope(zr[:, 0:256], zt[:, 0:256], 8, 32, 0, 1, ZA, [("zr", p, 0)], rp, rtab, p)
                                yield
                                rope(zr[:, 256:512], zt[:, 256:512], 8, 32, 0, 1, ZA, [("zr", p, 1)], rp, rtab, p)
                                yield
                                for g in range(2):
                                    tr(QdS[:, 0, g, tsl], zr[:, g * 128:(g + 1) * 128], 128, [("zr", p, 0)], ["QdS"])
                                    tr(QdS[:, 1, g, tsl], zt[:, g * 128:(g + 1) * 128], 128, ZA, ["QdS"], eng="act")
                                    tr(XS[:, X_KD + g * 1024 + ti * 128:X_KD + g * 1024 + (ti + 1) * 128],
                                       zr[:, 256 + g * 128:256 + (g + 1) * 128], 128, [("zr", p, 1)], ["XS"])
                                yield
                                cp("pool", XS[:, X_VD + ti * 256:X_VD + (ti + 1) * 256], zt[:, 512:768], r=ZB, w=["XS"])
                                cp("pool", XS[:, X_U + ti * 256:X_U + (ti + 1) * 256], zt[:, 768:1024], r=ZB, w=["XS"])
                            else:
                                sq_i, pos0 = ti // 2, (ti % 2) * 128
                                B.dma("sp", D["o_dk"][sq_i, l, pos0:pos0 + 128, :], zt[:, 256:512], r=ZA)
                                B.dma("sp", D["o_dv"][sq_i, l, pos0:pos0 + 128, :], zt[:, 512:768], r=ZB)
                                for g in range(2):
                                    tr(QdP[:, g, tsl], zt[:, g * 128:(g + 1) * 128], 128, ZA, ["QdP"], eng="act")
                                    tr(KdP[:, g, tsl], zt[:, 256 + g * 128:256 + (g + 1) * 128], 128, ZA, ["KdP"])
                                yield
                                vsrc = zt[:, 512:768].rearrange("p (a e c) -> p a e c", e=2, c=64)
                                vdst = VdP[:, ti, :].rearrange("p (a b) -> p a b", b=192)
                                cp("pool", vdst[:, :, 0:64], vsrc[:, :, 0, :], r=ZB, w=["VdP"])
                                cp("pool", vdst[:, :, 128:192], vsrc[:, :, 1, :], r=ZB, w=["VdP"])
                                cp("pool", UP[:, ti, :], zt[:, 768:1024], r=ZB, w=["UP"])
                            yield

                        def laneA(tiles):
                            for tl in tiles:
                                projA(tl)
                                yield
                                yield from postA(tl)
                        drive([laneA([0, 2]), laneA([1, 3])])
                        load_winB()
                        B.barrier()
                        A.release(mA)
                        ck(23)

                        ztB = [A.alloc([128, 864], F32) for _ in range(2)]
                        znB = [A.alloc([128, 768], F32) for _ in range(2)]
                        ssB = [A.alloc([128, 8], F32) for _ in range(2)]
                        jk1 = A.alloc([128, 256], F32)
                        jkB = [jk1, jk1]
                        rpB = [A.alloc([128, 4, 128], F32) for _ in range(2)] if grp == "S" else [None, None]
                        zrB = [A.alloc([128, 384], F32) for _ in range(2)] if grp == "S" else [None, None]
                        qmB = [A.alloc([128, 640], F32) for _ in range(2)]
                        cqB = [A.alloc([128, 2, 128], BF16) for _ in range(2)]
                        rtB = [A.alloc([128, 4, 128], F32) for _ in range(2)] if grp == "S" else [None, None]
                        def projB(tl):
                            ti = half * 4 + tl
                            p = tl % 2
                            zt, zn, ss, junk, rp, zr, qm, cqT, rtab = ztB[p], znB[p], ssB[p], jkB[p], rpB[p], zrB[p], qmB[p], cqB[p], rtB[p]
                            qnope, qrope, qrope_r, kr4 = qm[:, 0:256], qm[:, 256:384], qm[:, 384:512], qm[:, 512:640]
                            tsl = slice(ti * 128, (ti + 1) * 128)
                            for (c0, c1, eng, wb, wk) in ((0, 352, "act", win0, "win0"), (352, 864, "dve", win1, "win1")):
                                pb = ps_get()
                                for k in range(8):
                                    mm(psum[pb][:, 0:c1 - c0], hT[:, k, tl * 128:(tl + 1) * 128], wb[:, k, 0:c1 - c0],
                                       start=(k == 0), stop=(k == 7), r=[("hT", k), wk], w=[PK(pb)])
                                cp(eng, zt[:, c0:c1], psum[pb][:, 0:c1 - c0], r=[PK(pb)], w=[("ztb", p, c0)])

                        def postB(tl):
                            ti = half * 4 + tl
                            p = tl % 2
                            zt, zn, ss, junk, rp, zr, qm, cqT, rtab = ztB[p], znB[p], ssB[p], jkB[p], rpB[p], zrB[p], qmB[p], cqB[p], rtB[p]
                            qnope, qrope, qrope_r, kr4 = qm[:, 0:256], qm[:, 256:384], qm[:, 384:512], qm[:, 512:640]
                            tsl = slice(ti * 128, (ti + 1) * 128)
                            ZR = [("ztb", p, 0), ("ztb", p, 352)]
                            slices = [(0, 192), (192, 128)] + [(352 + 64 * h, 64) for h in range(4)] + \
                                     [(608 + 64 * h, 64) for h in range(2)]
                            SS = [("ss", p, j) for j in range(8)]
                            memset("dve", ss, 0.0, w=SS)
                            for j, (c0, n) in enumerate(slices):
                                act(junk[:, 0:n], zt[:, c0:c0 + n], AF.Square, r=ZR, w=[("junk", p), ("ss", p, j)],
                                    scale=float(n) ** -0.5, accum_out=ss[:, j:j + 1])
                            yield
                            act(ss, ss, AF.Sqrt, r=SS + ["eps"], w=[("ssr", p)], bias=epsc[:, 0:1], scale=1.0)
                            recip(ss, ss, r=[("ssr", p)], w=[("ssr", p)])
                            zdst = [0, 192, 320, 320 + 128, 320 + 64, 320 + 192, 576, 640]
                            gsrc = [0, 192, 320, 320, 320, 320, 384, 384]
                            for j, (c0, n) in enumerate(slices):
                                stt("dve", zn[:, zdst[j]:zdst[j] + n], zt[:, c0:c0 + n], ss[:, j:j + 1], G[:, gsrc[j]:gsrc[j] + n],
                                    ALU.mult, ALU.mult, r=ZR + [("ssr", p), "gtab"], w=[("zn", p, j)])
                            ZN = [("zn", p, j) for j in range(8)]
                            ck(24)
                            yield
                            tr(cqT[:, 0, :], zn[:, 0:128], 128, ZN, [("cqT", p, 0)])
                            tr(cqT[0:64, 1, :], zn[:, 128:192], 64, ZN, [("cqT", p, 1)])
                            pb = ps_get()
                            mm(psum[pb][:, 0:384], cqT[:, 0, :], wuq[:, l * 2, :], start=True, stop=False,
                               r=[("cqT", p, 0), "wuq"], w=[PK(pb)])
                            mm(psum[pb][:, 0:384], cqT[0:64, 1, :], wuq[0:64, l * 2 + 1, :], start=False, stop=True,
                               r=[("cqT", p, 1), "wuq"], w=[PK(pb)])
                            qraw = psum[pb][:, 0:384].rearrange("p (h c) -> p h c", c=96)
                            cp("dve", qnope.rearrange("p (h c) -> p h c", c=64), qraw[:, :, 0:64], r=[PK(pb)], w=[("qnope", p)])
                            cp("dve", qrope.rearrange("p (h c) -> p h c", c=32), qraw[:, :, 64:96], r=[PK(pb)], w=[("qrope", p)])
                            ck(25)
                            yield
                            KR = zt[:, 320:352]
                            DV = zt[:, 736:864]
                            if grp == "S":
                                load_rope(ti, rtab, p)
                                rope(zr[:, 0:256], zn[:, 320:576], 4, 64, 2, 3, ZN, [("zr", p, 2)], rp, rtab, p)
                                yield
                                rope(zr[:, 256:384], zn[:, 576:704], 2, 64, 2, 3, ZN, [("zr", p, 3)], rp, rtab, p)
                                yield
                                rope(qrope_r, qrope, 4, 32, 0, 1, [("qrope", p)], [("qrope_r", p)], rp, rtab, p)
                                yield
                                rope(kr4[:, 0:32], KR, 1, 32, 0, 1, ZR, [("kr4a", p)], rp, rtab, p)
                                for h in range(1, 4):
                                    cp("pool", kr4[:, 32 * h:32 * h + 32], kr4[:, 0:32], r=[("kr4a", p)], w=[("kr4", p)])
                                ck(26)
                                yield
                                for g in range(2):
                                    tr(QnS[:, g, tsl], qnope[:, g * 128:(g + 1) * 128], 128, [("qnope", p)], ["QnS"])
                                    tr(QgS[:, 0, g, tsl], zr[:, g * 128:(g + 1) * 128], 128, [("zr", p, 2)], ["QgS"])
                                    tr(QgS[:, 1, g, tsl], zn[:, 320 + g * 128:320 + (g + 1) * 128], 128, ZN, ["QgS"], eng="act")
                                yield
                                tr(QrS[:, 0, tsl], qrope_r, 128, [("qrope_r", p)], ["QrS"])
                                tr(QrS[:, 1, tsl], qrope, 128, [("qrope", p)], ["QrS"], eng="act")
                                tr(XS[:, X_KG + ti * 128:X_KG + (ti + 1) * 128], zr[:, 256:384], 128, [("zr", p, 3)], ["XS"])
                                tr(XS[:, X_CKV + ti * 128:X_CKV + (ti + 1) * 128], zn[:, 192:320], 128, ZN, ["XS"], eng="act")
                                tr(XS[:, X_KR + ti * 128:X_KR + (ti + 1) * 128], kr4, 128, [("kr4", p), ("kr4a", p)], ["XS"])
                                cp("pool", XS[:, X_VG + ti * 128:X_VG + (ti + 1) * 128], DV, r=ZR, w=["XS"])
                            else:
                                sq_i, pos0 = ti // 2, (ti % 2) * 128
                                B.dma("sp", D["o_kr"][sq_i, l, pos0:pos0 + 128, :], KR, r=ZR)
                                B.dma("sp", D["o_gv"][sq_i, l, pos0:pos0 + 128, :], DV, r=ZR)
                                B.dma("sp", D["o_ckv"][sq_i, l, pos0:pos0 + 128, :], zn[:, 192:320], r=ZN)
                                B.dma("sp", D["o_gk"][sq_i, l, pos0:pos0 + 128, :], zn[:, 576:704], r=ZN)
                                for h in range(4):
                                    cp("pool", kr4[:, 32 * h:32 * h + 32], KR, r=ZR, w=[("kr4", p)])
                                for g in range(2):
                                    tr(QnP[:, g, tsl], qnope[:, g * 128:(g + 1) * 128], 128, [("qnope", p)], ["QnP"])
                                    tr(QgP[:, g, tsl], zn[:, 320 + g * 128:320 + (g + 1) * 128], 128, ZN, ["QgP"], eng="act")
                                yield
                                tr(QrP[:, tsl], qrope, 128, [("qrope", p)], ["QrP"])
                                tr(KgP[:, tsl], zn[:, 576:704], 128, ZN, ["KgP"])
                                tr(KrP[:, tsl], kr4, 128, [("kr4", p)], ["KrP"], eng="act")
                                tr(CkP[:, tsl], zn[:, 192:320], 128, ZN, [("CkP", ti)])
                                vsrc = DV.rearrange("p (a c) -> p a c", c=64)
                                vdst = VgP[:, ti, :].rearrange("p (a b) -> p a b", b=192)
                                cp("pool", vdst[:, :, 0:64], vsrc, r=ZR, w=["VgP"])
                                cp("pool", vdst[:, :, 128:192], vsrc, r=ZR, w=["VgP"])
                                yield
                                pb2 = ps_get()
                                mm(psum[pb2][:, 0:256], CkP[:, tsl], wuv[:, l, :], start=True, stop=True,
                                   r=[("CkP", ti), "wuv"], w=[PK(pb2)])
                                vsrc = psum[pb2][:, 0:256].rearrange("p (a e c) -> p a e c", e=2, c=64)
                                vdst = VmP[:, ti, :].rearrange("p (a b) -> p a b", b=192)
                                cp("dve", vdst[:, :, 0:64], vsrc[:, :, 0, :], r=[PK(pb2)], w=["VmP"])
                                cp("dve", vdst[:, :, 128:192], vsrc[:, :, 1, :], r=[PK(pb2)], w=["VmP"])
                                for g in range(2):
                                    pb3 = ps_get()
                                    mm(psum[pb3][:, 0:128], wuk[:, l, g * 128:(g + 1) * 128], CkP[:, tsl], start=True, stop=True,
                                       r=[("CkP", ti), "wuk"], w=[PK(pb3)])
                                    cp("act", KnP[:, g, tsl], psum[pb3][:, 0:128], r=[PK(pb3)], w=["KnP"])
                            yield

                        def laneB(tiles):
                            for tl in tiles:
                                projB(tl)
                                yield
                                yield from postB(tl)
                        drive([laneB([0, 2]), laneB([1, 3])])
                        if half == 0:
                            load_winA()
                        B.barrier()
                        A.release(m1)
                    A.release(mI)

                def epi_plain(dst_fn, rcp):
                    def epi(ob, nq=None):
                        pass
                    return epi

                def make_epi(dst_fn, eo, nq, rcp, wkey="yT"):
                    def epi(ob):
                        o0, d0 = (0, 64) if eo == 0 else (64, 0)
                        recip(rcp[d0:d0 + 64, 0:nq], psum[ob][d0:d0 + 64, 0:nq], r=[PK(ob)], w=["rcp"])
                        tt("dve", dst_fn(o0, o0 + 64), psum[ob][o0:o0 + 64, 0:nq], rcp[d0:d0 + 64, 0:nq], ALU.mult,
                           r=[PK(ob), "rcp"], w=[wkey])
                    return epi

                def diff_epilogue(Omaps, q0, nq, dtmp):
                    y, ysq, rs = dtmp
                    for pp in range(2):
                        stt("dve", y[:, 0:nq], Omaps[:, 2 * pp + 1, 0:nq], lam[:, l * 4:l * 4 + 1], Omaps[:, 2 * pp, 0:nq],
                            ALU.mult, ALU.add, r=["Om", "lam"], w=["dy"])
                        act(ysq[:, 0:nq], y[:, 0:nq], AF.Square, r=["dy"], w=["dysq"])
                        pb = ps_get()
                        mm(psum[pb][:, 0:nq], onesbd, ysq[:, 0:nq], start=True, stop=True,
                           r=["dysq", "onesbd"], w=[PK(pb)])
                        act(rs[:, 0:nq], psum[pb][:, 0:nq], AF.Sqrt, r=[PK(pb), "eps"], w=["drs"],
                            bias=epsc[:, 0:1], scale=1.0 / 64)
                        recip(rs[:, 0:nq], rs[:, 0:nq], r=["drs"], w=["drs"])
                        stt("dve", y[:, 0:nq], y[:, 0:nq], gsub[:, l:l + 1], rs[:, 0:nq], ALU.mult, ALU.mult,
                            r=["dy", "drs", "gsub"], w=["dy"])
                        act(yT[:, pp, q0:q0 + nq], y[:, 0:nq], AF.Identity, r=["dy"], w=["yT"], scale=1.0 - lam_init)

                def fnet_stage2(PQ, q0, nq, scl):
                    for hc in range(2):
                        pb = ps_get()
                        mm(psum[pb][:, 0:nq], c64, PQ[:, 0, hc, 0:nq], start=True, stop=False, r=["PQ", "c64"], w=[PK(pb)])
                        mm(psum[pb][:, 0:nq], s64n, PQ[:, 1, hc, 0:nq], start=False, stop=True, r=["PQ", "c64"], w=[PK(pb)])
                        act(yT[:, 2 + hc, q0:q0 + nq], psum[pb][:, 0:nq], AF.Identity, r=[PK(pb)], w=["yT"], scale=scl)

                def outproj(t0):
                    v = 1 if t0 >= TP else 0
                    m1 = A.mark()
                    wo = A.alloc([128, 8, 1024], BF16)
                    B.dma("pool", wo, D["wout"][l].rearrange("p (h o) -> p h o", o=1024), w=["wo"])
                    for tq in range(2):
                        for o in range(8):
                            pb = ps_get()
                            osl = slice(o * 128, (o + 1) * 128)
                            qsl = slice(tq * 512, (tq + 1) * 512)
                            for c in range(8):
                                mm(psum[pb][:, :], wo[:, c, osl], yT[:, c, qsl], start=(c == 0), stop=(c == 7),
                                   r=["wo", "yT"], w=[PK(pb)])
                            xs = xT[:, o, t0 + tq * 512:t0 + (tq + 1) * 512]
                            stt("dve", xs, psum[pb][:, :], mods_ap(l, 2, o, v), xs, ALU.mult, ALU.add,
                                r=[PK(pb), "mods", "xT"], w=["xT"])
                    B.barrier()
                    A.release(m1)

                if not DEBUG.get("skipS"):
                    mX = A.mark()
                    XS = A.alloc([128, XW], BF16)
                    inproj("S")
                    ck(2)
                    for j in range(3):
                        B.dma("sp", xin[l][j].ap(), XS[:, j * 4096:j * 4096 + SEGW[j]], r=["XS"], w=[("xin", j)])
                    B.barrier()
                    A.release(mX)
                    for j in range(3):
                        if not DEBUG.get("nocc"):
                            B.op("pool", lambda e, l=l, j=j: e.collective_compute(
                                "AllGather", ALU.bypass, replica_groups=[[0, 1, 2, 3], [4, 5, 6, 7]],
                                ins=[xin[l][j].ap().opt()], outs=[xout[l][j].ap().opt()]),
                                r=[("xin", j)], w=[("xout", j)], cc=True)
                XOUT = [("xout", 0), ("xout", 1), ("xout", 2)]
                xov = [xout[l][j].ap().rearrange("(r p) w -> p r w", p=128) for j in range(3)]

                def xo_piece(r_, off, n):
                    j = off // 4096
                    return xov[j][:, r_, off - j * 4096:off - j * 4096 + n]

                mP = A.mark()
                QdP = A.alloc([128, 2, TP], BF16)
                QnP = A.alloc([128, 2, TP], BF16)
                QrP = A.alloc([128, TP], BF16)
                QgP = A.alloc([128, 2, TP], BF16)
                KdP = A.alloc([128, 2, TP], BF16)
                KnP = A.alloc([128, 2, TP], BF16)
                KrP = A.alloc([128, TP], BF16)
                KgP = A.alloc([128, TP], BF16)
                CkP = A.alloc([128, TP], BF16)
                VdP = A.alloc([128, 8, 384], BF16)
                VmP = A.alloc([128, 8, 384], BF16)
                VgP = A.alloc([128, 8, 384], BF16)
                UP = A.alloc([128, 8, 256], BF16)
                for Vx, kx in ((VdP, "VdP"), (VmP, "VmP"), (VgP, "VgP")):
                    memset("pool", Vx.rearrange("p c (a b) -> p c a b", b=192)[:, :, :, 64:128], 1.0, w=[kx])
                inproj("P")
                ck(3)
                yT = A.alloc([128, 8, 1024], BF16)
                mPa = A.mark()
                ptb = [A.alloc([128, 1024], BF16) for _ in range(3)]
                rcp = A.alloc([128, 512], F32)
                Om = A.alloc([128, 4, 256], F32)
                dtmp = (A.alloc([128, 512], F32), A.alloc([128, 512], BF16), A.alloc([128, 512], F32))
                PQ = A.alloc([128, 2, 2, 512], BF16)

                for s in range(4):
                    q0 = s * 256
                    qs = slice(q0, q0 + 256)
                    jobs = []
                    for m in range(8):
                        g, pr = m // 4, (m % 4) * 32
                        jobs.append(dict(
                            nk=2, r=["QdP", "KdP", "VdP"],
                            qk=lambda kc, g=g, pr=pr, q0=q0: [(KdP[pr:pr + 32, g, q0 + kc * 128:q0 + (kc + 1) * 128],
                                                               QdP[pr:pr + 32, g, q0:q0 + 256])],
                            v=lambda kc, m=m, s=s: VdP[:, s * 2 + kc, (m // 4) * 192 + ((m // 2) % 2) * 64:(m // 4) * 192 + ((m // 2) % 2) * 64 + 128],
                            epi=make_epi(lambda lo, hi, m=m: Om[lo:hi, (m // 4) * 2 + m % 2, :], (m // 2) % 2, 256, rcp, "Om")))
                    attention2([(jobs[2 * i], jobs[2 * i + 1]) for i in range(4)], 32 ** -0.5, ptb, 256)
                    diff_epilogue(Om, q0, 256, dtmp)
                    jobs = []
                    for h in range(4):
                        g, pr = h // 2, (h % 2) * 64
                        jobs.append(dict(
                            nk=2, r=["QnP", "KnP", "QrP", "KrP", "VmP"],
                            qk=lambda kc, g=g, pr=pr, h=h, q0=q0: [
                                (KnP[pr:pr + 64, g, q0 + kc * 128:q0 + (kc + 1) * 128], QnP[pr:pr + 64, g, q0:q0 + 256]),
                                (KrP[32 * h:32 * h + 32, q0 + kc * 128:q0 + (kc + 1) * 128], QrP[32 * h:32 * h + 32, q0:q0 + 256])],
                            v=lambda kc, h=h, s=s: VmP[:, s * 2 + kc, (h // 2) * 192 + (h % 2) * 64:(h // 2) * 192 + (h % 2) * 64 + 128],
                            epi=make_epi(lambda lo, hi, h=h, qs=qs: yT[lo:hi, 4 + h // 2, qs], h % 2, 256, rcp)))
                    attention2([(jobs[0], jobs[1]), (jobs[2], jobs[3])], 96 ** -0.5, ptb, 256)
                    jobs = []
                    for h in range(4):
                        kv, ab = h // 2, h % 2
                        jobs.append(dict(
                            nk=2, r=["QgP", "KgP", "VgP"],
                            qk=lambda kc, kv=kv, ab=ab, q0=q0: [(KgP[64 * kv:64 * kv + 64, q0 + kc * 128:q0 + (kc + 1) * 128],
                                                                 QgP[64 * kv:64 * kv + 64, ab, q0:q0 + 256])],
                            v=lambda kc, kv=kv, ab=ab, s=s: VgP[:, s * 2 + kc, kv * 192 + ab * 64:kv * 192 + ab * 64 + 128],
                            epi=make_epi(lambda lo, hi, h=h, qs=qs: yT[lo:hi, 6 + h // 2, qs], h % 2, 256, rcp)))
                    attention2([(jobs[0], jobs[2]), (jobs[1], jobs[3])], 64 ** -0.5, ptb, 256)
                    for tab, tabt in enumerate((c256, s256)):
                        for hc in range(2):
                            pb = ps_get()
                            for sc in range(2):
                                mm(psum[pb][:, 0:256], UP[:, s * 2 + sc, hc * 128:(hc + 1) * 128], tabt[:, sc, :],
                                   start=(sc == 0), stop=(sc == 1), r=["UP", "c256"], w=[PK(pb)])
                            cp("dve", PQ[:, tab, hc, 0:256], psum[pb][:, 0:256], r=[PK(pb)], w=["PQ"])
                    fnet_stage2(PQ, q0, 256, (256 * 64) ** -0.5)
                B.barrier()
                A.release(mPa)
                ck(4)
                outproj(0)
                ck(5)
                A.release(mP)

                if not DEBUG.get("skipS"):
                    mS = A.mark()
                    yT = A.alloc([128, 8, 1024], BF16)
                    mSa = A.mark()
                    ptb = [A.alloc([128, 1024], BF16) for _ in range(3)]
                    rcp = A.alloc([128, 512], F32)
                    m1 = A.mark()
                    Ug = A.alloc([128, 32, 256], BF16)
                    PQ = A.alloc([128, 2, 2, 512], BF16)
                    tabb = [A.alloc([128, 2, 8, 512], BF16) for _ in range(2)]
                    for r_ in range(4):
                        B.dma("sp", Ug[:, r_ * 8:(r_ + 1) * 8, :],
                              xo_piece(r_, X_U, 2048).rearrange("p (c n) -> p c n", n=256), r=XOUT, w=["Ug"])
                    cbv = D["cbig"].rearrange("p (t s k) -> p t s k", t=2, k=512)
                    sbv = D["sbig"].rearrange("p (t s k) -> p t s k", t=2, k=512)
                    ld = 0
                    for kt in range(2):
                        banks = [ps_get(hold=True) for _ in range(4)]
                        for s8 in range(4):
                            tb = tabb[ld % 2]
                            key = ("tabb", ld % 2)
                            ld += 1
                            B.dma("sp", tb[:, 0, :, :], cbv[:, kt, s8 * 8:(s8 + 1) * 8, :], w=[key])
                            B.dma("sp", tb[:, 1, :, :], sbv[:, kt, s8 * 8:(s8 + 1) * 8, :], w=[key])
                            for si in range(8):
                                sc = s8 * 8 + si
                                for tab in range(2):
                                    for hc in range(2):
                                        b = banks[tab * 2 + hc]
                                        mm(psum[b][:, :], Ug[:, sc, hc * 128:(hc + 1) * 128], tb[:, tab, si, :],
                                           start=(sc == 0), stop=(sc == 31), r=["Ug", key], w=[PK(b)])
                        for tab in range(2):
                            for hc in range(2):
                                b = banks[tab * 2 + hc]
                                cp("dve" if hc else "act", PQ[:, tab, hc, :], psum[b][:, :], r=[PK(b)], w=["PQ"])
                                ps_rel(b)
                        fnet_stage2(PQ, kt * 512, 512, (4096 * 64) ** -0.5)
                    B.barrier()
                    A.release(m1)

                    ck(6)
                    m1 = A.mark()
                    Kd = A.alloc([128, 2, 4608], BF16)
                    Vd = A.alloc([128, 36, 384], BF16)
                    Om = A.alloc([128, 4, 512], F32)
                    dtmp = (A.alloc([128, 512], F32), A.alloc([128, 512], BF16), A.alloc([128, 512], F32))
                    memset("pool", Vd.rearrange("p c (a b) -> p c a b", b=192)[:, :, :, 64:128], 1.0, w=["Vdones"])

                    def vload(q, Vx, c0, nchunk, src, key, rkeys):
                        sv = src.rearrange("p (c a e n) -> p c a e n", a=2, e=2, n=64)
                        dv = Vx[:, c0:c0 + nchunk, :].rearrange("p c (a b) -> p c a b", b=192)
                        for a in range(2):
                            B.dma(q, dv[:, :, a, 0:64], sv[:, :, a, 0, :], r=rkeys, w=[key])
                            B.dma(q, dv[:, :, a, 128:192], sv[:, :, a, 1, :], r=rkeys, w=[key])
                    for r_ in range(4):
                        B.dma("sp", Kd[:, :, r_ * 1024:(r_ + 1) * 1024],
                              xo_piece(r_, X_KD, 2048).rearrange("p (g n) -> p g n", n=1024), r=XOUT, w=[("Kd", r_)])
                        vload("sp", Vd, r_ * 8, 8, xo_piece(r_, X_VD, 2048), ("Vd", r_), XOUT)
                    B.dma("pool", Kd[:, :, 4096:4608], D["ckdT"][l].rearrange("p (g n) -> p g n", n=512), w=[("Kd", 4)])
                    vload("pool", Vd, 32, 4, D["cvd"][l], ("Vd", 4), [])
                    QB = [A.alloc([128, 2, 512], BF16) for _ in range(4)]
                    for b_ in range(4):
                        memset("pool", QB[b_], 0.0, w=[("QB", b_)])
                    for tq in range(2):
                        q0 = tq * 512
                        jobs = []
                        for m in range(8):
                            g, pr = m // 4, (m % 4) * 32
                            jobs.append(dict(
                                nk=36, r=[("QB", m % 4), "Vdones"],
                                rk=lambda kc: [('Kd', kc // 8 if kc < 32 else 4), ('Vd', kc // 8 if kc < 32 else 4)],
                                pre=lambda g=g, pr=pr, q0=q0, b_=m % 4: cp("pool", QB[b_][pr:pr + 32, :, :],
                                                                           QdS[pr:pr + 32, :, g, q0:q0 + 512],
                                                                           r=["QdS"], w=[("QB", b_)]),
                                qk=lambda kc, g=g, b_=m % 4: [(Kd[:, g, kc * 128:(kc + 1) * 128],
                                                               QB[b_][:, 0 if kc < 32 else 1, :])],
                                v=lambda kc, m=m: Vd[:, kc, (m // 4) * 192 + ((m // 2) % 2) * 64:(m // 4) * 192 + ((m // 2) % 2) * 64 + 128],
                                epi=make_epi(lambda lo, hi, m=m: Om[lo:hi, (m // 4) * 2 + m % 2, :], (m // 2) % 2, 512, rcp, "Om")))
                        attention2([(jobs[2 * i], jobs[2 * i + 1]) for i in range(4)], 32 ** -0.5, ptb, 512)
                        diff_epilogue(Om, q0, 512, dtmp)
                    B.barrier()
                    A.release(m1)

                    ck(7)
                    m1 = A.mark()
                    Ck = A.alloc([128, 4608], BF16)
                    for r_ in range(4):
                        B.dma("sp", Ck[:, r_ * 1024:(r_ + 1) * 1024], xo_piece(r_, X_CKV, 1024), r=XOUT, w=[("Ck", r_)])
                    B.dma("pool", Ck[:, 4096:4608], D["cckvT"][l], w=[("Ck", 4)])
                    for hp in range(2):
                        m2 = A.mark()
                        KK = [A.alloc([128, 4608], BF16) for _ in range(2)]
                        Vm = A.alloc([128, 36, 192], BF16)
                        QMB = [A.alloc([128, 2, 512], BF16) for _ in range(4)]
                        memset("pool", Vm[:, :, 64:128], 1.0, w=["Vmones"])
                        for hh in range(2):
                            h = hp * 2 + hh
                            memset("pool", KK[hh][96:128, :], 0.0, w=[("KK", hh, "z")])
                            for tq_ in range(2):
                                memset("pool", QMB[tq_ * 2 + hh][96:128, :, :], 0.0, w=[("QMB", tq_ * 2 + hh)])
                            for r_ in range(4):
                                B.dma("sp", KK[hh][64:96, r_ * 1024:(r_ + 1) * 1024], xo_piece(r_, X_KR, 1024)[64:96, :],
                                      r=XOUT, w=[("KK", hh, "r", r_)])
                            B.dma("pool", KK[hh][64:96, 4096:4608], D["ckrT"][l][64:96, :], w=[("KK", hh, "r", 4)])
                            for kt in range(9):
                                pb = ps_get()
                                mm(psum[pb][0:64, :], wuk[:, l, h * 64:(h + 1) * 64], Ck[:, kt * 512:(kt + 1) * 512],
                                   start=True, stop=True, r=[("Ck", kt // 2), "wuk"], w=[PK(pb)])
                                cp("dve" if kt % 2 else "act", KK[hh][0:64, kt * 512:(kt + 1) * 512], psum[pb][0:64, :],
                                   r=[PK(pb)], w=[("KK", hh, "n", kt)])
                        for kc in range(36):
                            pb = ps_get()
                            mm(psum[pb][:, 0:128], Ck[:, kc * 128:(kc + 1) * 128], wuv[:, l, hp * 128:(hp + 1) * 128], start=True, stop=True,
                               r=[("Ck", kc // 8 if kc < 32 else 4), "wuv"], w=[PK(pb)])
                            cp("dve", Vm[:, kc, :].rearrange("p (a b) -> p a b", b=64)[:, 0:3:2, :],
                               psum[pb][:, 0:128].rearrange("p (h c) -> p h c", c=64), r=[PK(pb)], w=[("Vm", kc)])
                        mpairs = []
                        for tq in range(2):
                            q0 = tq * 512
                            jobs = []
                            for hh in range(2):
                                h = hp * 2 + hh
                                pr = hh * 64
                                qi = tq * 2 + hh

                                def pre(qi=qi, h=h, pr=pr, hp=hp, q0=q0):
                                    for ver in range(2):
                                        cp("dve", QMB[qi][0:64, ver, :], QnS[pr:pr + 64, hp, q0:q0 + 512], r=["QnS"], w=[("QMB", qi)])
                                        cp("dve", QMB[qi][64:96, ver, :], QrS[32 * h:32 * h + 32, ver, q0:q0 + 512],
                                           r=["QrS"], w=[("QMB", qi)])
                                jobs.append(dict(
                                    nk=36, r=[("QMB", qi), ("KK", hh, "z"), "Vmones"], pre=pre,
                                    rk=lambda kc, hh=hh: [("KK", hh, "r", kc // 8 if kc < 32 else 4), ("KK", hh, "n", kc // 4),
                                                          ("Vm", kc)],
                                    qk=lambda kc, hh=hh, qi=qi: [(KK[hh][:, kc * 128:(kc + 1) * 128], QMB[qi][:, 0 if kc < 32 else 1, :])],
                                    v=lambda kc, hh=hh: Vm[:, kc, hh * 64:hh * 64 + 128],
                                    epi=make_epi(lambda lo, hi, hp=hp, q0=q0: yT[lo:hi, 4 + hp, q0:q0 + 512], hh, 512, rcp)))
                            mpairs.append((jobs[0], jobs[1]))
                        attention2(mpairs, 96 ** -0.5, ptb, 512)
                        B.barrier()
                        A.release(m2)
                    A.release(m1)

                    ck(8)
                    m1 = A.mark()
                    Kg = A.alloc([128, 4608], BF16)
                    Vg = A.alloc([128, 36, 384], BF16)
                    memset("pool", Vg.rearrange("p c (a b) -> p c a b", b=192)[:, :, :, 64:128], 1.0, w=["Vgones"])

                    def vloadg(q, c0, nchunk, src, rkeys):
                        sv = src.rearrange("p (c a n) -> p c a n", a=2, n=64)
                        dv = Vg[:, c0:c0 + nchunk, :].rearrange("p c (a b) -> p c a b", b=192)
                        B.dma(q, dv[:, :, :, 0:64], sv, r=rkeys, w=[("Vg", c0 // 8)])
                        B.dma(q, dv[:, :, :, 128:192], sv, r=rkeys, w=[("Vg", c0 // 8)])
                    for r_ in range(4):
                        B.dma("sp", Kg[:, r_ * 1024:(r_ + 1) * 1024], xo_piece(r_, X_KG, 1024), r=XOUT, w=[("Kg", r_)])
                        vloadg("sp", r_ * 8, 8, xo_piece(r_, X_VG, 1024), XOUT)
                    B.dma("pool", Kg[:, 4096:4608], D["ckgT"][l], w=[("Kg", 4)])
                    vloadg("pool", 32, 4, D["cvg"][l], [])
                    for tq in range(2):
                        q0 = tq * 512
                        jobs = []
                        for h in range(4):
                            kv, ab = h // 2, h % 2
                            jobs.append(dict(
                                nk=36, r=["QgS", "Vgones"],
                                rk=lambda kc: [('Kg', kc // 8 if kc < 32 else 4), ('Vg', kc // 8 if kc < 32 else 4)],
                                qk=lambda kc, kv=kv, ab=ab, q0=q0: [(Kg[64 * kv:64 * kv + 64, kc * 128:(kc + 1) * 128],
                                                                     QgS[64 * kv:64 * kv + 64, 0 if kc < 32 else 1, ab, q0:q0 + 512])],
                                v=lambda kc, kv=kv, ab=ab: Vg[:, kc, kv * 192 + ab * 64:kv * 192 + ab * 64 + 128],
                                epi=make_epi(lambda lo, hi, h=h, q0=q0: yT[lo:hi, 6 + h // 2, q0:q0 + 512], h % 2, 512, rcp)))
                        attention2([(jobs[0], jobs[2]), (jobs[1], jobs[3])], 64 ** -0.5, ptb, 512)
                    B.barrier()
                    A.release(mSa)
                    ck(9)
                    outproj(TP)
                    ck(10)
                A.release(mL)

                mM = A.mark()
                hT2 = A.alloc([128, 8, T], BF16)
                m1 = A.mark()
                sq = [A.alloc([128, 8, 512], BF16) for _ in range(2)]
                ntmp = [A.alloc([128, 2, 512], F32) for _ in range(2)]
                rstd = [A.alloc([128, 512], F32) for _ in range(2)]
                for grp_t0, v in ((0, 0), (TP, 1)):
                    fm_norm(grp_t0, 1024, lambda k, o, g0=grp_t0: hT2[:, k, g0 + o:g0 + o + 512],
                            lambda k, v=v: modA_ap(l, v, 1, k), lambda k, v=v: mods_ap(l, 3, k, v), ntmp, sq, rstd)
                B.barrier()
                A.release(m1)
                hid = A.alloc([128, 4, T], BF16)
                rl = [A.alloc([128, 512], F32) for _ in range(2)]
                w1b = [A.alloc([128, 8, 512], BF16) for _ in range(2)]
                w2b = [A.alloc([128, 4, 1024], BF16) for _ in range(2)]
                adw = [A.alloc([128, 8, 1024], BF16) for _ in range(2)] if l + 1 < NL else None
                w1v = D["w1"][l].rearrange("p (k c) -> p k c", c=4096)
                w2v = D["w2"][l].rearrange("p (j o) -> p j o", o=1024)
                for e8 in range(8):
                    wb1, wb2 = w1b[e8 % 2], w2b[e8 % 2]
                    k1, k2 = ("w1b", e8 % 2), ("w2b", e8 % 2)
                    B.dma("pool", wb1, w1v[:, :, e8 * 512:(e8 + 1) * 512], w=[k1])
                    B.dma("pool", wb2, w2v[:, e8 * 4:(e8 + 1) * 4, :], w=[k2])
                    if adw is not None and e8 < 6:
                        mods_load(l + 1, e8, adw[e8 % 2], ("adw", e8 % 2))
                    for tq in range(4):
                        qsl = slice(tq * 512, (tq + 1) * 512)
                        for jj in range(4):
                            pb = ps_get()
                            for k in range(8):
                                mm(psum[pb][:, :], wb1[:, k, jj * 128:(jj + 1) * 128], hT2[:, k, qsl], start=(k == 0), stop=(k == 7),
                                   r=[k1, ("hT", k)], w=[PK(pb)])
                            rk = ("rl", (tq * 4 + jj) % 2)
                            act(rl[(tq * 4 + jj) % 2], psum[pb][:, :], AF.Relu, r=[PK(pb)], w=[rk])
                            tt("pool", hid[:, jj, qsl], rl[(tq * 4 + jj) % 2], rl[(tq * 4 + jj) % 2], ALU.mult, r=[rk],
                               w=[("hid", jj, tq)])
                    for tq in range(4):
                        qsl = slice(tq * 512, (tq + 1) * 512)
                        v = 0 if tq < 2 else 1
                        for o in range(8):
                            pb = ps_get()
                            for jj in range(4):
                                mm(psum[pb][:, :], wb2[:, jj, o * 128:(o + 1) * 128], hid[:, jj, qsl], start=(jj == 0), stop=(jj == 3),
                                   r=[k2, ("hid", jj, tq)], w=[PK(pb)])
                            stt("dve", xT[:, o, qsl], psum[pb][:, :], mods_ap(l, 5, o, v), xT[:, o, qsl], ALU.mult, ALU.add,
                                r=[PK(pb), "mods", "xT"], w=["xT"])
                    if adw is not None and e8 < 6:
                        mods_piece(l + 1, e8, adw[e8 % 2], ("adw", e8 % 2))
                if adw is not None:
                    mods_finish(l + 1)
                B.barrier()
                A.release(mM)
                ck(11)

            mF = A.mark()
            sq = [A.alloc([128, 8, 512], BF16) for _ in range(2)]
            ntmp = [A.alloc([128, 2, 512], F32) for _ in range(2)]
            rstd = [A.alloc([128, 512], F32) for _ in range(2)]
            yo = A.alloc([128, 8, 1024], F32)
            yTv = D["yT"].rearrange("p (c t) -> p c t", t=T)
            for g0 in (0, TP):
                fm_norm(g0, 1024, lambda k, o: yo[:, k, o:o + 512], lambda k: gains[:, 32 + k:33 + k], lambda k: None,
                        ntmp, sq, rstd)
                B.dma("sp", yTv[:, :, g0:g0 + 1024], yo, r=[("hT", k) for k in range(8)], w=["yout"])

        except StopBuild:
            pass
        print("arena peak words", A.peak, "of", NW, {e: len(B.ops[e]) for e in B.ENGS})
        block = es.enter_context(nc.Block())
        B.emit(block, sems, dsems)
    return nc


def _prep(inp):
    f32 = np.float32
    bf = ml_dtypes.bfloat16

    def fm(w, k):
        C = w.shape[1]
        return np.ascontiguousarray(w.reshape(k, 128, C).transpose(1, 0, 2).reshape(128, k * C))

    shared = {}
    shared["adaw"] = np.stack([fm(inp["ada_w"][l], 8) for l in range(NL)])
    shared["adab"] = np.stack([np.ascontiguousarray(inp["ada_b"][l].reshape(48, 128).T) for l in range(NL)])
    shared["gmix"] = np.stack([np.ascontiguousarray(inp["norm_mix_g"][l].reshape(8, 128).T) for l in range(NL)])
    shared["gmlp"] = np.stack([np.ascontiguousarray(inp["norm_mlp_g"][l].reshape(8, 128).T) for l in range(NL)])
    shared["gfin"] = np.ascontiguousarray(inp["final_norm_g"].reshape(8, 128).T)
    shared["win"] = np.stack([fm(inp["w_in"][l], 8) for l in range(NL)])
    wuq = np.zeros((NL, 128, 2, 384), f32)
    for l in range(NL):
        wuq[l, :, 0, :] = inp["mla_w_uq"][l][0:128]
        wuq[l, 0:64, 1, :] = inp["mla_w_uq"][l][128:192]
    shared["wuq"] = wuq.reshape(NL, 128, 768)
    ukv = inp["mla_w_ukv"].reshape(NL, 128, 4, 128)
    shared["wuk"] = np.ascontiguousarray(ukv[:, :, :, 0:64].reshape(NL, 128, 256))
    shared["wuv"] = np.ascontiguousarray(ukv[:, :, :, 64:128].reshape(NL, 128, 256))
    shared["wout"] = np.stack([fm(inp["w_out"][l], 8) for l in range(NL)])
    shared["w1"] = np.stack([fm(inp["mlp_w1"][l], 8) for l in range(NL)])
    shared["w2"] = np.stack([fm(inp["mlp_w2"][l], 32) for l in range(NL)])
    gt = np.concatenate([inp["mla_q_norm_g"], inp["mla_kv_norm_g"], inp["gqa_q_norm_g"], inp["gqa_k_norm_g"]], axis=1)
    shared["gt"] = np.ascontiguousarray(np.broadcast_to(gt[:, None, :], (NL, 128, 448))).astype(f32)
    shared["gsub"] = np.ascontiguousarray(inp["diff_subln_g"].reshape(NL, 64, 1))
    shared["lamp"] = np.ascontiguousarray(np.broadcast_to(inp["diff_lambda"].reshape(NL, 1, 128), (NL, 128, 128))).astype(f32)
    shared["ident"] = np.eye(128, dtype=f32)
    s = np.arange(256, dtype=np.float64)
    ang = 2 * np.pi * np.outer(s, s) / 256
    shared["c256"] = fm(np.cos(ang), 2).astype(bf)
    shared["s256"] = fm(np.sin(ang), 2).astype(bf)
    c = np.arange(64, dtype=np.float64)
    a64 = 2 * np.pi * np.outer(c, c) / 64
    c64 = np.zeros((128, 128)); s64 = np.zeros((128, 128))
    for g in range(2):
        c64[g * 64:(g + 1) * 64, g * 64:(g + 1) * 64] = np.cos(a64)
        s64[g * 64:(g + 1) * 64, g * 64:(g + 1) * 64] = -np.sin(a64)
    shared["c64"] = c64.astype(bf)
    shared["s64n"] = s64.astype(bf)

    def rope_tabs(pos, d, H):
        a = d // 2
        inv = 10000.0 ** (-np.arange(0, a, 2, dtype=np.float64) / a)
        row = (pos // 64).astype(np.float64); col = (pos % 64).astype(np.float64)
        cr, sr = np.cos(row[:, None] * inv), np.sin(row[:, None] * inv)
        cc, sc = np.cos(col[:, None] * inv), np.sin(col[:, None] * inv)
        cs = np.stack([cr, cc], axis=1)
        sn = np.stack([sr, sc], axis=1)
        cs = np.broadcast_to(cs[:, None], (len(pos), H, 2, a // 2)).reshape(len(pos), -1)
        sn = np.broadcast_to(sn[:, None], (len(pos), H, 2, a // 2)).reshape(len(pos), -1)

        def lay(x):
            return np.ascontiguousarray(x.reshape(8, 128, 128).transpose(1, 0, 2).reshape(128, 1024)).astype(f32)
        return lay(cs), lay(sn)

    maps = []
    kk = np.arange(1024, dtype=np.float64)
    ss_ = np.arange(4096, dtype=np.float64)
    for i in range(8):
        b, r = i // 4, i % 4
        m = dict(shared)
        xp = inp["x_prompt"][4 * i:4 * i + 4].reshape(1024, 1024)
        xs = inp["x_sample"][b, r * 1024:(r + 1) * 1024]
        x = np.concatenate([xp, xs], axis=0)
        m["xT"] = np.ascontiguousarray(x.T.reshape(8, 128, T).transpose(1, 0, 2).reshape(128, 8 * T))
        cvv = np.stack([inp["c_ctx"], inp["c"][b]], axis=1)
        m["cv"] = np.ascontiguousarray(cvv.reshape(8, 128, 2).transpose(1, 0, 2).reshape(128, 16))
        pos = np.arange(r * 1024, (r + 1) * 1024)
        m["rc32"], m["rs32"] = rope_tabs(pos, 32, 8)
        m["rc64"], m["rs64"] = rope_tabs(pos, 64, 4)
        ang = 2 * np.pi * ((np.outer(ss_, kk + r * 1024)) % 4096) / 4096
        def lay_big(x):
            return np.ascontiguousarray(x.reshape(32, 128, 2, 512).transpose(1, 2, 0, 3).reshape(128, 32 * 1024)).astype(bf)
        m["cbig"] = lay_big(np.cos(ang))
        m["sbig"] = lay_big(np.sin(ang))
        m["ckdT"] = np.stack([np.ascontiguousarray(inp["cache_diff_k"][b, l].reshape(512, 2, 128).transpose(2, 1, 0).reshape(128, 1024)) for l in range(NL)])
        m["cvd"] = np.stack([fm(inp["cache_diff_v"][b, l].reshape(512, 256), 4) for l in range(NL)])
        m["cckvT"] = np.stack([np.ascontiguousarray(inp["cache_mla_ckv"][b, l].T) for l in range(NL)])
        m["ckrT"] = np.stack([np.ascontiguousarray(np.tile(inp["cache_mla_krope"][b, l].T, (4, 1))) for l in range(NL)])
        m["ckgT"] = np.stack([np.ascontiguousarray(inp["cache_gqa_k"][b, l].reshape(512, 128).T) for l in range(NL)])
        m["cvg"] = np.stack([fm(inp["cache_gqa_v"][b, l].reshape(512, 128), 4) for l in range(NL)])
        maps.append(m)
    return maps


_NC = None


def kernel(**inputs):
    global _NC
    inp = {k: np.asarray(v) for k, v in inputs.items()}
    maps = _prep(inp)
    if _NC is None:
        _NC = build_program()
    res = run_bass_kernel_spmd(_NC, maps, core_ids=list(range(8)))
    R = res.results
    y_prompt = np.zeros((32, 256, 1024), np.float32)
    y_sample = np.zeros((2, 4096, 1024), np.float32)
    outs = {k: np.zeros(s, np.float32) for k, s in (("o_dk", (32, NL, 256, 4, 64)), ("o_dv", (32, NL, 256, 4, 64)),
                                                     ("o_ckv", (32, NL, 256, 128)), ("o_kr", (32, NL, 256, 32)),
                                                     ("o_gk", (32, NL, 256, 2, 64)), ("o_gv", (32, NL, 256, 2, 64)))}
    for i in range(8):
        b, r = i // 4, i % 4
        yT = np.asarray(R[i]["yT"]).reshape(128, 8, T)
        y = yT.transpose(2, 1, 0).reshape(T, 1024)
        y_prompt[4 * i:4 * i + 4] = y[0:TP].reshape(4, 256, 1024)
        y_sample[b, r * 1024:(r + 1) * 1024] = y[TP:]
        for k in outs:
            outs[k][4 * i:4 * i + 4] = np.asarray(R[i][k]).reshape(outs[k][4 * i:4 * i + 4].shape)
    return (y_prompt, y_sample, outs["o_dk"], outs["o_dv"], outs["o_ckv"], outs["o_kr"], outs["o_gk"], outs["o_gv"])
```

```python
import contextlib
import math
import numpy as np
import ml_dtypes
import concourse.bass as bass
import concourse.mybir as mybir
from concourse.bass_utils import run_bass_kernel_spmd

F32 = mybir.dt.float32
BF16 = mybir.dt.bfloat16
AF = mybir.ActivationFunctionType
ALU = mybir.AluOpType

NL = 2
EPS = 1e-6
TP = 1024
TS = 1024
T = TP + TS
XW = 10240
X_KD, X_KG, X_CKV, X_KR, X_VD, X_VG, X_U = 0, 2048, 3072, 4096, 5120, 7168, 8192
SAME_ENG_SYNC = True
DEBUG = {}
NDMASEM = 12


class Op:
    __slots__ = ("eng", "fn", "deps", "dma", "sem", "cnt", "need", "idx", "cc")

    def __init__(self, eng, fn, dma):
        self.eng, self.fn, self.dma = eng, fn, dma
        self.cc = False
        self.deps = set()
        self.sem = None
        self.cnt = 0
        self.need = False


class StopBuild(Exception):
    pass


def ck(n):
    if DEBUG.get('stop') == n:
        raise StopBuild()


class Builder:
    ENGS = ("pe", "act", "dve", "pool", "sp")

    def __init__(self, nc):
        self.nc = nc
        self.ops = {e: [] for e in self.ENGS}
        self.lastw = {}
        self.readers = {}
        self.pending = {e: set() for e in self.ENGS}
        self.dma_ring = {"sp": [None] * NDMASEM, "pool": [None] * NDMASEM}
        self.dma_rr = {"sp": 0, "pool": 0}
        self.dmas_since_bar = []

    def op(self, eng, fn, r=(), w=(), dma=False, slot=None, cc=False):
        o = Op(eng, fn, dma)
        o.cc = cc
        deps = set()
        if dma:
            i = self.dma_rr[eng]
            self.dma_rr[eng] = (i + 1) % NDMASEM
            prev = self.dma_ring[eng][i]
            if prev is not None:
                deps.add(prev)
            self.dma_ring[eng][i] = o
            o.sem = (eng, i)
        slot = o.sem if dma else (("cc", len(self.ops[eng])) if cc else eng)
        for k in r:
            deps.update(self.lastw.get(k, {}).values())
        for k in w:
            deps.update(self.lastw.get(k, {}).values())
            deps.update(self.readers.get(k, {}).values())
        for k in r:
            self.readers.setdefault(k, {})[slot] = o
        for k in w:
            self.lastw.setdefault(k, {})[slot] = o
            self.readers[k] = {}
        deps.update(self.pending[eng])
        self.pending[eng] = set()
        deps.discard(o)
        o.deps = deps
        self.ops[eng].append(o)
        return o

    def dma(self, eng, out, in_, r=(), w=()):
        return self.op(eng, lambda e: e.dma_start(out=out, in_=in_), r=r, w=w, dma=True)

    def barrier(self):
        lasts = set()
        for e in self.ENGS:
            for o in reversed(self.ops[e]):
                if not o.dma and not o.cc:
                    lasts.add(o)
                    break
        for e in ("sp", "pool"):
            for o in self.dma_ring[e]:
                if o is not None:
                    lasts.add(o)
        for e in self.ENGS:
            self.pending[e] |= lasts

    def emit(self, block, sems, dsems):
        nc = self.nc
        for e in self.ENGS:
            for o in self.ops[e]:
                for d in o.deps:
                    if d.dma or d.cc:
                        continue
                    if d.eng == o.eng and (d.eng == "pe" or not SAME_ENG_SYNC):
                        continue
                    d.need = True
        for e in self.ENGS:
            c = 0
            cd = {}
            for o in self.ops[e]:
                if o.dma:
                    cd[o.sem] = cd.get(o.sem, 0) + 16
                    o.cnt = cd[o.sem]
                elif o.cc:
                    cd["cc"] = cd.get("cc", 0) + 1
                    o.cnt = cd["cc"]
                elif o.need:
                    c += 1
                    o.cnt = c

        def run(engname, engobj):
            waited = {}
            for o in self.ops[engname]:
                for d in sorted(o.deps, key=lambda x: (x.eng, x.cnt)):
                    if d.dma:
                        key, s, v = d.sem, dsems[d.sem], d.cnt
                    elif d.cc:
                        key, s, v = "cc", dsems["cc"], d.cnt
                    else:
                        if d.eng == o.eng and (d.eng == "pe" or not SAME_ENG_SYNC):
                            continue
                        key, s, v = d.eng, sems[d.eng], d.cnt
                    if waited.get(key, 0) >= v:
                        continue
                    waited[key] = v
                    engobj.wait_ge(s, v)
                ins = o.fn(engobj)
                if o.dma:
                    ins.then_inc(dsems[o.sem], 16)
                elif o.cc:
                    ins.then_inc(dsems["cc"], 1)
                elif o.need:
                    ins.then_inc(sems[o.eng], 1)

        @block.tensor
        def _(t):
            run("pe", t)

        @block.scalar
        def _(t):
            run("act", t)

        @block.vector
        def _(t):
            run("dve", t)

        @block.gpsimd
        def _(t):
            run("pool", t)

        @block.sync
        def _(t):
            run("sp", t)
            for q in ("sp", "pool"):
                for o in self.dma_ring[q]:
                    if o is not None:
                        t.wait_ge(dsems[o.sem], o.cnt)


class Arena:
    def __init__(self, tensor, nwords):
        self.t = tensor
        self.n = nwords
        self.top = 0
        self.peak = 0

    def alloc(self, shape, dt):
        free = 1
        for s in shape[1:]:
            free *= s
        words = free if dt == F32 else (free + 1) // 2
        words = (words + 7) // 8 * 8
        off = self.top
        self.top += words
        self.peak = max(self.peak, self.top)
        assert self.top <= self.n, f"arena overflow {self.top} > {self.n}"
        ap = self.t[0:shape[0], off:off + words]
        if dt != F32:
            ap = ap.bitcast(dt)
        ap = ap[:, 0:free]
        if len(shape) == 3:
            ap = ap.rearrange("p (a b) -> p a b", b=shape[2])
        elif len(shape) == 4:
            ap = ap.rearrange("p (a b c) -> p a b c", b=shape[2], c=shape[3])
        return ap

    def mark(self):
        return self.top

    def release(self, m):
        self.top = m


def build_program():
    nc = bass.Bass("TRN2", target_bir_lowering=False)
    D = {}

    def din(name, shape, dt=F32):
        D[name] = nc.dram_tensor(name, list(shape), dt, kind="ExternalInput").ap()
        return D[name]

    def dout(name, shape, dt=F32):
        D[name] = nc.dram_tensor(name, list(shape), dt, kind="ExternalOutput").ap()
        return D[name]

    din("xT", [128, 8 * T])
    din("cv", [128, 16])
    din("adaw", [NL, 128, 8 * 6144])
    din("adab", [NL, 128, 48])
    din("gmix", [NL, 128, 8])
    din("gmlp", [NL, 128, 8])
    din("gfin", [128, 8])
    din("win", [NL, 128, 8 * 1888])
    din("wuq", [NL, 128, 2 * 384])
    din("wuk", [NL, 128, 256])
    din("wuv", [NL, 128, 256])
    din("wout", [NL, 128, 8 * 1024])
    din("w1", [NL, 128, 8 * 4096])
    din("w2", [NL, 128, 32 * 1024])
    din("gt", [NL, 128, 448])
    din("gsub", [NL, 64, 1])
    din("lamp", [NL, 128, 128])
    for nm in ("rc32", "rs32", "rc64", "rs64"):
        din(nm, [128, 8 * 128])
    din("c256", [128, 2 * 256], BF16)
    din("s256", [128, 2 * 256], BF16)
    din("cbig", [128, 32 * 1024], BF16)
    din("sbig", [128, 32 * 1024], BF16)
    din("c64", [128, 128], BF16)
    din("s64n", [128, 128], BF16)
    din("ident", [128, 128])
    din("ckdT", [NL, 128, 2 * 512])
    din("cvd", [NL, 128, 4 * 256])
    din("cckvT", [NL, 128, 512])
    din("ckrT", [NL, 128, 512])
    din("ckgT", [NL, 128, 512])
    din("cvg", [NL, 128, 4 * 128])
    dout("yT", [128, 8 * T])
    dout("o_dk", [4, NL, 256, 256])
    dout("o_dv", [4, NL, 256, 256])
    dout("o_ckv", [4, NL, 256, 128])
    dout("o_kr", [4, NL, 256, 32])
    dout("o_gk", [4, NL, 256, 128])
    dout("o_gv", [4, NL, 256, 128])
    SEGW = (4096, 4096, 2048)
    xin = [[nc.dram_tensor(f"xin{l}_{j}", [128, SEGW[j]], BF16) for j in range(3)] for l in range(NL)]
    xout = [[nc.dram_tensor(f"xout{l}_{j}", [4 * 128, SEGW[j]], BF16) for j in range(3)] for l in range(NL)]

    es = contextlib.ExitStack()
    with es:
        NW = 52000
        arena_t = es.enter_context(nc.sbuf_tensor("arena", [128, NW], F32))
        A = Arena(arena_t, NW)
        psall = es.enter_context(nc.psum_tensor("psall", [128, 4096], F32))
        psum = [psall[:, i * 512:(i + 1) * 512] for i in range(8)]
        sems = {e: es.enter_context(nc.semaphore(f"s_{e}")) for e in Builder.ENGS}
        dsems = {(q, i): es.enter_context(nc.semaphore(f"d_{q}{i}")) for q in ("sp", "pool") for i in range(NDMASEM)}
        s_cc = es.enter_context(nc.semaphore("s_cc"))
        dsems["cc"] = s_cc
        B = Builder(nc)

        ps_state = {"rr": 0, "held": set(), "lo": 0}

        def ps_get(hold=False):
            for _ in range(8):
                i = ps_state["rr"]
                ps_state["rr"] = (i + 1) % 8
                if i < ps_state["lo"]:
                    continue
                if i not in ps_state["held"]:
                    if hold:
                        ps_state["held"].add(i)
                    return i
            raise RuntimeError("no psum bank")

        def ps_rel(i):
            ps_state["held"].discard(i)

        def PK(i):
            return ("ps", i)

        uid = [0]

        def U():
            uid[0] += 1
            return ("u", uid[0])

        def mm(out, lhsT, rhs, start, stop, r, w):
            try:
                bp = lhsT.base_partition()
            except AssertionError:
                bp = 96
            kw = {"tile_position": (96, 0)} if bp == 96 else {}
            return B.op("pe", lambda e: e.matmul(out, lhsT, rhs, start=start, stop=stop, **kw), r=r, w=w)

        def transpose(out, in_, ident, r, w):
            return B.op("pe", lambda e: e.transpose(out, in_, ident), r=r, w=w)

        def act(out, in_, func, r, w, bias=None, scale=None, accum_out=None):
            kw = {}
            if bias is not None:
                kw["bias"] = bias
            if scale is not None:
                kw["scale"] = scale
            if accum_out is not None:
                kw["accum_out"] = accum_out
            return B.op("act", lambda e: e.activation(out=out, in_=in_, func=func, **kw), r=r, w=w)

        def tt(eng, out, in0, in1, op, r, w):
            return B.op(eng, lambda e: e.tensor_tensor(out=out, in0=in0, in1=in1, op=op), r=r, w=w)

        def stt(eng, out, in0, scalar, in1, op0, op1, r, w):
            return B.op(eng, lambda e: e.scalar_tensor_tensor(out=out, in0=in0, scalar=scalar, in1=in1,
                                                              op0=op0, op1=op1), r=r, w=w)

        def ts(eng, out, in0, s1, s2, op0, op1, r, w):
            return B.op(eng, lambda e: e.tensor_scalar(out=out, in0=in0, scalar1=s1, scalar2=s2,
                                                       op0=op0, op1=op1), r=r, w=w)

        def cp(eng, out, in_, r, w):
            if eng == "act":
                return act(out, in_, AF.Copy, r, w)
            return B.op(eng, lambda e: e.tensor_copy(out=out, in_=in_), r=r, w=w)

        def recip(out, in_, r, w):
            return B.op("dve", lambda e: e.reciprocal(out=out, in_=in_), r=r, w=w)

        def memset(eng, ap, val, w):
            return B.op(eng, lambda e: e.memset(ap, val), w=w)

        xT = A.alloc([128, 8, T], F32)
        ident = A.alloc([128, 128], F32)
        ones_bf = A.alloc([128, 128], BF16)
        epsc = A.alloc([128, 1], F32)
        mods = A.alloc([128, NL * 48 * 2], F32)
        modA = A.alloc([128, NL * 2 * 2 * 8], F32)
        gains = A.alloc([128, NL * 2 * 8 + 8], F32)
        gtab = A.alloc([128, NL * 448], F32)
        gsub = A.alloc([128, NL], F32)
        onesbd = A.alloc([128, 128], BF16)
        lam = A.alloc([128, NL * 4], F32)
        c64 = A.alloc([128, 128], BF16)
        s64n = A.alloc([128, 128], BF16)
        c256 = A.alloc([128, 2, 256], BF16)
        s256 = A.alloc([128, 2, 256], BF16)
        wuq = A.alloc([128, NL * 2, 384], BF16)
        wuk = A.alloc([128, NL, 256], BF16)
        wuv = A.alloc([128, NL, 256], BF16)
        cvb = A.alloc([128, 8, 2], BF16)
        adab = A.alloc([128, NL, 48], F32)
        mrow = A.alloc([128, 1024], F32)

        def mods_ap(l, m, k, v):
            i = (l * 48 + m * 8 + k) * 2 + v
            return mods[:, i:i + 1]

        def modA_ap(l, v, which, k):
            i = ((l * 2 + v) * 2 + which) * 8 + k
            return modA[:, i:i + 1]

        try:
            B.dma("sp", xT, D["xT"].rearrange("p (c t) -> p c t", t=T), w=["xT"])
            B.dma("sp", ident, D["ident"], w=["ident"])
            memset("dve", ones_bf, 1.0, w=["ones"])
            memset("dve", epsc, EPS, w=["eps"])
            memset("dve", onesbd, 0.0, w=["onesbd"])
            memset("dve", onesbd[0:64, 0:64], 1.0, w=["onesbd"])
            memset("dve", onesbd[64:128, 64:128], 1.0, w=["onesbd"])
            B.dma("sp", gains[:, 0:16].rearrange("p (l k) -> p l k", k=8), D["gmix"].rearrange("l p k -> p l k"), w=["gains"])
            B.dma("sp", gains[:, 16:32].rearrange("p (l k) -> p l k", k=8), D["gmlp"].rearrange("l p k -> p l k"), w=["gains"])
            B.dma("sp", gains[:, 32:40], D["gfin"], w=["gains"])
            B.dma("sp", gtab.rearrange("p (l c) -> p l c", c=448), D["gt"].rearrange("l p c -> p l c"), w=["gtab"])
            for l_ in range(NL):
                B.dma("sp", gsub[0:64, l_:l_ + 1], D["gsub"][l_], w=["gsub"])
                B.dma("sp", gsub[64:128, l_:l_ + 1], D["gsub"][l_], w=["gsub"])
            B.dma("pool", c64, D["c64"], w=["c64"])
            B.dma("pool", s64n, D["s64n"], w=["c64"])
            B.dma("pool", c256, D["c256"].rearrange("p (a b) -> p a b", b=256), w=["c256"])
            B.dma("pool", s256, D["s256"].rearrange("p (a b) -> p a b", b=256), w=["c256"])
            for l_ in range(NL):
                B.dma("pool", wuq[:, l_ * 2:l_ * 2 + 2, :], D["wuq"][l_].rearrange("p (a b) -> p a b", b=384), w=["wuq"])
            B.dma("pool", wuk, D["wuk"].rearrange("l p c -> p l c"), w=["wuk"])
            B.dma("pool", wuv, D["wuv"].rearrange("l p c -> p l c"), w=["wuv"])

            m0 = A.mark()
            cvf = A.alloc([128, 16], F32)
            lamp = A.alloc([128, NL, 128], F32)
            lprod = A.alloc([128, NL, 64], F32)
            lsum = A.alloc([128, NL * 2], F32)
            adw = [A.alloc([128, 8, 1024], BF16) for _ in range(2)]
            B.dma("sp", cvf, D["cv"], w=["cvf"])
            B.dma("sp", adab, D["adab"].rearrange("l p j -> p l j"), w=["adab"])
            B.dma("sp", lamp, D["lamp"].rearrange("l p c -> p l c"), w=["lamp"])
            act(cvb.rearrange("p a b -> p (a b)"), cvf, AF.Silu, r=["cvf"], w=["cvb"])

            def mods_load(l, m, buf, key):
                B.dma("pool", buf, D["adaw"][l].rearrange("p (k c) -> p k c", c=6144)[:, :, m * 1024:(m + 1) * 1024],
                      w=[key])

            def mods_piece(l, m, buf, key):
                for hf in range(2):
                    pb = ps_get()
                    for k in range(8):
                        mm(psum[pb][0:2, :], cvb[:, k, :], buf[:, k, hf * 512:(hf + 1) * 512], start=(k == 0), stop=(k == 7),
                           r=[key, "cvb"], w=[PK(pb)])
                    cp("dve", mrow[0:2, hf * 512:(hf + 1) * 512], psum[pb][0:2, :], r=[PK(pb)], w=[("mrow", hf)])
                pb = ps_get()
                for j in range(8):
                    transpose(psum[pb][:, 2 * j:2 * j + 2], mrow[0:2, j * 128:(j + 1) * 128], ident[0:2, 0:2],
                              r=[("mrow", j // 4), "ident"], w=[PK(pb)])
                base = (l * 48 + m * 8) * 2
                for v in range(2):
                    dst = mods[:, base:base + 16].rearrange("p (j v) -> p j v", v=2)[:, :, v]
                    src = psum[pb][:, 0:16].rearrange("p (j v) -> p j v", v=2)[:, :, v]
                    tt("dve", dst, src, adab[:, l, m * 8:(m + 1) * 8], ALU.add, r=[PK(pb), "adab"], w=["mods"])

            def mods_finish(l):
                for v in range(2):
                    for which, (msc, goff) in enumerate(((1, 0), (4, 16))):
                        for k in range(8):
                            ts("dve", modA_ap(l, v, which, k), mods_ap(l, msc, k, v), 1.0,
                               gains[:, goff + l * 8 + k:goff + l * 8 + k + 1], ALU.add, ALU.mult,
                               r=["mods", "gains"], w=["modA"])

            for m in range(6):
                mods_load(0, m, adw[m % 2], ("adw", m % 2))
                mods_piece(0, m, adw[m % 2], ("adw", m % 2))
            mods_finish(0)
            for l in range(NL):
                tt("dve", lprod[:, l, :].rearrange("p (a b) -> p a b", b=32),
                   lamp[:, l, :].rearrange("p (a t b) -> p a t b", t=2, b=32)[:, :, 0, :],
                   lamp[:, l, :].rearrange("p (a t b) -> p a t b", t=2, b=32)[:, :, 1, :], ALU.mult,
                   r=["lamp"], w=["lprod"])
                for j in range(2):
                    B.op("dve", lambda e, l=l, j=j: e.reduce_sum(out=lsum[:, l * 2 + j:l * 2 + j + 1],
                                                                 in_=lprod[:, l, j * 32:(j + 1) * 32],
                                                                 axis=mybir.AxisListType.X), r=["lprod"], w=["lsum"])
                act(lsum[:, l * 2:l * 2 + 2], lsum[:, l * 2:l * 2 + 2], AF.Exp, r=["lsum"], w=["lsum"])
                lam_init = 0.8 - 0.6 * math.exp(-0.3 * l)
                tt("dve", lam[:, l * 4:l * 4 + 1], lsum[:, l * 2 + 1:l * 2 + 2], lsum[:, l * 2:l * 2 + 1], ALU.subtract,
                   r=["lsum"], w=["lam"])
                B.op("dve", lambda e, l=l, li=lam_init: e.tensor_scalar_add(out=lam[:, l * 4:l * 4 + 1], in0=lam[:, l * 4:l * 4 + 1],
                                                                      scalar1=-li), r=["lam"], w=["lam"])
            B.barrier()
            A.release(m0)
            ck(1)

            def fm_norm(t0, ntok, hT_dst, scale_ap_fn, bias_ap_fn, tmp, sq, rstd, out_f32=None):
                tmps = tmp if isinstance(tmp, list) else [tmp]
                sqs = sq if isinstance(sq, list) else [sq]
                rstds = rstd if isinstance(rstd, list) else [rstd]
                for it, tt0 in enumerate(range(t0, t0 + ntok, 512)):
                    p = it % len(sqs)
                    tmp_, sq_, rstd_ = tmps[p], sqs[p], rstds[p]
                    sl = slice(tt0, tt0 + 512)
                    pb = ps_get()
                    for k in range(8):
                        if k % 2 == 0:
                            act(sq_[:, k, :], xT[:, k, sl], AF.Square, r=["xT"], w=[("sq", p, k)])
                        else:
                            tt("dve", sq_[:, k, :], xT[:, k, sl], xT[:, k, sl], ALU.mult, r=["xT"], w=[("sq", p, k)])
                    for k in range(8):
                        mm(psum[pb][:, :], ones_bf, sq_[:, k, :], start=(k == 0), stop=(k == 7),
                           r=[("sq", p, k), "ones"], w=[PK(pb)])
                    act(rstd_, psum[pb][:, :], AF.Ln, r=[PK(pb), "eps"], w=[("rstd", p)], bias=epsc[:, 0:1], scale=1.0 / 1024)
                    act(rstd_, rstd_, AF.Exp, r=[("rstd", p)], w=[("rstd", p)], scale=-0.5)
                    for k in range(8):
                        tt("dve", tmp_[:, k % 2, :], xT[:, k, sl], rstd_, ALU.mult, r=["xT", ("rstd", p)], w=[("ntmp", p, k % 2)])
                        b = bias_ap_fn(k)
                        act(hT_dst(k, tt0 - t0), tmp_[:, k % 2, :], AF.Identity, r=[("ntmp", p, k % 2), "mods", "modA", "gains"],
                            w=[("hT", k)], scale=scale_ap_fn(k), **({"bias": b} if b is not None else {}))

            def attention2(pairs, scale, ptb, nq):
                ps_state["lo"] = 4
                merge = (nq == 512)
                pre_done = set()

                def run_pre(ix):
                    if ix < len(pairs) and ix not in pre_done:
                        pre_done.add(ix)
                        for job in pairs[ix]:
                            if "pre" in job:
                                job["pre"]()

                for pidx, pair in enumerate(pairs):
                    nk = pair[0]["nk"]
                    run_pre(pidx)
                    obs = [ps_get(hold=True), ps_get(hold=True)]
                    rr0 = pair[0]["r"] + pair[1]["r"]

                    def rkeys(kc):
                        out = list(rr0)
                        for job in pair:
                            if "rk" in job:
                                out += job["rk"](kc)
                        return out

                    def qk(kc):
                        sbase = (kc % 2) * 2
                        for i, job in enumerate(pair):
                            spec = job["qk"](kc)
                            for j, (kT, qT) in enumerate(spec):
                                mm(psum[sbase + i][:, 0:nq], kT, qT, start=(j == 0), stop=(j == len(spec) - 1),
                                   r=rkeys(kc), w=[PK(sbase + i)])

                    def pv_exp(kc):
                        sbase = (kc % 2) * 2
                        pi_ = kc % len(ptb)
                        pt = ptb[pi_]
                        if merge:
                            act(pt[:, 0:1024], psall[:, sbase * 512:(sbase + 2) * 512], AF.Exp,
                                r=[PK(sbase), PK(sbase + 1)], w=[("pt", pi_)], scale=scale)
                        else:
                            for i in range(2):
                                act(pt[:, i * 512:i * 512 + nq], psum[sbase + i][:, 0:nq], AF.Exp,
                                    r=[PK(sbase + i)], w=[("pt", pi_)], scale=scale)

                    def pv_mm(kc):
                        pi_ = kc % len(ptb)
                        pt = ptb[pi_]
                        for i, job in enumerate(pair):
                            mm(psum[obs[i]][:, 0:nq], job["v"](kc), pt[:, i * 512:i * 512 + nq], start=(kc == 0),
                               stop=(kc == nk - 1), r=rkeys(kc) + [("pt", pi_)], w=[PK(obs[i])])

                    qk(0)
                    if nk > 1:
                        qk(1)
                    run_pre(pidx + 1)
                    for kc in range(nk):
                        pv_exp(kc)
                        if kc + 2 < nk:
                            qk(kc + 2)
                        pv_mm(kc)
                    for i, job in enumerate(pair):
                        job["epi"](obs[i])
                        ps_rel(obs[i])
                ps_state["lo"] = 0

            for l in range(NL):
                lam_init = 0.8 - 0.6 * math.exp(-0.3 * l)
                neglam = lam[0:64, l * 4:l * 4 + 1]
                mL = A.mark()
                QdS = A.alloc([128, 2, 2, TS], BF16)
                QnS = A.alloc([128, 2, TS], BF16)
                QrS = A.alloc([128, 2, TS], BF16)
                QgS = A.alloc([128, 2, 2, TS], BF16)

                def drive(gens):
                    live = list(gens)
                    first = True
                    while live:
                        for g in list(live):
                            try:
                                next(g)
                                if first:
                                    next(g)
                                    next(g)
                                    first = False
                            except StopIteration:
                                live.remove(g)

                def inproj(grp):
                    t0 = TP if grp == "S" else 0
                    v = 1 if grp == "S" else 0
                    G = gtab[:, l * 448:(l + 1) * 448]
                    winv = D["win"][l].rearrange("p (k c) -> p k c", c=1888)
                    mI = A.mark()
                    hT = A.alloc([128, 8, 512], BF16)
                    win0 = A.alloc([128, 8, 512], BF16)
                    win1 = A.alloc([128, 8, 512], BF16)

                    def load_winA():
                        B.dma("pool", win0, winv[:, :, 0:512], w=["win0"])
                        B.dma("pool", win1, winv[:, :, 512:1024], w=["win1"])

                    def load_winB():
                        B.dma("pool", win0[:, :, 0:352], winv[:, :, 1024:1376], w=["win0"])
                        B.dma("pool", win1, winv[:, :, 1376:1888], w=["win1"])
                    load_winA()
                    for half in range(2):
                        m1 = A.mark()
                        sq = A.alloc([128, 8, 512], BF16)
                        ntmp = A.alloc([128, 2, 512], F32)
                        rstd = A.alloc([128, 512], F32)
                        fm_norm(t0 + half * 512, 512, lambda k, o: hT[:, k, :], lambda k: modA_ap(l, v, 0, k),
                                lambda k: mods_ap(l, 0, k, v), ntmp, sq, rstd)
                        B.barrier()
                        A.release(m1)
                        ck(20)

                        def tr(dst, src, ncols, rk, wk, eng="dve"):
                            pb = ps_get()
                            transpose(psum[pb][0:ncols, 0:128], src, ident, r=rk + ["ident"], w=[PK(pb)])
                            cp(eng, dst, psum[pb][0:ncols, 0:128], r=[PK(pb)], w=wk)

                        def rope(dst, src, nh, d, ci, si, rk, wk, rp, rtab, p):
                            q = d // 4
                            n = nh * d
                            sv = src.rearrange("p (a t q) -> p a t q", t=2, q=q)
                            dv = dst.rearrange("p (a t q) -> p a t q", t=2, q=q)
                            cs = rtab[:, ci, 0:n // 2].rearrange("p (a q) -> p a q", q=q)
                            sn = rtab[:, si, 0:n // 2].rearrange("p (a q) -> p a q", q=q)
                            t = [rp[:, i, 0:n // 2].rearrange("p (a q) -> p a q", q=q) for i in range(4)]
                            RK = rk + [("rope", p)]
                            tt("pool", t[0], sv[:, :, 0, :], cs, ALU.mult, r=RK, w=[("rp", p, 0)])
                            tt("pool", t[1], sv[:, :, 1, :], sn, ALU.mult, r=RK, w=[("rp", p, 1)])
                            tt("dve", t[2], sv[:, :, 0, :], sn, ALU.mult, r=RK, w=[("rp", p, 2)])
                            tt("dve", t[3], sv[:, :, 1, :], cs, ALU.mult, r=RK, w=[("rp", p, 3)])
                            tt("pool", dv[:, :, 0, :], t[0], t[1], ALU.subtract, r=[("rp", p, 0), ("rp", p, 1)], w=wk)
                            tt("dve", dv[:, :, 1, :], t[2], t[3], ALU.add, r=[("rp", p, 2), ("rp", p, 3)], w=wk)

                        def load_rope(ti, rtab, p):
                            for j, nm in enumerate(("rc32", "rs32", "rc64", "rs64")):
                                B.dma("sp", rtab[:, j, :], D[nm][:, ti * 128:(ti + 1) * 128], w=[("rope", p)])

                        mA = A.mark()
                        ztA = [A.alloc([128, 1024], F32) for _ in range(2)]
                        zrA = [A.alloc([128, 512], F32) for _ in range(2)] if grp == "S" else [None, None]
                        rpA = [A.alloc([128, 4, 128], F32) for _ in range(2)] if grp == "S" else [None, None]
                        rtA = [A.alloc([128, 4, 128], F32) for _ in range(2)] if grp == "S" else [None, None]
                        def projA(tl):
                            ti = half * 4 + tl
                            p = tl % 2
                            zt, zr, rp, rtab = ztA[p], zrA[p], rpA[p], rtA[p]
                            tsl = slice(ti * 128, (ti + 1) * 128)
                            for (c0, c1, eng, wb, wk) in ((0, 512, "act", win0, "win0"), (512, 1024, "dve", win1, "win1")):
                                pb = ps_get()
                                for k in range(8):
                                    mm(psum[pb][:, 0:c1 - c0], hT[:, k, tl * 128:(tl + 1) * 128], wb[:, k, 0:c1 - c0],
                                       start=(k == 0), stop=(k == 7), r=[("hT", k), wk], w=[PK(pb)])
                                cp(eng, zt[:, c0:c1], psum[pb][:, 0:c1 - c0], r=[PK(pb)], w=[("zt", p, c0)])

                        def postA(tl):
                            ti = half * 4 + tl
                            p = tl % 2
                            zt, zr, rp, rtab = ztA[p], zrA[p], rpA[p], rtA[p]
                            tsl = slice(ti * 128, (ti + 1) * 128)
                            ZA, ZB = [("zt", p, 0)], [("zt", p, 512)]
                            if grp == "S":
                                load_rope(ti, rtab, p)
                                rope(zr[:, 0:256], zt[:, 0:256], 8, 32, 0, 1, ZA, [("zr", p, 0)], rp, rtab, p)
                                yield
                                rope(zr[:, 256:512], zt[:, 256:512], 8, 32, 0, 1, ZA, [("zr", p, 1)], rp, rtab, p)
                                yield
                                for g in range(2):
                                    tr(QdS[:, 0, g, tsl], zr[:, g * 128:(g + 1) * 128], 128, [("zr", p, 0)], ["QdS"])
                                    tr(QdS[:, 1, g, tsl], zt[:, g * 128:(g + 1) * 128], 128, ZA, ["QdS"], eng="act")
                                    tr(XS[:, X_KD + g * 1024 + ti * 128:X_KD + g * 1024 + (ti + 1) * 128],
                                       zr[:, 256 + g * 128:256 + (g + 1) * 128], 128, [("zr", p, 1)], ["XS"])
                                yield
                                cp("pool", XS[:, X_VD + ti * 256:X_VD + (ti + 1) * 256], zt[:, 512:768], r=ZB, w=["XS"])
                                cp("pool", XS[:, X_U + ti * 256:X_U + (ti + 1) * 256], zt[:, 768:1024], r=ZB, w=["XS"])
                            else:
                                sq_i, pos0 = ti // 2, (ti % 2) * 128
                                B.dma("sp", D["o_dk"][sq_i, l, pos0:pos0 + 128, :], zt[:, 256:512], r=ZA)
                                B.dma("sp", D["o_dv"][sq_i, l, pos0:pos0 + 128, :], zt[:, 512:768], r=ZB)
                                for g in range(2):
                                    tr(QdP[:, g, tsl], zt[:, g * 128:(g + 1) * 128], 128, ZA, ["QdP"], eng="act")
                                    tr(KdP[:, g, tsl], zt[:, 256 + g * 128:256 + (g + 1) * 128], 128, ZA, ["KdP"])
                                yield
                                vsrc = zt[:, 512:768].rearrange("p (a e c) -> p a e c", e=2, c=64)
                                vdst = VdP[:, ti, :].rearrange("p (a b) -> p a b", b=192)
                                cp("pool", vdst[:, :, 0:64], vsrc[:, :, 0, :], r=ZB, w=["VdP"])
                                cp("pool", vdst[:, :, 128:192], vsrc[:, :, 1, :], r=ZB, w=["VdP"])
                                cp("pool", UP[:, ti, :], zt[:, 768:1024], r=ZB, w=["UP"])
                            yield

                        def laneA(tiles):
                            for tl in tiles:
                                projA(tl)
                                yield
                                yield from postA(tl)
                        drive([laneA([0, 2]), laneA([1, 3])])
                        load_winB()
                        B.barrier()
                        A.release(mA)
                        ck(23)

                        ztB = [A.alloc([128, 864], F32) for _ in range(2)]
                        znB = [A.alloc([128, 768], F32) for _ in range(2)]
                        ssB = [A.alloc([128, 8], F32) for _ in range(2)]
                        jk1 = A.alloc([128, 256], F32)
                        jkB = [jk1, jk1]
                        rpB = [A.alloc([128, 4, 128], F32) for _ in range(2)] if grp == "S" else [None, None]
                        zrB = [A.alloc([128, 384], F32) for _ in range(2)] if grp == "S" else [None, None]
                        qmB = [A.alloc([128, 640], F32) for _ in range(2)]
                        cqB = [A.alloc([128, 2, 128], BF16) for _ in range(2)]
                        rtB = [A.alloc([128, 4, 128], F32) for _ in range(2)] if grp == "S" else [None, None]
                        def projB(tl):
                            ti = half * 4 + tl
                            p = tl % 2
                            zt, zn, ss, junk, rp, zr, qm, cqT, rtab = ztB[p], znB[p], ssB[p], jkB[p], rpB[p], zrB[p], qmB[p], cqB[p], rtB[p]
                            qnope, qrope, qrope_r, kr4 = qm[:, 0:256], qm[:, 256:384], qm[:, 384:512], qm[:, 512:640]
                            tsl = slice(ti * 128, (ti + 1) * 128)
                            for (c0, c1, eng, wb, wk) in ((0, 352, "act", win0, "win0"), (352, 864, "dve", win1, "win1")):
                                pb = ps_get()
                                for k in range(8):
                                    mm(psum[pb][:, 0:c1 - c0], hT[:, k, tl * 128:(tl + 1) * 128], wb[:, k, 0:c1 - c0],
                                       start=(k == 0), stop=(k == 7), r=[("hT", k), wk], w=[PK(pb)])
                                cp(eng, zt[:, c0:c1], psum[pb][:, 0:c1 - c0], r=[PK(pb)], w=[("ztb", p, c0)])

                        def postB(tl):
                            ti = half * 4 + tl
                            p = tl % 2
                            zt, zn, ss, junk, rp, zr, qm, cqT, rtab = ztB[p], znB[p], ssB[p], jkB[p], rpB[p], zrB[p], qmB[p], cqB[p], rtB[p]
                            qnope, qrope, qrope_r, kr4 = qm[:, 0:256], qm[:, 256:384], qm[:, 384:512], qm[:, 512:640]
                            tsl = slice(ti * 128, (ti + 1) * 128)
                            ZR = [("ztb", p, 0), ("ztb", p, 352)]
                            slices = [(0, 192), (192, 128)] + [(352 + 64 * h, 64) for h in range(4)] + \
                                     [(608 + 64 * h, 64) for h in range(2)]
                            SS = [("ss", p, j) for j in range(8)]
                            memset("dve", ss, 0.0, w=SS)
                            for j, (c0, n) in enumerate(slices):
                                act(junk[:, 0:n], zt[:, c0:c0 + n], AF.Square, r=ZR, w=[("junk", p), ("ss", p, j)],
                                    scale=float(n) ** -0.5, accum_out=ss[:, j:j + 1])
                            yield
                            act(ss, ss, AF.Ln, r=SS + ["eps"], w=[("ssr", p)], bias=epsc[:, 0:1], scale=1.0)
                            act(ss, ss, AF.Exp, r=[("ssr", p)], w=[("ssr", p)], scale=-0.5)
                            zdst = [0, 192, 320, 320 + 128, 320 + 64, 320 + 192, 576, 640]
                            gsrc = [0, 192, 320, 320, 320, 320, 384, 384]
                            for j, (c0, n) in enumerate(slices):
                                stt("dve", zn[:, zdst[j]:zdst[j] + n], zt[:, c0:c0 + n], ss[:, j:j + 1], G[:, gsrc[j]:gsrc[j] + n],
                                    ALU.mult, ALU.mult, r=ZR + [("ssr", p), "gtab"], w=[("zn", p, j)])
                            ZN = [("zn", p, j) for j in range(8)]
                            ck(24)
                            yield
                            tr(cqT[:, 0, :], zn[:, 0:128], 128, ZN, [("cqT", p, 0)])
                            tr(cqT[0:64, 1, :], zn[:, 128:192], 64, ZN, [("cqT", p, 1)])
                            pb = ps_get()
                            mm(psum[pb][:, 0:384], cqT[:, 0, :], wuq[:, l * 2, :], start=True, stop=False,
                               r=[("cqT", p, 0), "wuq"], w=[PK(pb)])
                            mm(psum[pb][:, 0:384], cqT[0:64, 1, :], wuq[0:64, l * 2 + 1, :], start=False, stop=True,
                               r=[("cqT", p, 1), "wuq"], w=[PK(pb)])
                            qraw = psum[pb][:, 0:384].rearrange("p (h c) -> p h c", c=96)
                            cp("dve", qnope.rearrange("p (h c) -> p h c", c=64), qraw[:, :, 0:64], r=[PK(pb)], w=[("qnope", p)])
                            cp("dve", qrope.rearrange("p (h c) -> p h c", c=32), qraw[:, :, 64:96], r=[PK(pb)], w=[("qrope", p)])
                            ck(25)
                            yield
                            KR = zt[:, 320:352]
                            DV = zt[:, 736:864]
                            if grp == "S":
                                load_rope(ti, rtab, p)
                                rope(zr[:, 0:256], zn[:, 320:576], 4, 64, 2, 3, ZN, [("zr", p, 2)], rp, rtab, p)
                                yield
                                rope(zr[:, 256:384], zn[:, 576:704], 2, 64, 2, 3, ZN, [("zr", p, 3)], rp, rtab, p)
                                yield
                                rope(qrope_r, qrope, 4, 32, 0, 1, [("qrope", p)], [("qrope_r", p)], rp, rtab, p)
                                yield
                                rope(kr4[:, 0:32], KR, 1, 32, 0, 1, ZR, [("kr4a", p)], rp, rtab, p)
                                for h in range(1, 4):
                                    cp("pool", kr4[:, 32 * h:32 * h + 32], kr4[:, 0:32], r=[("kr4a", p)], w=[("kr4", p)])
                                ck(26)
                                yield
                                for g in range(2):
                                    tr(QnS[:, g, tsl], qnope[:, g * 128:(g + 1) * 128], 128, [("qnope", p)], ["QnS"])
                                    tr(QgS[:, 0, g, tsl], zr[:, g * 128:(g + 1) * 128], 128, [("zr", p, 2)], ["QgS"])
                                    tr(QgS[:, 1, g, tsl], zn[:, 320 + g * 128:320 + (g + 1) * 128], 128, ZN, ["QgS"], eng="act")
                                yield
                                tr(QrS[:, 0, tsl], qrope_r, 128, [("qrope_r", p)], ["QrS"])
                                tr(QrS[:, 1, tsl], qrope, 128, [("qrope", p)], ["QrS"], eng="act")
                                tr(XS[:, X_KG + ti * 128:X_KG + (ti + 1) * 128], zr[:, 256:384], 128, [("zr", p, 3)], ["XS"])
                                tr(XS[:, X_CKV + ti * 128:X_CKV + (ti + 1) * 128], zn[:, 192:320], 128, ZN, ["XS"], eng="act")
                                tr(XS[:, X_KR + ti * 128:X_KR + (ti + 1) * 128], kr4, 128, [("kr4", p), ("kr4a", p)], ["XS"])
                                cp("pool", XS[:, X_VG + ti * 128:X_VG + (ti + 1) * 128], DV, r=ZR, w=["XS"])
                            else:
                                sq_i, pos0 = ti // 2, (ti % 2) * 128
                                B.dma("sp", D["o_kr"][sq_i, l, pos0:pos0 + 128, :], KR, r=ZR)
                                B.dma("sp", D["o_gv"][sq_i, l, pos0:pos0 + 128, :], DV, r=ZR)
                                B.dma("sp", D["o_ckv"][sq_i, l, pos0:pos0 + 128, :], zn[:, 192:320], r=ZN)
                                B.dma("sp", D["o_gk"][sq_i, l, pos0:pos0 + 128, :], zn[:, 576:704], r=ZN)
                                for h in range(4):
                                    cp("pool", kr4[:, 32 * h:32 * h + 32], KR, r=ZR, w=[("kr4", p)])
                                for g in range(2):
                                    tr(QnP[:, g, tsl], qnope[:, g * 128:(g + 1) * 128], 128, [("qnope", p)], ["QnP"])
                                    tr(QgP[:, g, tsl], zn[:, 320 + g * 128:320 + (g + 1) * 128], 128, ZN, ["QgP"], eng="act")
                                yield
                                tr(QrP[:, tsl], qrope, 128, [("qrope", p)], ["QrP"])
                                tr(KgP[:, tsl], zn[:, 576:704], 128, ZN, ["KgP"])
                                tr(KrP[:, tsl], kr4, 128, [("kr4", p)], ["KrP"], eng="act")
                                tr(CkP[:, tsl], zn[:, 192:320], 128, ZN, [("CkP", ti)])
                                vsrc = DV.rearrange("p (a c) -> p a c", c=64)
                                vdst = VgP[:, ti, :].rearrange("p (a b) -> p a b", b=192)
                                cp("pool", vdst[:, :, 0:64], vsrc, r=ZR, w=["VgP"])
                                cp("pool", vdst[:, :, 128:192], vsrc, r=ZR, w=["VgP"])
                                yield
                                pb2 = ps_get()
                                mm(psum[pb2][:, 0:256], CkP[:, tsl], wuv[:, l, :], start=True, stop=True,
                                   r=[("CkP", ti), "wuv"], w=[PK(pb2)])
                                vsrc = psum[pb2][:, 0:256].rearrange("p (a e c) -> p a e c", e=2, c=64)
                                vdst = VmP[:, ti, :].rearrange("p (a b) -> p a b", b=192)
                                cp("dve", vdst[:, :, 0:64], vsrc[:, :, 0, :], r=[PK(pb2)], w=["VmP"])
                                cp("dve", vdst[:, :, 128:192], vsrc[:, :, 1, :], r=[PK(pb2)], w=["VmP"])
                                for g in range(2):
                                    pb3 = ps_get()
                                    mm(psum[pb3][:, 0:128], wuk[:, l, g * 128:(g + 1) * 128], CkP[:, tsl], start=True, stop=True,
                                       r=[("CkP", ti), "wuk"], w=[PK(pb3)])
                                    cp("act", KnP[:, g, tsl], psum[pb3][:, 0:128], r=[PK(pb3)], w=["KnP"])
                            yield

                        def laneB(tiles):
                            for tl in tiles:
                                projB(tl)
                                yield
                                yield from postB(tl)
                        drive([laneB([0, 2]), laneB([1, 3])])
                        if half == 0:
                            load_winA()
                        B.barrier()
                        A.release(m1)
                    A.release(mI)

                def epi_plain(dst_fn, rcp):
                    def epi(ob, nq=None):
                        pass
                    return epi

                def make_epi(dst_fn, eo, nq, rcp, wkey="yT"):
                    def epi(ob):
                        o0, d0 = (0, 64) if eo == 0 else (64, 0)
                        if nq == 256:
                            act(rcp[d0:d0 + 64, 0:nq], psum[ob][d0:d0 + 64, 0:nq], AF.Ln, r=[PK(ob)], w=["rcp"])
                            act(rcp[d0:d0 + 64, 0:nq], rcp[d0:d0 + 64, 0:nq], AF.Exp, r=["rcp"], w=["rcp"], scale=-1.0)
                        else:
                            recip(rcp[d0:d0 + 64, 0:nq], psum[ob][d0:d0 + 64, 0:nq], r=[PK(ob)], w=["rcp"])
                        tt("dve", dst_fn(o0, o0 + 64), psum[ob][o0:o0 + 64, 0:nq], rcp[d0:d0 + 64, 0:nq], ALU.mult,
                           r=[PK(ob), "rcp"], w=[wkey])
                    return epi

                def diff_epilogue(Omaps, q0, nq, dtmp):
                    y, ysq, rs = dtmp
                    for pp in range(2):
                        stt("dve", y[:, 0:nq], Omaps[:, 2 * pp + 1, 0:nq], lam[:, l * 4:l * 4 + 1], Omaps[:, 2 * pp, 0:nq],
                            ALU.mult, ALU.add, r=["Om", "lam"], w=["dy"])
                        act(ysq[:, 0:nq], y[:, 0:nq], AF.Square, r=["dy"], w=["dysq"])
                        pb = ps_get()
                        mm(psum[pb][:, 0:nq], onesbd, ysq[:, 0:nq], start=True, stop=True,
                           r=["dysq", "onesbd"], w=[PK(pb)])
                        act(rs[:, 0:nq], psum[pb][:, 0:nq], AF.Ln, r=[PK(pb), "eps"], w=["drs"],
                            bias=epsc[:, 0:1], scale=1.0 / 64)
                        act(rs[:, 0:nq], rs[:, 0:nq], AF.Exp, r=["drs"], w=["drs"], scale=-0.5)
                        stt("dve", y[:, 0:nq], y[:, 0:nq], gsub[:, l:l + 1], rs[:, 0:nq], ALU.mult, ALU.mult,
                            r=["dy", "drs", "gsub"], w=["dy"])
                        act(yT[:, pp, q0:q0 + nq], y[:, 0:nq], AF.Identity, r=["dy"], w=["yT"], scale=1.0 - lam_init)

                def fnet_stage2(PQ, q0, nq, scl):
                    for hc in range(2):
                        pb = ps_get()
                        mm(psum[pb][:, 0:nq], c64, PQ[:, 0, hc, 0:nq], start=True, stop=False, r=["PQ", "c64"], w=[PK(pb)])
                        mm(psum[pb][:, 0:nq], s64n, PQ[:, 1, hc, 0:nq], start=False, stop=True, r=["PQ", "c64"], w=[PK(pb)])
                        act(yT[:, 2 + hc, q0:q0 + nq], psum[pb][:, 0:nq], AF.Identity, r=[PK(pb)], w=["yT"], scale=scl)

                def outproj(t0):
                    v = 1 if t0 >= TP else 0
                    m1 = A.mark()
                    wo = A.alloc([128, 8, 1024], BF16)
                    B.dma("pool", wo, D["wout"][l].rearrange("p (h o) -> p h o", o=1024), w=["wo"])
                    for tq in range(2):
                        for o in range(8):
                            pb = ps_get()
                            osl = slice(o * 128, (o + 1) * 128)
                            qsl = slice(tq * 512, (tq + 1) * 512)
                            for c in range(8):
                                mm(psum[pb][:, :], wo[:, c, osl], yT[:, c, qsl], start=(c == 0), stop=(c == 7),
                                   r=["wo", "yT"], w=[PK(pb)])
                            xs = xT[:, o, t0 + tq * 512:t0 + (tq + 1) * 512]
                            stt("dve", xs, psum[pb][:, :], mods_ap(l, 2, o, v), xs, ALU.mult, ALU.add,
                                r=[PK(pb), "mods", "xT"], w=["xT"])
                    B.barrier()
                    A.release(m1)

                if not DEBUG.get("skipS"):
                    mX = A.mark()
                    XS = A.alloc([128, XW], BF16)
                    inproj("S")
                    ck(2)
                    for j in range(3):
                        B.dma("sp", xin[l][j].ap(), XS[:, j * 4096:j * 4096 + SEGW[j]], r=["XS"], w=[("xin", j)])
                    B.barrier()
                    A.release(mX)
                    for j in range(3):
                        if not DEBUG.get("nocc"):
                            B.op("pool", lambda e, l=l, j=j: e.collective_compute(
                                "AllGather", ALU.bypass, replica_groups=[[0, 1, 2, 3], [4, 5, 6, 7]],
                                ins=[xin[l][j].ap().opt()], outs=[xout[l][j].ap().opt()]),
                                r=[("xin", j)], w=[("xout", j)], cc=True)
                XOUT = [("xout", 0), ("xout", 1), ("xout", 2)]
                xov = [xout[l][j].ap().rearrange("(r p) w -> p r w", p=128) for j in range(3)]

                def xo_piece(r_, off, n):
                    j = off // 4096
                    return xov[j][:, r_, off - j * 4096:off - j * 4096 + n]

                mP = A.mark()
                QdP = A.alloc([128, 2, TP], BF16)
                QnP = A.alloc([128, 2, TP], BF16)
                QrP = A.alloc([128, TP], BF16)
                QgP = A.alloc([128, 2, TP], BF16)
                KdP = A.alloc([128, 2, TP], BF16)
                KnP = A.alloc([128, 2, TP], BF16)
                KrP = A.alloc([128, TP], BF16)
                KgP = A.alloc([128, TP], BF16)
                CkP = A.alloc([128, TP], BF16)
                VdP = A.alloc([128, 8, 384], BF16)
                VmP = A.alloc([128, 8, 384], BF16)
                VgP = A.alloc([128, 8, 384], BF16)
                UP = A.alloc([128, 8, 256], BF16)
                for Vx, kx in ((VdP, "VdP"), (VmP, "VmP"), (VgP, "VgP")):
                    memset("pool", Vx.rearrange("p c (a b) -> p c a b", b=192)[:, :, :, 64:128], 1.0, w=[kx])
                inproj("P")
                ck(3)
                yT = A.alloc([128, 8, 1024], BF16)
                mPa = A.mark()
                ptb = [A.alloc([128, 1024], BF16) for _ in range(3)]
                rcp = A.alloc([128, 512], F32)
                Om = A.alloc([128, 4, 256], F32)
                dtmp = (A.alloc([128, 512], F32), A.alloc([128, 512], BF16), A.alloc([128, 512], F32))
                PQ = A.alloc([128, 2, 2, 512], BF16)

                for s in range(4):
                    q0 = s * 256
                    qs = slice(q0, q0 + 256)
                    jobs = []
                    for m in range(8):
                        g, pr = m // 4, (m % 4) * 32
                        jobs.append(dict(
                            nk=2, r=["QdP", "KdP", "VdP"],
                            qk=lambda kc, g=g, pr=pr, q0=q0: [(KdP[pr:pr + 32, g, q0 + kc * 128:q0 + (kc + 1) * 128],
                                                               QdP[pr:pr + 32, g, q0:q0 + 256])],
                            v=lambda kc, m=m, s=s: VdP[:, s * 2 + kc, (m // 4) * 192 + ((m // 2) % 2) * 64:(m // 4) * 192 + ((m // 2) % 2) * 64 + 128],
                            epi=make_epi(lambda lo, hi, m=m: Om[lo:hi, (m // 4) * 2 + m % 2, :], (m // 2) % 2, 256, rcp, "Om")))
                    attention2([(jobs[2 * i], jobs[2 * i + 1]) for i in range(4)], 32 ** -0.5, ptb, 256)
                    diff_epilogue(Om, q0, 256, dtmp)
                    jobs = []
                    for h in range(4):
                        g, pr = h // 2, (h % 2) * 64
                        jobs.append(dict(
                            nk=2, r=["QnP", "KnP", "QrP", "KrP", "VmP"],
                            qk=lambda kc, g=g, pr=pr, h=h, q0=q0: [
                                (KnP[pr:pr + 64, g, q0 + kc * 128:q0 + (kc + 1) * 128], QnP[pr:pr + 64, g, q0:q0 + 256]),
                                (KrP[32 * h:32 * h + 32, q0 + kc * 128:q0 + (kc + 1) * 128], QrP[32 * h:32 * h + 32, q0:q0 + 256])],
                            v=lambda kc, h=h, s=s: VmP[:, s * 2 + kc, (h // 2) * 192 + (h % 2) * 64:(h // 2) * 192 + (h % 2) * 64 + 128],
                            epi=make_epi(lambda lo, hi, h=h, qs=qs: yT[lo:hi, 4 + h // 2, qs], h % 2, 256, rcp)))
                    attention2([(jobs[0], jobs[1]), (jobs[2], jobs[3])], 96 ** -0.5, ptb, 256)
                    jobs = []
                    for h in range(4):
                        kv, ab = h // 2, h % 2
                        jobs.append(dict(
                            nk=2, r=["QgP", "KgP", "VgP"],
                            qk=lambda kc, kv=kv, ab=ab, q0=q0: [(KgP[64 * kv:64 * kv + 64, q0 + kc * 128:q0 + (kc + 1) * 128],
                                                                 QgP[64 * kv:64 * kv + 64, ab, q0:q0 + 256])],
                            v=lambda kc, kv=kv, ab=ab, s=s: VgP[:, s * 2 + kc, kv * 192 + ab * 64:kv * 192 + ab * 64 + 128],
                            epi=make_epi(lambda lo, hi, h=h, qs=qs: yT[lo:hi, 6 + h // 2, qs], h % 2, 256, rcp)))
                    attention2([(jobs[0], jobs[2]), (jobs[1], jobs[3])], 64 ** -0.5, ptb, 256)
                    for tab, tabt in enumerate((c256, s256)):
                        for hc in range(2):
                            pb = ps_get()
                            for sc in range(2):
                                mm(psum[pb][:, 0:256], UP[:, s * 2 + sc, hc * 128:(hc + 1) * 128], tabt[:, sc, :],
                                   start=(sc == 0), stop=(sc == 1), r=["UP", "c256"], w=[PK(pb)])
                            cp("dve", PQ[:, tab, hc, 0:256], psum[pb][:, 0:256], r=[PK(pb)], w=["PQ"])
                    fnet_stage2(PQ, q0, 256, (256 * 64) ** -0.5)
                B.barrier()
                A.release(mPa)
                ck(4)
                outproj(0)
                ck(5)
                A.release(mP)

                if not DEBUG.get("skipS"):
                    mS = A.mark()
                    yT = A.alloc([128, 8, 1024], BF16)
                    mSa = A.mark()
                    ptb = [A.alloc([128, 1024], BF16) for _ in range(3)]
                    rcp = A.alloc([128, 512], F32)
                    m1 = A.mark()
                    Ug = A.alloc([128, 32, 256], BF16)
                    PQ = A.alloc([128, 2, 2, 512], BF16)
                    tabb = [A.alloc([128, 2, 8, 512], BF16) for _ in range(2)]
                    for r_ in range(4):
                        B.dma("sp", Ug[:, r_ * 8:(r_ + 1) * 8, :],
                              xo_piece(r_, X_U, 2048).rearrange("p (c n) -> p c n", n=256), r=XOUT, w=["Ug"])
                    cbv = D["cbig"].rearrange("p (t s k) -> p t s k", t=2, k=512)
                    sbv = D["sbig"].rearrange("p (t s k) -> p t s k", t=2, k=512)
                    ld = 0
                    for kt in range(2):
                        banks = [ps_get(hold=True) for _ in range(4)]
                        for s8 in range(4):
                            tb = tabb[ld % 2]
                            key = ("tabb", ld % 2)
                            ld += 1
                            B.dma("sp", tb[:, 0, :, :], cbv[:, kt, s8 * 8:(s8 + 1) * 8, :], w=[key])
                            B.dma("sp", tb[:, 1, :, :], sbv[:, kt, s8 * 8:(s8 + 1) * 8, :], w=[key])
                            for si in range(8):
                                sc = s8 * 8 + si
                                for tab in range(2):
                                    for hc in range(2):
                                        b = banks[tab * 2 + hc]
                                        mm(psum[b][:, :], Ug[:, sc, hc * 128:(hc + 1) * 128], tb[:, tab, si, :],
                                           start=(sc == 0), stop=(sc == 31), r=["Ug", key], w=[PK(b)])
                        for tab in range(2):
                            for hc in range(2):
                                b = banks[tab * 2 + hc]
                                cp("dve" if hc else "act", PQ[:, tab, hc, :], psum[b][:, :], r=[PK(b)], w=["PQ"])
                                ps_rel(b)
                        fnet_stage2(PQ, kt * 512, 512, (4096 * 64) ** -0.5)
                    B.barrier()
                    A.release(m1)

                    ck(6)
                    m1 = A.mark()
                    Kd = A.alloc([128, 2, 4608], BF16)
                    Vd = A.alloc([128, 36, 384], BF16)
                    Om = A.alloc([128, 4, 512], F32)
                    dtmp = (A.alloc([128, 512], F32), A.alloc([128, 512], BF16), A.alloc([128, 512], F32))
                    memset("pool", Vd.rearrange("p c (a b) -> p c a b", b=192)[:, :, :, 64:128], 1.0, w=["Vdones"])

                    def vload(q, Vx, c0, nchunk, src, key, rkeys):
                        sv = src.rearrange("p (c a e n) -> p c a e n", a=2, e=2, n=64)
                        dv = Vx[:, c0:c0 + nchunk, :].rearrange("p c (a b) -> p c a b", b=192)
                        for a in range(2):
                            B.dma(q, dv[:, :, a, 0:64], sv[:, :, a, 0, :], r=rkeys, w=[key])
                            B.dma(q, dv[:, :, a, 128:192], sv[:, :, a, 1, :], r=rkeys, w=[key])
                    for r_ in range(4):
                        B.dma("sp", Kd[:, :, r_ * 1024:(r_ + 1) * 1024],
                              xo_piece(r_, X_KD, 2048).rearrange("p (g n) -> p g n", n=1024), r=XOUT, w=[("Kd", r_)])
                        vload("sp", Vd, r_ * 8, 8, xo_piece(r_, X_VD, 2048), ("Vd", r_), XOUT)
                    B.dma("pool", Kd[:, :, 4096:4608], D["ckdT"][l].rearrange("p (g n) -> p g n", n=512), w=[("Kd", 4)])
                    vload("pool", Vd, 32, 4, D["cvd"][l], ("Vd", 4), [])
                    QB = [A.alloc([128, 2, 512], BF16) for _ in range(4)]
                    for b_ in range(4):
                        memset("pool", QB[b_], 0.0, w=[("QB", b_)])
                    for tq in range(2):
                        q0 = tq * 512
                        jobs = []
                        for m in range(8):
                            g, pr = m // 4, (m % 4) * 32
                            jobs.append(dict(
                                nk=36, r=[("QB", m % 4), "Vdones"],
                                rk=lambda kc: [('Kd', kc // 8 if kc < 32 else 4), ('Vd', kc // 8 if kc < 32 else 4)],
                                pre=lambda g=g, pr=pr, q0=q0, b_=m % 4: cp("pool", QB[b_][pr:pr + 32, :, :],
                                                                           QdS[pr:pr + 32, :, g, q0:q0 + 512],
                                                                           r=["QdS"], w=[("QB", b_)]),
                                qk=lambda kc, g=g, b_=m % 4: [(Kd[:, g, kc * 128:(kc + 1) * 128],
                                                               QB[b_][:, 0 if kc < 32 else 1, :])],
                                v=lambda kc, m=m: Vd[:, kc, (m // 4) * 192 + ((m // 2) % 2) * 64:(m // 4) * 192 + ((m // 2) % 2) * 64 + 128],
                                epi=make_epi(lambda lo, hi, m=m: Om[lo:hi, (m // 4) * 2 + m % 2, :], (m // 2) % 2, 512, rcp, "Om")))
                        attention2([(jobs[2 * i], jobs[2 * i + 1]) for i in range(4)], 32 ** -0.5, ptb, 512)
                        diff_epilogue(Om, q0, 512, dtmp)
                    B.barrier()
                    A.release(m1)

                    ck(7)
                    m1 = A.mark()
                    Ck = A.alloc([128, 4608], BF16)
                    for r_ in range(4):
                        B.dma("sp", Ck[:, r_ * 1024:(r_ + 1) * 1024], xo_piece(r_, X_CKV, 1024), r=XOUT, w=[("Ck", r_)])
                    B.dma("pool", Ck[:, 4096:4608], D["cckvT"][l], w=[("Ck", 4)])
                    for hp in range(2):
                        m2 = A.mark()
                        KK = [A.alloc([128, 4608], BF16) for _ in range(2)]
                        Vm = A.alloc([128, 36, 192], BF16)
                        QMB = [A.alloc([128, 2, 512], BF16) for _ in range(4)]
                        memset("pool", Vm[:, :, 64:128], 1.0, w=["Vmones"])
                        for hh in range(2):
                            h = hp * 2 + hh
                            memset("pool", KK[hh][96:128, :], 0.0, w=[("KK", hh, "z")])
                            for tq_ in range(2):
                                memset("pool", QMB[tq_ * 2 + hh][96:128, :, :], 0.0, w=[("QMB", tq_ * 2 + hh)])
                            for r_ in range(4):
                                B.dma("sp", KK[hh][64:96, r_ * 1024:(r_ + 1) * 1024], xo_piece(r_, X_KR, 1024)[64:96, :],
                                      r=XOUT, w=[("KK", hh, "r", r_)])
                            B.dma("pool", KK[hh][64:96, 4096:4608], D["ckrT"][l][64:96, :], w=[("KK", hh, "r", 4)])
                            for kt in range(9):
                                pb = ps_get()
                                mm(psum[pb][0:64, :], wuk[:, l, h * 64:(h + 1) * 64], Ck[:, kt * 512:(kt + 1) * 512],
                                   start=True, stop=True, r=[("Ck", kt // 2), "wuk"], w=[PK(pb)])
                                cp("dve" if kt % 2 else "act", KK[hh][0:64, kt * 512:(kt + 1) * 512], psum[pb][0:64, :],
                                   r=[PK(pb)], w=[("KK", hh, "n", kt)])
                        for kc in range(36):
                            pb = ps_get()
                            mm(psum[pb][:, 0:128], Ck[:, kc * 128:(kc + 1) * 128], wuv[:, l, hp * 128:(hp + 1) * 128], start=True, stop=True,
                               r=[("Ck", kc // 8 if kc < 32 else 4), "wuv"], w=[PK(pb)])
                            cp("dve", Vm[:, kc, :].rearrange("p (a b) -> p a b", b=64)[:, 0:3:2, :],
                               psum[pb][:, 0:128].rearrange("p (h c) -> p h c", c=64), r=[PK(pb)], w=[("Vm", kc)])
                        mpairs = []
                        for tq in range(2):
                            q0 = tq * 512
                            jobs = []
                            for hh in range(2):
                                h = hp * 2 + hh
                                pr = hh * 64
                                qi = tq * 2 + hh

                                def pre(qi=qi, h=h, pr=pr, hp=hp, q0=q0):
                                    for ver in range(2):
                                        cp("dve", QMB[qi][0:64, ver, :], QnS[pr:pr + 64, hp, q0:q0 + 512], r=["QnS"], w=[("QMB", qi)])
                                        cp("dve", QMB[qi][64:96, ver, :], QrS[32 * h:32 * h + 32, ver, q0:q0 + 512],
                                           r=["QrS"], w=[("QMB", qi)])
                                jobs.append(dict(
                                    nk=36, r=[("QMB", qi), ("KK", hh, "z"), "Vmones"], pre=pre,
                                    rk=lambda kc, hh=hh: [("KK", hh, "r", kc // 8 if kc < 32 else 4), ("KK", hh, "n", kc // 4),
                                                          ("Vm", kc)],
                                    qk=lambda kc, hh=hh, qi=qi: [(KK[hh][:, kc * 128:(kc + 1) * 128], QMB[qi][:, 0 if kc < 32 else 1, :])],
                                    v=lambda kc, hh=hh: Vm[:, kc, hh * 64:hh * 64 + 128],
                                    epi=make_epi(lambda lo, hi, hp=hp, q0=q0: yT[lo:hi, 4 + hp, q0:q0 + 512], hh, 512, rcp)))
                            mpairs.append((jobs[0], jobs[1]))
                        attention2(mpairs, 96 ** -0.5, ptb, 512)
                        B.barrier()
                        A.release(m2)
                    A.release(m1)

                    ck(8)
                    m1 = A.mark()
                    Kg = A.alloc([128, 4608], BF16)
                    Vg = A.alloc([128, 36, 384], BF16)
                    memset("pool", Vg.rearrange("p c (a b) -> p c a b", b=192)[:, :, :, 64:128], 1.0, w=["Vgones"])

                    def vloadg(q, c0, nchunk, src, rkeys):
                        sv = src.rearrange("p (c a n) -> p c a n", a=2, n=64)
                        dv = Vg[:, c0:c0 + nchunk, :].rearrange("p c (a b) -> p c a b", b=192)
                        B.dma(q, dv[:, :, :, 0:64], sv, r=rkeys, w=[("Vg", c0 // 8)])
                        B.dma(q, dv[:, :, :, 128:192], sv, r=rkeys, w=[("Vg", c0 // 8)])
                    for r_ in range(4):
                        B.dma("sp", Kg[:, r_ * 1024:(r_ + 1) * 1024], xo_piece(r_, X_KG, 1024), r=XOUT, w=[("Kg", r_)])
                        vloadg("sp", r_ * 8, 8, xo_piece(r_, X_VG, 1024), XOUT)
                    B.dma("pool", Kg[:, 4096:4608], D["ckgT"][l], w=[("Kg", 4)])
                    vloadg("pool", 32, 4, D["cvg"][l], [])
                    for tq in range(2):
                        q0 = tq * 512
                        jobs = []
                        for h in range(4):
                            kv, ab = h // 2, h % 2
                            jobs.append(dict(
                                nk=36, r=["QgS", "Vgones"],
                                rk=lambda kc: [('Kg', kc // 8 if kc < 32 else 4), ('Vg', kc // 8 if kc < 32 else 4)],
                                qk=lambda kc, kv=kv, ab=ab, q0=q0: [(Kg[64 * kv:64 * kv + 64, kc * 128:(kc + 1) * 128],
                                                                     QgS[64 * kv:64 * kv + 64, 0 if kc < 32 else 1, ab, q0:q0 + 512])],
                                v=lambda kc, kv=kv, ab=ab: Vg[:, kc, kv * 192 + ab * 64:kv * 192 + ab * 64 + 128],
                                epi=make_epi(lambda lo, hi, h=h, q0=q0: yT[lo:hi, 6 + h // 2, q0:q0 + 512], h % 2, 512, rcp)))
                        attention2([(jobs[0], jobs[2]), (jobs[1], jobs[3])], 64 ** -0.5, ptb, 512)
                    B.barrier()
                    A.release(mSa)
                    ck(9)
                    outproj(TP)
                    ck(10)
                A.release(mL)

                mM = A.mark()
                hT2 = A.alloc([128, 8, T], BF16)
                m1 = A.mark()
                sq = [A.alloc([128, 8, 512], BF16) for _ in range(2)]
                ntmp = [A.alloc([128, 2, 512], F32) for _ in range(2)]
                rstd = [A.alloc([128, 512], F32) for _ in range(2)]
                for grp_t0, v in ((0, 0), (TP, 1)):
                    fm_norm(grp_t0, 1024, lambda k, o, g0=grp_t0: hT2[:, k, g0 + o:g0 + o + 512],
                            lambda k, v=v: modA_ap(l, v, 1, k), lambda k, v=v: mods_ap(l, 3, k, v), ntmp, sq, rstd)
                B.barrier()
                A.release(m1)
                hid = A.alloc([128, 4, T], BF16)
                rl = [A.alloc([128, 512], F32) for _ in range(2)]
                w1b = [A.alloc([128, 8, 512], BF16) for _ in range(2)]
                w2b = [A.alloc([128, 4, 1024], BF16) for _ in range(2)]
                adw = [A.alloc([128, 8, 1024], BF16) for _ in range(2)] if l + 1 < NL else None
                w1v = D["w1"][l].rearrange("p (k c) -> p k c", c=4096)
                w2v = D["w2"][l].rearrange("p (j o) -> p j o", o=1024)
                for e8 in range(8):
                    wb1, wb2 = w1b[e8 % 2], w2b[e8 % 2]
                    k1, k2 = ("w1b", e8 % 2), ("w2b", e8 % 2)
                    B.dma("pool", wb1, w1v[:, :, e8 * 512:(e8 + 1) * 512], w=[k1])
                    B.dma("pool", wb2, w2v[:, e8 * 4:(e8 + 1) * 4, :], w=[k2])
                    if adw is not None and e8 < 6:
                        mods_load(l + 1, e8, adw[e8 % 2], ("adw", e8 % 2))
                    for tq in range(4):
                        qsl = slice(tq * 512, (tq + 1) * 512)
                        for jj in range(4):
                            pb = ps_get()
                            for k in range(8):
                                mm(psum[pb][:, :], wb1[:, k, jj * 128:(jj + 1) * 128], hT2[:, k, qsl], start=(k == 0), stop=(k == 7),
                                   r=[k1, ("hT", k)], w=[PK(pb)])
                            rk = ("rl", (tq * 4 + jj) % 2)
                            act(rl[(tq * 4 + jj) % 2], psum[pb][:, :], AF.Relu, r=[PK(pb)], w=[rk])
                            tt("pool", hid[:, jj, qsl], rl[(tq * 4 + jj) % 2], rl[(tq * 4 + jj) % 2], ALU.mult, r=[rk],
                               w=[("hid", jj, tq)])
                    for tq in range(4):
                        qsl = slice(tq * 512, (tq + 1) * 512)
                        v = 0 if tq < 2 else 1
                        for o in range(8):
                            pb = ps_get()
                            for jj in range(4):
                                mm(psum[pb][:, :], wb2[:, jj, o * 128:(o + 1) * 128], hid[:, jj, qsl], start=(jj == 0), stop=(jj == 3),
                                   r=[k2, ("hid", jj, tq)], w=[PK(pb)])
                            stt("dve", xT[:, o, qsl], psum[pb][:, :], mods_ap(l, 5, o, v), xT[:, o, qsl], ALU.mult, ALU.add,
                                r=[PK(pb), "mods", "xT"], w=["xT"])
                    if adw is not None and e8 < 6:
                        mods_piece(l + 1, e8, adw[e8 % 2], ("adw", e8 % 2))
                if adw is not None:
                    mods_finish(l + 1)
                B.barrier()
                A.release(mM)
                ck(11)

            mF = A.mark()
            sq = [A.alloc([128, 8, 512], BF16) for _ in range(2)]
            ntmp = [A.alloc([128, 2, 512], F32) for _ in range(2)]
            rstd = [A.alloc([128, 512], F32) for _ in range(2)]
            yo = A.alloc([128, 8, 1024], F32)
            yTv = D["yT"].rearrange("p (c t) -> p c t", t=T)
            for g0 in (0, TP):
                fm_norm(g0, 1024, lambda k, o: yo[:, k, o:o + 512], lambda k: gains[:, 32 + k:33 + k], lambda k: None,
                        ntmp, sq, rstd)
                B.dma("sp", yTv[:, :, g0:g0 + 1024], yo, r=[("hT", k) for k in range(8)], w=["yout"])

        except StopBuild:
            pass
        print("arena peak words", A.peak, "of", NW, {e: len(B.ops[e]) for e in B.ENGS})
        block = es.enter_context(nc.Block())
        B.emit(block, sems, dsems)
    return nc


def _prep(inp):
    f32 = np.float32
    bf = ml_dtypes.bfloat16

    def fm(w, k):
        C = w.shape[1]
        return np.ascontiguousarray(w.reshape(k, 128, C).transpose(1, 0, 2).reshape(128, k * C))

    shared = {}
    shared["adaw"] = np.stack([fm(inp["ada_w"][l], 8) for l in range(NL)])
    shared["adab"] = np.stack([np.ascontiguousarray(inp["ada_b"][l].reshape(48, 128).T) for l in range(NL)])
    shared["gmix"] = np.stack([np.ascontiguousarray(inp["norm_mix_g"][l].reshape(8, 128).T) for l in range(NL)])
    shared["gmlp"] = np.stack([np.ascontiguousarray(inp["norm_mlp_g"][l].reshape(8, 128).T) for l in range(NL)])
    shared["gfin"] = np.ascontiguousarray(inp["final_norm_g"].reshape(8, 128).T)
    shared["win"] = np.stack([fm(inp["w_in"][l], 8) for l in range(NL)])
    wuq = np.zeros((NL, 128, 2, 384), f32)
    for l in range(NL):
        wuq[l, :, 0, :] = inp["mla_w_uq"][l][0:128]
        wuq[l, 0:64, 1, :] = inp["mla_w_uq"][l][128:192]
    shared["wuq"] = wuq.reshape(NL, 128, 768)
    ukv = inp["mla_w_ukv"].reshape(NL, 128, 4, 128)
    shared["wuk"] = np.ascontiguousarray(ukv[:, :, :, 0:64].reshape(NL, 128, 256))
    shared["wuv"] = np.ascontiguousarray(ukv[:, :, :, 64:128].reshape(NL, 128, 256))
    shared["wout"] = np.stack([fm(inp["w_out"][l], 8) for l in range(NL)])
    shared["w1"] = np.stack([fm(inp["mlp_w1"][l], 8) for l in range(NL)])
    shared["w2"] = np.stack([fm(inp["mlp_w2"][l], 32) for l in range(NL)])
    gt = np.concatenate([inp["mla_q_norm_g"], inp["mla_kv_norm_g"], inp["gqa_q_norm_g"], inp["gqa_k_norm_g"]], axis=1)
    shared["gt"] = np.ascontiguousarray(np.broadcast_to(gt[:, None, :], (NL, 128, 448))).astype(f32)
    shared["gsub"] = np.ascontiguousarray(inp["diff_subln_g"].reshape(NL, 64, 1))
    shared["lamp"] = np.ascontiguousarray(np.broadcast_to(inp["diff_lambda"].reshape(NL, 1, 128), (NL, 128, 128))).astype(f32)
    shared["ident"] = np.eye(128, dtype=f32)
    s = np.arange(256, dtype=np.float64)
    ang = 2 * np.pi * np.outer(s, s) / 256
    shared["c256"] = fm(np.cos(ang), 2).astype(bf)
    shared["s256"] = fm(np.sin(ang), 2).astype(bf)
    c = np.arange(64, dtype=np.float64)
    a64 = 2 * np.pi * np.outer(c, c) / 64
    c64 = np.zeros((128, 128)); s64 = np.zeros((128, 128))
    for g in range(2):
        c64[g * 64:(g + 1) * 64, g * 64:(g + 1) * 64] = np.cos(a64)
        s64[g * 64:(g + 1) * 64, g * 64:(g + 1) * 64] = -np.sin(a64)
    shared["c64"] = c64.astype(bf)
    shared["s64n"] = s64.astype(bf)

    def rope_tabs(pos, d, H):
        a = d // 2
        inv = 10000.0 ** (-np.arange(0, a, 2, dtype=np.float64) / a)
        row = (pos // 64).astype(np.float64); col = (pos % 64).astype(np.float64)
        cr, sr = np.cos(row[:, None] * inv), np.sin(row[:, None] * inv)
        cc, sc = np.cos(col[:, None] * inv), np.sin(col[:, None] * inv)
        cs = np.stack([cr, cc], axis=1)
        sn = np.stack([sr, sc], axis=1)
        cs = np.broadcast_to(cs[:, None], (len(pos), H, 2, a // 2)).reshape(len(pos), -1)
        sn = np.broadcast_to(sn[:, None], (len(pos), H, 2, a // 2)).reshape(len(pos), -1)

        def lay(x):
            return np.ascontiguousarray(x.reshape(8, 128, 128).transpose(1, 0, 2).reshape(128, 1024)).astype(f32)
        return lay(cs), lay(sn)

    maps = []
    kk = np.arange(1024, dtype=np.float64)
    ss_ = np.arange(4096, dtype=np.float64)
    for i in range(8):
        b, r = i // 4, i % 4
        m = dict(shared)
        xp = inp["x_prompt"][4 * i:4 * i + 4].reshape(1024, 1024)
        xs = inp["x_sample"][b, r * 1024:(r + 1) * 1024]
        x = np.concatenate([xp, xs], axis=0)
        m["xT"] = np.ascontiguousarray(x.T.reshape(8, 128, T).transpose(1, 0, 2).reshape(128, 8 * T))
        cvv = np.stack([inp["c_ctx"], inp["c"][b]], axis=1)
        m["cv"] = np.ascontiguousarray(cvv.reshape(8, 128, 2).transpose(1, 0, 2).reshape(128, 16))
        pos = np.arange(r * 1024, (r + 1) * 1024)
        m["rc32"], m["rs32"] = rope_tabs(pos, 32, 8)
        m["rc64"], m["rs64"] = rope_tabs(pos, 64, 4)
        ang = 2 * np.pi * ((np.outer(ss_, kk + r * 1024)) % 4096) / 4096
        def lay_big(x):
            return np.ascontiguousarray(x.reshape(32, 128, 2, 512).transpose(1, 2, 0, 3).reshape(128, 32 * 1024)).astype(bf)
        m["cbig"] = lay_big(np.cos(ang))
        m["sbig"] = lay_big(np.sin(ang))
        m["ckdT"] = np.stack([np.ascontiguousarray(inp["cache_diff_k"][b, l].reshape(512, 2, 128).transpose(2, 1, 0).reshape(128, 1024)) for l in range(NL)])
        m["cvd"] = np.stack([fm(inp["cache_diff_v"][b, l].reshape(512, 256), 4) for l in range(NL)])
        m["cckvT"] = np.stack([np.ascontiguousarray(inp["cache_mla_ckv"][b, l].T) for l in range(NL)])
        m["ckrT"] = np.stack([np.ascontiguousarray(np.tile(inp["cache_mla_krope"][b, l].T, (4, 1))) for l in range(NL)])
        m["ckgT"] = np.stack([np.ascontiguousarray(inp["cache_gqa_k"][b, l].reshape(512, 128).T) for l in range(NL)])
        m["cvg"] = np.stack([fm(inp["cache_gqa_v"][b, l].reshape(512, 128), 4) for l in range(NL)])
        maps.append(m)
    return maps


_NC = None


def kernel(**inputs):
    global _NC
    inp = {k: np.asarray(v) for k, v in inputs.items()}
    maps = _prep(inp)
    if _NC is None:
        _NC = build_program()
    res = run_bass_kernel_spmd(_NC, maps, core_ids=list(range(8)))
    R = res.results
    y_prompt = np.zeros((32, 256, 1024), np.float32)
    y_sample = np.zeros((2, 4096, 1024), np.float32)
    outs = {k: np.zeros(s, np.float32) for k, s in (("o_dk", (32, NL, 256, 4, 64)), ("o_dv", (32, NL, 256, 4, 64)),
                                                     ("o_ckv", (32, NL, 256, 128)), ("o_kr", (32, NL, 256, 32)),
                                                     ("o_gk", (32, NL, 256, 2, 64)), ("o_gv", (32, NL, 256, 2, 64)))}
    for i in range(8):
        b, r = i // 4, i % 4
        yT = np.asarray(R[i]["yT"]).reshape(128, 8, T)
        y = yT.transpose(2, 1, 0).reshape(T, 1024)
        y_prompt[4 * i:4 * i + 4] = y[0:TP].reshape(4, 256, 1024)
        y_sample[b, r * 1024:(r + 1) * 1024] = y[TP:]
        for k in outs:
            outs[k][4 * i:4 * i + 4] = np.asarray(R[i][k]).reshape(outs[k][4 * i:4 * i + 4].shape)
    return (y_prompt, y_sample, outs["o_dk"], outs["o_dv"], outs["o_ckv"], outs["o_kr"], outs["o_gk"], outs["o_gv"])
```

```python
import contextlib
import math
import numpy as np
import ml_dtypes
import concourse.bass as bass
import concourse.mybir as mybir
from concourse.bass_utils import run_bass_kernel_spmd

F32 = mybir.dt.float32
BF16 = mybir.dt.bfloat16
AF = mybir.ActivationFunctionType
ALU = mybir.AluOpType

NL = 2
EPS = 1e-6
TP = 1024
TS = 1024
T = TP + TS
XW = 10240
X_KD, X_KG, X_CKV, X_KR, X_VD, X_VG, X_U = 0, 2048, 3072, 4096, 5120, 7168, 8192
SAME_ENG_SYNC = True
DEBUG = {}
NDMASEM = 12


class Op:
    __slots__ = ("eng", "fn", "deps", "dma", "sem", "cnt", "need", "idx", "cc")

    def __init__(self, eng, fn, dma):
        self.eng, self.fn, self.dma = eng, fn, dma
        self.cc = False
        self.deps = set()
        self.sem = None
        self.cnt = 0
        self.need = False


class StopBuild(Exception):
    pass


def ck(n):
    if DEBUG.get('stop') == n:
        raise StopBuild()


class Builder:
    ENGS = ("pe", "act", "dve", "pool", "sp")

    def __init__(self, nc):
        self.nc = nc
        self.ops = {e: [] for e in self.ENGS}
        self.lastw = {}
        self.readers = {}
        self.pending = {e: set() for e in self.ENGS}
        self.dma_ring = {"sp": [None] * NDMASEM, "pool": [None] * NDMASEM}
        self.dma_rr = {"sp": 0, "pool": 0}
        self.dmas_since_bar = []

    def op(self, eng, fn, r=(), w=(), dma=False, slot=None, cc=False):
        o = Op(eng, fn, dma)
        o.cc = cc
        deps = set()
        if dma:
            i = self.dma_rr[eng]
            self.dma_rr[eng] = (i + 1) % NDMASEM
            prev = self.dma_ring[eng][i]
            if prev is not None:
                deps.add(prev)
            self.dma_ring[eng][i] = o
            o.sem = (eng, i)
        slot = o.sem if dma else (("cc", len(self.ops[eng])) if cc else eng)
        for k in r:
            deps.update(self.lastw.get(k, {}).values())
        for k in w:
            deps.update(self.lastw.get(k, {}).values())
            deps.update(self.readers.get(k, {}).values())
        for k in r:
            self.readers.setdefault(k, {})[slot] = o
        for k in w:
            self.lastw.setdefault(k, {})[slot] = o
            self.readers[k] = {}
        deps.update(self.pending[eng])
        self.pending[eng] = set()
        deps.discard(o)
        o.deps = deps
        self.ops[eng].append(o)
        return o

    def dma(self, eng, out, in_, r=(), w=()):
        return self.op(eng, lambda e: e.dma_start(out=out, in_=in_), r=r, w=w, dma=True)

    def barrier(self):
        lasts = set()
        for e in self.ENGS:
            for o in reversed(self.ops[e]):
                if not o.dma and not o.cc:
                    lasts.add(o)
                    break
        for e in ("sp", "pool"):
            for o in self.dma_ring[e]:
                if o is not None:
                    lasts.add(o)
        for e in self.ENGS:
            self.pending[e] |= lasts

    def emit(self, block, sems, dsems):
        nc = self.nc
        for e in self.ENGS:
            for o in self.ops[e]:
                for d in o.deps:
                    if d.dma or d.cc:
                        continue
                    if d.eng == o.eng and (d.eng == "pe" or not SAME_ENG_SYNC):
                        continue
                    d.need = True
        for e in self.ENGS:
            c = 0
            cd = {}
            for o in self.ops[e]:
                if o.dma:
                    cd[o.sem] = cd.get(o.sem, 0) + 16
                    o.cnt = cd[o.sem]
                elif o.cc:
                    cd["cc"] = cd.get("cc", 0) + 1
                    o.cnt = cd["cc"]
                elif o.need:
                    c += 1
                    o.cnt = c

        def run(engname, engobj):
            waited = {}
            for o in self.ops[engname]:
                for d in sorted(o.deps, key=lambda x: (x.eng, x.cnt)):
                    if d.dma:
                        key, s, v = d.sem, dsems[d.sem], d.cnt
                    elif d.cc:
                        key, s, v = "cc", dsems["cc"], d.cnt
                    else:
                        if d.eng == o.eng and (d.eng == "pe" or not SAME_ENG_SYNC):
                            continue
                        key, s, v = d.eng, sems[d.eng], d.cnt
                    if waited.get(key, 0) >= v:
                        continue
                    waited[key] = v
                    engobj.wait_ge(s, v)
                ins = o.fn(engobj)
                if o.dma:
                    ins.then_inc(dsems[o.sem], 16)
                elif o.cc:
                    ins.then_inc(dsems["cc"], 1)
                elif o.need:
                    ins.then_inc(sems[o.eng], 1)

        @block.tensor
        def _(t):
            run("pe", t)

        @block.scalar
        def _(t):
            run("act", t)

        @block.vector
        def _(t):
            run("dve", t)

        @block.gpsimd
        def _(t):
            run("pool", t)

        @block.sync
        def _(t):
            run("sp", t)
            for q in ("sp", "pool"):
                for o in self.dma_ring[q]:
                    if o is not None:
                        t.wait_ge(dsems[o.sem], o.cnt)


class Arena:
    def __init__(self, tensor, nwords):
        self.t = tensor
        self.n = nwords
        self.top = 0
        self.peak = 0

    def alloc(self, shape, dt):
        free = 1
        for s in shape[1:]:
            free *= s
        words = free if dt == F32 else (free + 1) // 2
        words = (words + 7) // 8 * 8
        off = self.top
        self.top += words
        self.peak = max(self.peak, self.top)
        assert self.top <= self.n, f"arena overflow {self.top} > {self.n}"
        ap = self.t[0:shape[0], off:off + words]
        if dt != F32:
            ap = ap.bitcast(dt)
        ap = ap[:, 0:free]
        if len(shape) == 3:
            ap = ap.rearrange("p (a b) -> p a b", b=shape[2])
        elif len(shape) == 4:
            ap = ap.rearrange("p (a b c) -> p a b c", b=shape[2], c=shape[3])
        return ap

    def mark(self):
        return self.top

    def release(self, m):
        self.top = m


def build_program():
    nc = bass.Bass("TRN2", target_bir_lowering=False)
    D = {}

    def din(name, shape, dt=F32):
        D[name] = nc.dram_tensor(name, list(shape), dt, kind="ExternalInput").ap()
        return D[name]

    def dout(name, shape, dt=F32):
        D[name] = nc.dram_tensor(name, list(shape), dt, kind="ExternalOutput").ap()
        return D[name]

    din("xT", [128, 8 * T])
    din("cv", [128, 16])
    din("adaw", [NL, 128, 8 * 6144])
    din("adab", [NL, 128, 48])
    din("gmix", [NL, 128, 8])
    din("gmlp", [NL, 128, 8])
    din("gfin", [128, 8])
    din("win", [NL, 128, 8 * 1888])
    din("wuq", [NL, 128, 2 * 384])
    din("wuk", [NL, 128, 256])
    din("wuv", [NL, 128, 256])
    din("wout", [NL, 128, 8 * 1024])
    din("w1", [NL, 128, 8 * 4096])
    din("w2", [NL, 128, 32 * 1024])
    din("gt", [NL, 128, 448])
    din("gsub", [NL, 64, 1])
    din("lamp", [NL, 128, 128])
    for nm in ("rc32", "rs32", "rc64", "rs64"):
        din(nm, [128, 8 * 128])
    din("c256", [128, 2 * 256], BF16)
    din("s256", [128, 2 * 256], BF16)
    din("cbig", [128, 32 * 1024], BF16)
    din("sbig", [128, 32 * 1024], BF16)
    din("c64", [128, 128], BF16)
    din("s64n", [128, 128], BF16)
    din("ident", [128, 128])
    din("ckdT", [NL, 128, 2 * 512])
    din("cvd", [NL, 128, 4 * 256])
    din("cckvT", [NL, 128, 512])
    din("ckrT", [NL, 128, 512])
    din("ckgT", [NL, 128, 512])
    din("cvg", [NL, 128, 4 * 128])
    dout("yT", [128, 8 * T])
    dout("o_dk", [4, NL, 256, 256])
    dout("o_dv", [4, NL, 256, 256])
    dout("o_ckv", [4, NL, 256, 128])
    dout("o_kr", [4, NL, 256, 32])
    dout("o_gk", [4, NL, 256, 128])
    dout("o_gv", [4, NL, 256, 128])
    SEGW = (4096, 4096, 2048)
    xin = [[nc.dram_tensor(f"xin{l}_{j}", [128, SEGW[j]], BF16) for j in range(3)] for l in range(NL)]
    xout = [[nc.dram_tensor(f"xout{l}_{j}", [4 * 128, SEGW[j]], BF16) for j in range(3)] for l in range(NL)]

    es = contextlib.ExitStack()
    with es:
        NW = 52400
        arena_t = es.enter_context(nc.sbuf_tensor("arena", [128, NW], F32))
        A = Arena(arena_t, NW)
        psall = es.enter_context(nc.psum_tensor("psall", [128, 4096], F32))
        psum = [psall[:, i * 512:(i + 1) * 512] for i in range(8)]
        sems = {e: es.enter_context(nc.semaphore(f"s_{e}")) for e in Builder.ENGS}
        dsems = {(q, i): es.enter_context(nc.semaphore(f"d_{q}{i}")) for q in ("sp", "pool") for i in range(NDMASEM)}
        s_cc = es.enter_context(nc.semaphore("s_cc"))
        dsems["cc"] = s_cc
        B = Builder(nc)

        ps_state = {"rr": 0, "held": set(), "lo": 0}

        def ps_get(hold=False):
            for _ in range(8):
                i = ps_state["rr"]
                ps_state["rr"] = (i + 1) % 8
                if i < ps_state["lo"]:
                    continue
                if i not in ps_state["held"]:
                    if hold:
                        ps_state["held"].add(i)
                    return i
            raise RuntimeError("no psum bank")

        def ps_rel(i):
            ps_state["held"].discard(i)

        def PK(i):
            return ("ps", i)

        uid = [0]

        def U():
            uid[0] += 1
            return ("u", uid[0])

        def mm(out, lhsT, rhs, start, stop, r, w):
            try:
                bp = lhsT.base_partition()
            except AssertionError:
                bp = 96
            kw = {"tile_position": (96, 0)} if bp == 96 else {}
            return B.op("pe", lambda e: e.matmul(out, lhsT, rhs, start=start, stop=stop, **kw), r=r, w=w)

        def transpose(out, in_, ident, r, w):
            return B.op("pe", lambda e: e.transpose(out, in_, ident), r=r, w=w)

        def act(out, in_, func, r, w, bias=None, scale=None, accum_out=None):
            kw = {}
            if bias is not None:
                kw["bias"] = bias
            if scale is not None:
                kw["scale"] = scale
            if accum_out is not None:
                kw["accum_out"] = accum_out
            return B.op("act", lambda e: e.activation(out=out, in_=in_, func=func, **kw), r=r, w=w)

        def tt(eng, out, in0, in1, op, r, w):
            return B.op(eng, lambda e: e.tensor_tensor(out=out, in0=in0, in1=in1, op=op), r=r, w=w)

        def stt(eng, out, in0, scalar, in1, op0, op1, r, w):
            return B.op(eng, lambda e: e.scalar_tensor_tensor(out=out, in0=in0, scalar=scalar, in1=in1,
                                                              op0=op0, op1=op1), r=r, w=w)

        def ts(eng, out, in0, s1, s2, op0, op1, r, w):
            return B.op(eng, lambda e: e.tensor_scalar(out=out, in0=in0, scalar1=s1, scalar2=s2,
                                                       op0=op0, op1=op1), r=r, w=w)

        def cp(eng, out, in_, r, w):
            if eng == "act":
                return act(out, in_, AF.Copy, r, w)
            return B.op(eng, lambda e: e.tensor_copy(out=out, in_=in_), r=r, w=w)

        def recip(out, in_, r, w):
            return B.op("dve", lambda e: e.reciprocal(out=out, in_=in_), r=r, w=w)

        def memset(eng, ap, val, w):
            return B.op(eng, lambda e: e.memset(ap, val), w=w)

        xT = A.alloc([128, 8, T], F32)
        ident = A.alloc([128, 128], F32)
        ones_bf = A.alloc([128, 128], BF16)
        epsc = A.alloc([128, 1], F32)
        mods = A.alloc([128, NL * 48 * 2], F32)
        modA = A.alloc([128, NL * 2 * 2 * 8], F32)
        gains = A.alloc([128, NL * 2 * 8 + 8], F32)
        gtab = A.alloc([128, NL * 448], F32)
        gsub = A.alloc([128, NL], F32)
        onesbd = A.alloc([128, 128], BF16)
        lam = A.alloc([128, NL * 4], F32)
        c64 = A.alloc([128, 128], BF16)
        s64n = A.alloc([128, 128], BF16)
        c256 = A.alloc([128, 2, 256], BF16)
        s256 = A.alloc([128, 2, 256], BF16)
        wuq = A.alloc([128, NL * 2, 384], BF16)
        wuk = A.alloc([128, NL, 256], BF16)
        wuv = A.alloc([128, NL, 256], BF16)
        cvb = A.alloc([128, 8, 2], BF16)
        adab = A.alloc([128, NL, 48], F32)
        mrow = A.alloc([128, 1024], F32)

        def mods_ap(l, m, k, v):
            i = (l * 48 + m * 8 + k) * 2 + v
            return mods[:, i:i + 1]

        def modA_ap(l, v, which, k):
            i = ((l * 2 + v) * 2 + which) * 8 + k
            return modA[:, i:i + 1]

        try:
            B.dma("sp", xT, D["xT"].rearrange("p (c t) -> p c t", t=T), w=["xT"])
            B.dma("sp", ident, D["ident"], w=["ident"])
            memset("dve", ones_bf, 1.0, w=["ones"])
            memset("dve", epsc, EPS, w=["eps"])
            memset("dve", onesbd, 0.0, w=["onesbd"])
            memset("dve", onesbd[0:64, 0:64], 1.0, w=["onesbd"])
            memset("dve", onesbd[64:128, 64:128], 1.0, w=["onesbd"])
            B.dma("sp", gains[:, 0:16].rearrange("p (l k) -> p l k", k=8), D["gmix"].rearrange("l p k -> p l k"), w=["gains"])
            B.dma("sp", gains[:, 16:32].rearrange("p (l k) -> p l k", k=8), D["gmlp"].rearrange("l p k -> p l k"), w=["gains"])
            B.dma("sp", gains[:, 32:40], D["gfin"], w=["gains"])
            B.dma("sp", gtab.rearrange("p (l c) -> p l c", c=448), D["gt"].rearrange("l p c -> p l c"), w=["gtab"])
            for l_ in range(NL):
                B.dma("sp", gsub[0:64, l_:l_ + 1], D["gsub"][l_], w=["gsub"])
                B.dma("sp", gsub[64:128, l_:l_ + 1], D["gsub"][l_], w=["gsub"])
            B.dma("pool", c64, D["c64"], w=["c64"])
            B.dma("pool", s64n, D["s64n"], w=["c64"])
            B.dma("pool", c256, D["c256"].rearrange("p (a b) -> p a b", b=256), w=["c256"])
            B.dma("pool", s256, D["s256"].rearrange("p (a b) -> p a b", b=256), w=["c256"])
            for l_ in range(NL):
                B.dma("pool", wuq[:, l_ * 2:l_ * 2 + 2, :], D["wuq"][l_].rearrange("p (a b) -> p a b", b=384), w=["wuq"])
            B.dma("pool", wuk, D["wuk"].rearrange("l p c -> p l c"), w=["wuk"])
            B.dma("pool", wuv, D["wuv"].rearrange("l p c -> p l c"), w=["wuv"])

            m0 = A.mark()
            cvf = A.alloc([128, 16], F32)
            lamp = A.alloc([128, NL, 128], F32)
            lprod = A.alloc([128, NL, 64], F32)
            lsum = A.alloc([128, NL * 2], F32)
            adw = [A.alloc([128, 8, 1024], BF16) for _ in range(2)]
            B.dma("sp", cvf, D["cv"], w=["cvf"])
            B.dma("sp", adab, D["adab"].rearrange("l p j -> p l j"), w=["adab"])
            B.dma("sp", lamp, D["lamp"].rearrange("l p c -> p l c"), w=["lamp"])
            act(cvb.rearrange("p a b -> p (a b)"), cvf, AF.Silu, r=["cvf"], w=["cvb"])

            def mods_load(l, m, buf, key):
                B.dma("pool", buf, D["adaw"][l].rearrange("p (k c) -> p k c", c=6144)[:, :, m * 1024:(m + 1) * 1024],
                      w=[key])

            def mods_piece(l, m, buf, key):
                for hf in range(2):
                    pb = ps_get()
                    for k in range(8):
                        mm(psum[pb][0:2, :], cvb[:, k, :], buf[:, k, hf * 512:(hf + 1) * 512], start=(k == 0), stop=(k == 7),
                           r=[key, "cvb"], w=[PK(pb)])
                    cp("dve", mrow[0:2, hf * 512:(hf + 1) * 512], psum[pb][0:2, :], r=[PK(pb)], w=[("mrow", hf)])
                pb = ps_get()
                for j in range(8):
                    transpose(psum[pb][:, 2 * j:2 * j + 2], mrow[0:2, j * 128:(j + 1) * 128], ident[0:2, 0:2],
                              r=[("mrow", j // 4), "ident"], w=[PK(pb)])
                base = (l * 48 + m * 8) * 2
                for v in range(2):
                    dst = mods[:, base:base + 16].rearrange("p (j v) -> p j v", v=2)[:, :, v]
                    src = psum[pb][:, 0:16].rearrange("p (j v) -> p j v", v=2)[:, :, v]
                    tt("dve", dst, src, adab[:, l, m * 8:(m + 1) * 8], ALU.add, r=[PK(pb), "adab"], w=["mods"])

            def mods_finish(l):
                for v in range(2):
                    for which, (msc, goff) in enumerate(((1, 0), (4, 16))):
                        for k in range(8):
                            ts("dve", modA_ap(l, v, which, k), mods_ap(l, msc, k, v), 1.0,
                               gains[:, goff + l * 8 + k:goff + l * 8 + k + 1], ALU.add, ALU.mult,
                               r=["mods", "gains"], w=["modA"])

            for m in range(6):
                mods_load(0, m, adw[m % 2], ("adw", m % 2))
                mods_piece(0, m, adw[m % 2], ("adw", m % 2))
            mods_finish(0)
            for l in range(NL):
                tt("dve", lprod[:, l, :].rearrange("p (a b) -> p a b", b=32),
                   lamp[:, l, :].rearrange("p (a t b) -> p a t b", t=2, b=32)[:, :, 0, :],
                   lamp[:, l, :].rearrange("p (a t b) -> p a t b", t=2, b=32)[:, :, 1, :], ALU.mult,
                   r=["lamp"], w=["lprod"])
                for j in range(2):
                    B.op("dve", lambda e, l=l, j=j: e.reduce_sum(out=lsum[:, l * 2 + j:l * 2 + j + 1],
                                                                 in_=lprod[:, l, j * 32:(j + 1) * 32],
                                                                 axis=mybir.AxisListType.X), r=["lprod"], w=["lsum"])
                act(lsum[:, l * 2:l * 2 + 2], lsum[:, l * 2:l * 2 + 2], AF.Exp, r=["lsum"], w=["lsum"])
                lam_init = 0.8 - 0.6 * math.exp(-0.3 * l)
                tt("dve", lam[:, l * 4:l * 4 + 1], lsum[:, l * 2 + 1:l * 2 + 2], lsum[:, l * 2:l * 2 + 1], ALU.subtract,
                   r=["lsum"], w=["lam"])
                B.op("dve", lambda e, l=l, li=lam_init: e.tensor_scalar_add(out=lam[:, l * 4:l * 4 + 1], in0=lam[:, l * 4:l * 4 + 1],
                                                                      scalar1=-li), r=["lam"], w=["lam"])
            B.barrier()
            A.release(m0)
            ck(1)

            def fm_norm(t0, ntok, hT_dst, scale_ap_fn, bias_ap_fn, tmp, sq, rstd, out_f32=None):
                tmps = tmp if isinstance(tmp, list) else [tmp]
                sqs = sq if isinstance(sq, list) else [sq]
                rstds = rstd if isinstance(rstd, list) else [rstd]
                for it, tt0 in enumerate(range(t0, t0 + ntok, 512)):
                    p = it % len(sqs)
                    tmp_, sq_, rstd_ = tmps[p], sqs[p], rstds[p]
                    sl = slice(tt0, tt0 + 512)
                    pb = ps_get()
                    for k in range(8):
                        if k % 2 == 0:
                            act(sq_[:, k, :], xT[:, k, sl], AF.Square, r=["xT"], w=[("sq", p, k)])
                        else:
                            tt("dve", sq_[:, k, :], xT[:, k, sl], xT[:, k, sl], ALU.mult, r=["xT"], w=[("sq", p, k)])
                    for k in range(8):
                        mm(psum[pb][:, :], ones_bf, sq_[:, k, :], start=(k == 0), stop=(k == 7),
                           r=[("sq", p, k), "ones"], w=[PK(pb)])
                    act(rstd_, psum[pb][:, :], AF.Ln, r=[PK(pb), "eps"], w=[("rstd", p)], bias=epsc[:, 0:1], scale=1.0 / 1024)
                    act(rstd_, rstd_, AF.Exp, r=[("rstd", p)], w=[("rstd", p)], scale=-0.5)
                    for k in range(8):
                        tt("dve", tmp_[:, k % 2, :], xT[:, k, sl], rstd_, ALU.mult, r=["xT", ("rstd", p)], w=[("ntmp", p, k % 2)])
                        b = bias_ap_fn(k)
                        act(hT_dst(k, tt0 - t0), tmp_[:, k % 2, :], AF.Identity, r=[("ntmp", p, k % 2), "mods", "modA", "gains"],
                            w=[("hT", k)], scale=scale_ap_fn(k), **({"bias": b} if b is not None else {}))

            def attention2(pairs, scale, ptb, nq):
                ps_state["lo"] = 4
                merge = (nq == 512)
                pre_done = set()

                def run_pre(ix):
                    if ix < len(pairs) and ix not in pre_done:
                        pre_done.add(ix)
                        for job in pairs[ix]:
                            if "pre" in job:
                                job["pre"]()

                for pidx, pair in enumerate(pairs):
                    nk = pair[0]["nk"]
                    run_pre(pidx)
                    obs = [ps_get(hold=True), ps_get(hold=True)]
                    rr0 = pair[0]["r"] + pair[1]["r"]

                    def rkeys(kc):
                        out = list(rr0)
                        for job in pair:
                            if "rk" in job:
                                out += job["rk"](kc)
                        return out

                    def qk(kc):
                        sbase = (kc % 2) * 2
                        for i, job in enumerate(pair):
                            spec = job["qk"](kc)
                            for j, (kT, qT) in enumerate(spec):
                                mm(psum[sbase + i][:, 0:nq], kT, qT, start=(j == 0), stop=(j == len(spec) - 1),
                                   r=rkeys(kc), w=[PK(sbase + i)])

                    def pv_exp(kc):
                        sbase = (kc % 2) * 2
                        pi_ = kc % len(ptb)
                        pt = ptb[pi_]
                        if merge:
                            act(pt[:, 0:1024], psall[:, sbase * 512:(sbase + 2) * 512], AF.Exp,
                                r=[PK(sbase), PK(sbase + 1)], w=[("pt", pi_)], scale=scale)
                        else:
                            for i in range(2):
                                act(pt[:, i * 512:i * 512 + nq], psum[sbase + i][:, 0:nq], AF.Exp,
                                    r=[PK(sbase + i)], w=[("pt", pi_)], scale=scale)

                    def pv_mm(kc):
                        pi_ = kc % len(ptb)
                        pt = ptb[pi_]
                        for i, job in enumerate(pair):
                            mm(psum[obs[i]][:, 0:nq], job["v"](kc), pt[:, i * 512:i * 512 + nq], start=(kc == 0),
                               stop=(kc == nk - 1), r=rkeys(kc) + [("pt", pi_)], w=[PK(obs[i])])

                    qk(0)
                    if nk > 1:
                        qk(1)
                    run_pre(pidx + 1)
                    for kc in range(nk):
                        pv_exp(kc)
                        if kc + 2 < nk:
                            qk(kc + 2)
                        pv_mm(kc)
                    for i, job in enumerate(pair):
                        job["epi"](obs[i])
                        ps_rel(obs[i])
                ps_state["lo"] = 0

            for l in range(NL):
                lam_init = 0.8 - 0.6 * math.exp(-0.3 * l)
                neglam = lam[0:64, l * 4:l * 4 + 1]
                mL = A.mark()
                QdS = A.alloc([128, 2, 2, TS], BF16)
                QnS = A.alloc([128, 2, TS], BF16)
                QrS = A.alloc([128, 2, TS], BF16)
                QgS = A.alloc([128, 2, 2, TS], BF16)

                def drive(gens):
                    live = list(gens)
                    first = True
                    while live:
                        for g in list(live):
                            try:
                                next(g)
                                if first:
                                    next(g)
                                    next(g)
                                    first = False
                            except StopIteration:
                                live.remove(g)

                def inproj(grp):
                    t0 = TP if grp == "S" else 0
                    v = 1 if grp == "S" else 0
                    G = gtab[:, l * 448:(l + 1) * 448]
                    winv = D["win"][l].rearrange("p (k c) -> p k c", c=1888)
                    mI = A.mark()
                    hT = A.alloc([128, 8, 512], BF16)
                    win0 = A.alloc([128, 8, 512], BF16)
                    win1 = A.alloc([128, 8, 512], BF16)

                    def load_winA():
                        B.dma("pool", win0, winv[:, :, 0:512], w=["win0"])
                        B.dma("pool", win1, winv[:, :, 512:1024], w=["win1"])

                    def load_winB():
                        B.dma("pool", win0[:, :, 0:352], winv[:, :, 1024:1376], w=["win0"])
                        B.dma("pool", win1, winv[:, :, 1376:1888], w=["win1"])
                    load_winA()
                    for half in range(2):
                        m1 = A.mark()
                        sq = A.alloc([128, 8, 512], BF16)
                        ntmp = A.alloc([128, 2, 512], F32)
                        rstd = A.alloc([128, 512], F32)
                        fm_norm(t0 + half * 512, 512, lambda k, o: hT[:, k, :], lambda k: modA_ap(l, v, 0, k),
                                lambda k: mods_ap(l, 0, k, v), ntmp, sq, rstd)
                        B.barrier()
                        A.release(m1)
                        ck(20)

                        def tr(dst, src, ncols, rk, wk, eng="dve"):
                            pb = ps_get()
                            transpose(psum[pb][0:ncols, 0:128], src, ident, r=rk + ["ident"], w=[PK(pb)])
                            cp(eng, dst, psum[pb][0:ncols, 0:128], r=[PK(pb)], w=wk)

                        def rope(dst, src, nh, d, ci, si, rk, wk, rp, rtab, p):
                            q = d // 4
                            n = nh * d
                            sv = src.rearrange("p (a t q) -> p a t q", t=2, q=q)
                            dv = dst.rearrange("p (a t q) -> p a t q", t=2, q=q)
                            cs = rtab[:, ci, 0:n // 2].rearrange("p (a q) -> p a q", q=q)
                            sn = rtab[:, si, 0:n // 2].rearrange("p (a q) -> p a q", q=q)
                            t = [rp[:, i, 0:n // 2].rearrange("p (a q) -> p a q", q=q) for i in range(4)]
                            RK = rk + [("rope", p)]
                            tt("pool", t[0], sv[:, :, 0, :], cs, ALU.mult, r=RK, w=[("rp", p, 0)])
                            tt("pool", t[1], sv[:, :, 1, :], sn, ALU.mult, r=RK, w=[("rp", p, 1)])
                            tt("dve", t[2], sv[:, :, 0, :], sn, ALU.mult, r=RK, w=[("rp", p, 2)])
                            tt("dve", t[3], sv[:, :, 1, :], cs, ALU.mult, r=RK, w=[("rp", p, 3)])
                            tt("pool", dv[:, :, 0, :], t[0], t[1], ALU.subtract, r=[("rp", p, 0), ("rp", p, 1)], w=wk)
                            tt("dve", dv[:, :, 1, :], t[2], t[3], ALU.add, r=[("rp", p, 2), ("rp", p, 3)], w=wk)

                        def load_rope(ti, rtab, p):
                            for j, nm in enumerate(("rc32", "rs32", "rc64", "rs64")):
                                B.dma("sp", rtab[:, j, :], D[nm][:, ti * 128:(ti + 1) * 128], w=[("rope", p)])

                        mA = A.mark()
                        ztA = [A.alloc([128, 1024], F32) for _ in range(2)]
                        zrA = [A.alloc([128, 512], F32) for _ in range(2)] if grp == "S" else [None, None]
                        rpA = [A.alloc([128, 4, 128], F32) for _ in range(2)] if grp == "S" else [None, None]
                        rtA = [A.alloc([128, 4, 128], F32) for _ in range(2)] if grp == "S" else [None, None]
                        def projA(tl):
                            ti = half * 4 + tl
                            p = tl % 2
                            zt, zr, rp, rtab = ztA[p], zrA[p], rpA[p], rtA[p]
                            tsl = slice(ti * 128, (ti + 1) * 128)
                            for (c0, c1, eng, wb, wk) in ((0, 512, "act", win0, "win0"), (512, 1024, "dve", win1, "win1")):
                                pb = ps_get()
                                for k in range(8):
                                    mm(psum[pb][:, 0:c1 - c0], hT[:, k, tl * 128:(tl + 1) * 128], wb[:, k, 0:c1 - c0],
                                       start=(k == 0), stop=(k == 7), r=[("hT", k), wk], w=[PK(pb)])
                                cp(eng, zt[:, c0:c1], psum[pb][:, 0:c1 - c0], r=[PK(pb)], w=[("zt", p, c0)])

                        def postA(tl):
                            ti = half * 4 + tl
                            p = tl % 2
                            zt, zr, rp, rtab = ztA[p], zrA[p], rpA[p], rtA[p]
                            tsl = slice(ti * 128, (ti + 1) * 128)
                            ZA, ZB = [("zt", p, 0)], [("zt", p, 512)]
                            if grp == "S":
                                load_rope(ti, rtab, p)
                                rope(zr[:, 0:256], zt[:, 0:256], 8, 32, 0, 1, ZA, [("zr", p, 0)], rp, rtab, p)
                                yield
                                rope(zr[:, 256:512], zt[:, 256:512], 8, 32, 0, 1, ZA, [("zr", p, 1)], rp, rtab, p)
                                yield
                                for g in range(2):
                                    tr(QdS[:, 0, g, tsl], zr[:, g * 128:(g + 1) * 128], 128, [("zr", p, 0)], ["QdS"])
                                    tr(QdS[:, 1, g, tsl], zt[:, g * 128:(g + 1) * 128], 128, ZA, ["QdS"], eng="act")
                                    tr(XS[:, X_KD + g * 1024 + ti * 128:X_KD + g * 1024 + (ti + 1) * 128],
                                       zr[:, 256 + g * 128:256 + (g + 1) * 128], 128, [("zr", p, 1)], ["XS"])
                                yield
                                cp("pool", XS[:, X_VD + ti * 256:X_VD + (ti + 1) * 256], zt[:, 512:768], r=ZB, w=["XS"])
                                cp("pool", XS[:, X_U + ti * 256:X_U + (ti + 1) * 256], zt[:, 768:1024], r=ZB, w=["XS"])
                            else:
                                sq_i, pos0 = ti // 2, (ti % 2) * 128
                                B.dma("sp", D["o_dk"][sq_i, l, pos0:pos0 + 128, :], zt[:, 256:512], r=ZA)
                                B.dma("sp", D["o_dv"][sq_i, l, pos0:pos0 + 128, :], zt[:, 512:768], r=ZB)
                                for g in range(2):
                                    tr(QdP[:, g, tsl], zt[:, g * 128:(g + 1) * 128], 128, ZA, ["QdP"], eng="act")
                                    tr(KdP[:, g, tsl], zt[:, 256 + g * 128:256 + (g + 1) * 128], 128, ZA, ["KdP"])
                                yield
                                vsrc = zt[:, 512:768].rearrange("p (a e c) -> p a e c", e=2, c=64)
                                vdst = VdP[:, ti, :].rearrange("p (a b) -> p a b", b=192)
                                cp("pool", vdst[:, :, 0:64], vsrc[:, :, 0, :], r=ZB, w=["VdP"])
                                cp("pool", vdst[:, :, 128:192], vsrc[:, :, 1, :], r=ZB, w=["VdP"])
                                cp("pool", UP[:, ti, :], zt[:, 768:1024], r=ZB, w=["UP"])
                            yield

                        def laneA(tiles):
                            for tl in tiles:
                                projA(tl)
                                yield
                                yield from postA(tl)
                        drive([laneA([0, 2]), laneA([1, 3])])
                        load_winB()
                        B.barrier()
                        A.release(mA)
                        ck(23)

                        ztB = [A.alloc([128, 864], F32) for _ in range(2)]
                        znB = [A.alloc([128, 768], F32) for _ in range(2)]
                        ssB = [A.alloc([128, 8], F32) for _ in range(2)]
                        jkB = [A.alloc([128, 736], F32) for _ in range(2)]
                        rpB = [A.alloc([128, 4, 128], F32) for _ in range(2)] if grp == "S" else [None, None]
                        zrB = [A.alloc([128, 384], F32) for _ in range(2)] if grp == "S" else [None, None]
                        qmB = [A.alloc([128, 640], F32) for _ in range(2)]
                        cqB = [A.alloc([128, 2, 128], BF16) for _ in range(2)]
                        rtB = [A.alloc([128, 4, 128], F32) for _ in range(2)] if grp == "S" else [None, None]
                        def projB(tl):
                            ti = half * 4 + tl
                            p = tl % 2
                            zt, zn, ss, junk, rp, zr, qm, cqT, rtab = ztB[p], znB[p], ssB[p], jkB[p], rpB[p], zrB[p], qmB[p], cqB[p], rtB[p]
                            qnope, qrope, qrope_r, kr4 = qm[:, 0:256], qm[:, 256:384], qm[:, 384:512], qm[:, 512:640]
                            tsl = slice(ti * 128, (ti + 1) * 128)
                            for (c0, c1, eng, wb, wk) in ((0, 352, "act", win0, "win0"), (352, 864, "dve", win1, "win1")):
                                pb = ps_get()
                                for k in range(8):
                                    mm(psum[pb][:, 0:c1 - c0], hT[:, k, tl * 128:(tl + 1) * 128], wb[:, k, 0:c1 - c0],
                                       start=(k == 0), stop=(k == 7), r=[("hT", k), wk], w=[PK(pb)])
                                cp(eng, zt[:, c0:c1], psum[pb][:, 0:c1 - c0], r=[PK(pb)], w=[("ztb", p, c0)])

                        def postB(tl):
                            ti = half * 4 + tl
                            p = tl % 2
                            zt, zn, ss, junk, rp, zr, qm, cqT, rtab = ztB[p], znB[p], ssB[p], jkB[p], rpB[p], zrB[p], qmB[p], cqB[p], rtB[p]
                            qnope, qrope, qrope_r, kr4 = qm[:, 0:256], qm[:, 256:384], qm[:, 384:512], qm[:, 512:640]
                            tsl = slice(ti * 128, (ti + 1) * 128)
                            ZR = [("ztb", p, 0), ("ztb", p, 352)]
                            slices = [(0, 192), (192, 128)] + [(352 + 64 * h, 64) for h in range(4)] + \
                                     [(608 + 64 * h, 64) for h in range(2)]
                            SS = [("ss", p, j) for j in range(8)]
                            for (c0, c1, n, j0, j1) in ((0, 192, 192, 0, 1), (192, 320, 128, 1, 2), (352, 736, 64, 2, 8)):
                                stt("dve", junk[:, c0:c1], zt[:, c0:c1], 1.0 / n, zt[:, c0:c1], ALU.mult, ALU.mult,
                                    r=ZR, w=[("junk", p, c0)])
                                B.op("dve", lambda e, c0=c0, c1=c1, n=n, j0=j0, j1=j1, junk=junk, ss=ss: e.reduce_sum(
                                    out=ss[:, j0:j1], in_=junk[:, c0:c1].rearrange("p (a b) -> p a b", b=n),
                                    axis=mybir.AxisListType.X), r=[("junk", p, c0)], w=[("ss", p, j) for j in range(j0, j1)])
                            yield
                            act(ss, ss, AF.Ln, r=SS + ["eps"], w=[("ssr", p)], bias=epsc[:, 0:1], scale=1.0)
                            act(ss, ss, AF.Exp, r=[("ssr", p)], w=[("ssr", p)], scale=-0.5)
                            zdst = [0, 192, 320, 320 + 128, 320 + 64, 320 + 192, 576, 640]
                            gsrc = [0, 192, 320, 320, 320, 320, 384, 384]
                            for j, (c0, n) in enumerate(slices):
                                stt("dve", zn[:, zdst[j]:zdst[j] + n], zt[:, c0:c0 + n], ss[:, j:j + 1], G[:, gsrc[j]:gsrc[j] + n],
                                    ALU.mult, ALU.mult, r=ZR + [("ssr", p), "gtab"], w=[("zn", p, j)])
                            ZN = [("zn", p, j) for j in range(8)]
                            ck(24)
                            yield
                            tr(cqT[:, 0, :], zn[:, 0:128], 128, ZN, [("cqT", p, 0)])
                            tr(cqT[0:64, 1, :], zn[:, 128:192], 64, ZN, [("cqT", p, 1)])
                            pb = ps_get()
                            mm(psum[pb][:, 0:384], cqT[:, 0, :], wuq[:, l * 2, :], start=True, stop=False,
                               r=[("cqT", p, 0), "wuq"], w=[PK(pb)])
                            mm(psum[pb][:, 0:384], cqT[0:64, 1, :], wuq[0:64, l * 2 + 1, :], start=False, stop=True,
                               r=[("cqT", p, 1), "wuq"], w=[PK(pb)])
                            qraw = psum[pb][:, 0:384].rearrange("p (h c) -> p h c", c=96)
                            cp("dve", qnope.rearrange("p (h c) -> p h c", c=64), qraw[:, :, 0:64], r=[PK(pb)], w=[("qnope", p)])
                            cp("dve", qrope.rearrange("p (h c) -> p h c", c=32), qraw[:, :, 64:96], r=[PK(pb)], w=[("qrope", p)])
                            ck(25)
                            yield
                            KR = zt[:, 320:352]
                            DV = zt[:, 736:864]
                            if grp == "S":
                                load_rope(ti, rtab, p)
                                rope(zr[:, 0:256], zn[:, 320:576], 4, 64, 2, 3, ZN, [("zr", p, 2)], rp, rtab, p)
                                yield
                                rope(zr[:, 256:384], zn[:, 576:704], 2, 64, 2, 3, ZN, [("zr", p, 3)], rp, rtab, p)
                                yield
                                rope(qrope_r, qrope, 4, 32, 0, 1, [("qrope", p)], [("qrope_r", p)], rp, rtab, p)
                                yield
                                rope(kr4[:, 0:32], KR, 1, 32, 0, 1, ZR, [("kr4a", p)], rp, rtab, p)
                                for h in range(1, 4):
                                    cp("pool", kr4[:, 32 * h:32 * h + 32], kr4[:, 0:32], r=[("kr4a", p)], w=[("kr4", p)])
                                ck(26)
                                yield
                                for g in range(2):
                                    tr(QnS[:, g, tsl], qnope[:, g * 128:(g + 1) * 128], 128, [("qnope", p)], ["QnS"])
                                    tr(QgS[:, 0, g, tsl], zr[:, g * 128:(g + 1) * 128], 128, [("zr", p, 2)], ["QgS"])
                                    tr(QgS[:, 1, g, tsl], zn[:, 320 + g * 128:320 + (g + 1) * 128], 128, ZN, ["QgS"], eng="act")
                                yield
                                tr(QrS[:, 0, tsl], qrope_r, 128, [("qrope_r", p)], ["QrS"])
                                tr(QrS[:, 1, tsl], qrope, 128, [("qrope", p)], ["QrS"], eng="act")
                                tr(XS[:, X_KG + ti * 128:X_KG + (ti + 1) * 128], zr[:, 256:384], 128, [("zr", p, 3)], ["XS"])
                                tr(XS[:, X_CKV + ti * 128:X_CKV + (ti + 1) * 128], zn[:, 192:320], 128, ZN, ["XS"], eng="act")
                                tr(XS[:, X_KR + ti * 128:X_KR + (ti + 1) * 128], kr4, 128, [("kr4", p), ("kr4a", p)], ["XS"])
                                cp("pool", XS[:, X_VG + ti * 128:X_VG + (ti + 1) * 128], DV, r=ZR, w=["XS"])
                            else:
                                sq_i, pos0 = ti // 2, (ti % 2) * 128
                                B.dma("sp", D["o_kr"][sq_i, l, pos0:pos0 + 128, :], KR, r=ZR)
                                B.dma("sp", D["o_gv"][sq_i, l, pos0:pos0 + 128, :], DV, r=ZR)
                                B.dma("sp", D["o_ckv"][sq_i, l, pos0:pos0 + 128, :], zn[:, 192:320], r=ZN)
                                B.dma("sp", D["o_gk"][sq_i, l, pos0:pos0 + 128, :], zn[:, 576:704], r=ZN)
                                for h in range(4):
                                    cp("pool", kr4[:, 32 * h:32 * h + 32], KR, r=ZR, w=[("kr4", p)])
                                for g in range(2):
                                    tr(QnP[:, g, tsl], qnope[:, g * 128:(g + 1) * 128], 128, [("qnope", p)], ["QnP"])
                                    tr(QgP[:, g, tsl], zn[:, 320 + g * 128:320 + (g + 1) * 128], 128, ZN, ["QgP"], eng="act")
                                yield
                                tr(QrP[:, tsl], qrope, 128, [("qrope", p)], ["QrP"])
                                tr(KgP[:, tsl], zn[:, 576:704], 128, ZN, ["KgP"])
                                tr(KrP[:, tsl], kr4, 128, [("kr4", p)], ["KrP"], eng="act")
                                tr(CkP[:, tsl], zn[:, 192:320], 128, ZN, [("CkP", ti)])
                                vsrc = DV.rearrange("p (a c) -> p a c", c=64)
                                vdst = VgP[:, ti, :].rearrange("p (a b) -> p a b", b=192)
                                cp("pool", vdst[:, :, 0:64], vsrc, r=ZR, w=["VgP"])
                                cp("pool", vdst[:, :, 128:192], vsrc, r=ZR, w=["VgP"])
                                yield
                                pb2 = ps_get()
                                mm(psum[pb2][:, 0:256], CkP[:, tsl], wuv[:, l, :], start=True, stop=True,
                                   r=[("CkP", ti), "wuv"], w=[PK(pb2)])
                                vsrc = psum[pb2][:, 0:256].rearrange("p (a e c) -> p a e c", e=2, c=64)
                                vdst = VmP[:, ti, :].rearrange("p (a b) -> p a b", b=192)
                                cp("dve", vdst[:, :, 0:64], vsrc[:, :, 0, :], r=[PK(pb2)], w=["VmP"])
                                cp("dve", vdst[:, :, 128:192], vsrc[:, :, 1, :], r=[PK(pb2)], w=["VmP"])
                                for g in range(2):
                                    pb3 = ps_get()
                                    mm(psum[pb3][:, 0:128], wuk[:, l, g * 128:(g + 1) * 128], CkP[:, tsl], start=True, stop=True,
                                       r=[("CkP", ti), "wuk"], w=[PK(pb3)])
                                    cp("act", KnP[:, g, tsl], psum[pb3][:, 0:128], r=[PK(pb3)], w=["KnP"])
                            yield

                        def laneB(tiles):
                            for tl in tiles:
                                projB(tl)
                                yield
                                yield from postB(tl)
                        drive([laneB([0, 2]), laneB([1, 3])])
                        if half == 0:
                            load_winA()
                        B.barrier()
                        A.release(m1)
                    A.release(mI)

                def epi_plain(dst_fn, rcp):
                    def epi(ob, nq=None):
                        pass
                    return epi

                def make_epi(dst_fn, eo, nq, rcp, wkey="yT"):
                    def epi(ob):
                        o0, d0 = (0, 64) if eo == 0 else (64, 0)
                        if nq == 256:
                            act(rcp[d0:d0 + 64, 0:nq], psum[ob][d0:d0 + 64, 0:nq], AF.Ln, r=[PK(ob)], w=["rcp"])
                            act(rcp[d0:d0 + 64, 0:nq], rcp[d0:d0 + 64, 0:nq], AF.Exp, r=["rcp"], w=["rcp"], scale=-1.0)
                        else:
                            recip(rcp[d0:d0 + 64, 0:nq], psum[ob][d0:d0 + 64, 0:nq], r=[PK(ob)], w=["rcp"])
                        tt("dve", dst_fn(o0, o0 + 64), psum[ob][o0:o0 + 64, 0:nq], rcp[d0:d0 + 64, 0:nq], ALU.mult,
                           r=[PK(ob), "rcp"], w=[wkey])
                    return epi

                def diff_epilogue(Omaps, q0, nq, dtmp):
                    y, ysq, rs = dtmp
                    for pp in range(2):
                        stt("dve", y[:, 0:nq], Omaps[:, 2 * pp + 1, 0:nq], lam[:, l * 4:l * 4 + 1], Omaps[:, 2 * pp, 0:nq],
                            ALU.mult, ALU.add, r=["Om", "lam"], w=["dy"])
                        act(ysq[:, 0:nq], y[:, 0:nq], AF.Square, r=["dy"], w=["dysq"])
                        pb = ps_get()
                        mm(psum[pb][:, 0:nq], onesbd, ysq[:, 0:nq], start=True, stop=True,
                           r=["dysq", "onesbd"], w=[PK(pb)])
                        act(rs[:, 0:nq], psum[pb][:, 0:nq], AF.Ln, r=[PK(pb), "eps"], w=["drs"],
                            bias=epsc[:, 0:1], scale=1.0 / 64)
                        act(rs[:, 0:nq], rs[:, 0:nq], AF.Exp, r=["drs"], w=["drs"], scale=-0.5)
                        stt("dve", y[:, 0:nq], y[:, 0:nq], gsub[:, l:l + 1], rs[:, 0:nq], ALU.mult, ALU.mult,
                            r=["dy", "drs", "gsub"], w=["dy"])
                        act(yT[:, pp, q0:q0 + nq], y[:, 0:nq], AF.Identity, r=["dy"], w=["yT"], scale=1.0 - lam_init)

                def fnet_stage2(PQ, q0, nq, scl):
                    for hc in range(2):
                        pb = ps_get()
                        mm(psum[pb][:, 0:nq], c64, PQ[:, 0, hc, 0:nq], start=True, stop=False, r=["PQ", "c64"], w=[PK(pb)])
                        mm(psum[pb][:, 0:nq], s64n, PQ[:, 1, hc, 0:nq], start=False, stop=True, r=["PQ", "c64"], w=[PK(pb)])
                        act(yT[:, 2 + hc, q0:q0 + nq], psum[pb][:, 0:nq], AF.Identity, r=[PK(pb)], w=["yT"], scale=scl)

                def outproj(t0):
                    v = 1 if t0 >= TP else 0
                    m1 = A.mark()
                    wo = A.alloc([128, 8, 1024], BF16)
                    B.dma("pool", wo, D["wout"][l].rearrange("p (h o) -> p h o", o=1024), w=["wo"])
                    for tq in range(2):
                        for o in range(8):
                            pb = ps_get()
                            osl = slice(o * 128, (o + 1) * 128)
                            qsl = slice(tq * 512, (tq + 1) * 512)
                            for c in range(8):
                                mm(psum[pb][:, :], wo[:, c, osl], yT[:, c, qsl], start=(c == 0), stop=(c == 7),
                                   r=["wo", "yT"], w=[PK(pb)])
                            xs = xT[:, o, t0 + tq * 512:t0 + (tq + 1) * 512]
                            stt("dve", xs, psum[pb][:, :], mods_ap(l, 2, o, v), xs, ALU.mult, ALU.add,
                                r=[PK(pb), "mods", "xT"], w=["xT"])
                    B.barrier()
                    A.release(m1)

                if not DEBUG.get("skipS"):
                    mX = A.mark()
                    XS = A.alloc([128, XW], BF16)
                    inproj("S")
                    ck(2)
                    for j in range(3):
                        B.dma("sp", xin[l][j].ap(), XS[:, j * 4096:j * 4096 + SEGW[j]], r=["XS"], w=[("xin", j)])
                    B.barrier()
                    A.release(mX)
                    for j in range(3):
                        if not DEBUG.get("nocc"):
                            B.op("pool", lambda e, l=l, j=j: e.collective_compute(
                                "AllGather", ALU.bypass, replica_groups=[[0, 1, 2, 3], [4, 5, 6, 7]],
                                ins=[xin[l][j].ap().opt()], outs=[xout[l][j].ap().opt()]),
                                r=[("xin", j)], w=[("xout", j)], cc=True)
                XOUT = [("xout", 0), ("xout", 1), ("xout", 2)]
                xov = [xout[l][j].ap().rearrange("(r p) w -> p r w", p=128) for j in range(3)]

                def xo_piece(r_, off, n):
                    j = off // 4096
                    return xov[j][:, r_, off - j * 4096:off - j * 4096 + n]

                mP = A.mark()
                QdP = A.alloc([128, 2, TP], BF16)
                QnP = A.alloc([128, 2, TP], BF16)
                QrP = A.alloc([128, TP], BF16)
                QgP = A.alloc([128, 2, TP], BF16)
                KdP = A.alloc([128, 2, TP], BF16)
                KnP = A.alloc([128, 2, TP], BF16)
                KrP = A.alloc([128, TP], BF16)
                KgP = A.alloc([128, TP], BF16)
                CkP = A.alloc([128, TP], BF16)
                VdP = A.alloc([128, 8, 384], BF16)
                VmP = A.alloc([128, 8, 384], BF16)
                VgP = A.alloc([128, 8, 384], BF16)
                UP = A.alloc([128, 8, 256], BF16)
                for Vx, kx in ((VdP, "VdP"), (VmP, "VmP"), (VgP, "VgP")):
                    memset("pool", Vx.rearrange("p c (a b) -> p c a b", b=192)[:, :, :, 64:128], 1.0, w=[kx])
                inproj("P")
                ck(3)
                yT = A.alloc([128, 8, 1024], BF16)
                mPa = A.mark()
                ptb = [A.alloc([128, 1024], BF16) for _ in range(3)]
                rcp = A.alloc([128, 512], F32)
                Om = A.alloc([128, 4, 256], F32)
                dtmp = (A.alloc([128, 512], F32), A.alloc([128, 512], BF16), A.alloc([128, 512], F32))
                PQ = A.alloc([128, 2, 2, 512], BF16)

                for s in range(4):
                    q0 = s * 256
                    qs = slice(q0, q0 + 256)
                    jobs = []
                    for m in range(8):
                        g, pr = m // 4, (m % 4) * 32
                        jobs.append(dict(
                            nk=2, r=["QdP", "KdP", "VdP"],
                            qk=lambda kc, g=g, pr=pr, q0=q0: [(KdP[pr:pr + 32, g, q0 + kc * 128:q0 + (kc + 1) * 128],
                                                               QdP[pr:pr + 32, g, q0:q0 + 256])],
                            v=lambda kc, m=m, s=s: VdP[:, s * 2 + kc, (m // 4) * 192 + ((m // 2) % 2) * 64:(m // 4) * 192 + ((m // 2) % 2) * 64 + 128],
                            epi=make_epi(lambda lo, hi, m=m: Om[lo:hi, (m // 4) * 2 + m % 2, :], (m // 2) % 2, 256, rcp, "Om")))
                    attention2([(jobs[2 * i], jobs[2 * i + 1]) for i in range(4)], 32 ** -0.5, ptb, 256)
                    diff_epilogue(Om, q0, 256, dtmp)
                    jobs = []
                    for h in range(4):
                        g, pr = h // 2, (h % 2) * 64
                        jobs.append(dict(
                            nk=2, r=["QnP", "KnP", "QrP", "KrP", "VmP"],
                            qk=lambda kc, g=g, pr=pr, h=h, q0=q0: [
                                (KnP[pr:pr + 64, g, q0 + kc * 128:q0 + (kc + 1) * 128], QnP[pr:pr + 64, g, q0:q0 + 256]),
                                (KrP[32 * h:32 * h + 32, q0 + kc * 128:q0 + (kc + 1) * 128], QrP[32 * h:32 * h + 32, q0:q0 + 256])],
                            v=lambda kc, h=h, s=s: VmP[:, s * 2 + kc, (h // 2) * 192 + (h % 2) * 64:(h // 2) * 192 + (h % 2) * 64 + 128],
                            epi=make_epi(lambda lo, hi, h=h, qs=qs: yT[lo:hi, 4 + h // 2, qs], h % 2, 256, rcp)))
                    attention2([(jobs[0], jobs[1]), (jobs[2], jobs[3])], 96 ** -0.5, ptb, 256)
                    jobs = []
                    for h in range(4):
                        kv, ab = h // 2, h % 2
                        jobs.append(dict(
                            nk=2, r=["QgP", "KgP", "VgP"],
                            qk=lambda kc, kv=kv, ab=ab, q0=q0: [(KgP[64 * kv:64 * kv + 64, q0 + kc * 128:q0 + (kc + 1) * 128],
                                                                 QgP[64 * kv:64 * kv + 64, ab, q0:q0 + 256])],
                            v=lambda kc, kv=kv, ab=ab, s=s: VgP[:, s * 2 + kc, kv * 192 + ab * 64:kv * 192 + ab * 64 + 128],
                            epi=make_epi(lambda lo, hi, h=h, qs=qs: yT[lo:hi, 6 + h // 2, qs], h % 2, 256, rcp)))
                    attention2([(jobs[0], jobs[2]), (jobs[1], jobs[3])], 64 ** -0.5, ptb, 256)
                    for tab, tabt in enumerate((c256, s256)):
                        for hc in range(2):
                            pb = ps_get()
                            for sc in range(2):
                                mm(psum[pb][:, 0:256], UP[:, s * 2 + sc, hc * 128:(hc + 1) * 128], tabt[:, sc, :],
                                   start=(sc == 0), stop=(sc == 1), r=["UP", "c256"], w=[PK(pb)])
                            cp("dve", PQ[:, tab, hc, 0:256], psum[pb][:, 0:256], r=[PK(pb)], w=["PQ"])
                    fnet_stage2(PQ, q0, 256, (256 * 64) ** -0.5)
                B.barrier()
                A.release(mPa)
                ck(4)
                outproj(0)
                ck(5)
                A.release(mP)

                if not DEBUG.get("skipS"):
                    mS = A.mark()
                    yT = A.alloc([128, 8, 1024], BF16)
                    mSa = A.mark()
                    ptb = [A.alloc([128, 1024], BF16) for _ in range(3)]
                    rcp = A.alloc([128, 512], F32)
                    m1 = A.mark()
                    Ug = A.alloc([128, 32, 256], BF16)
                    PQ = A.alloc([128, 2, 2, 512], BF16)
                    tabb = [A.alloc([128, 2, 8, 512], BF16) for _ in range(2)]
                    for r_ in range(4):
                        B.dma("sp", Ug[:, r_ * 8:(r_ + 1) * 8, :],
                              xo_piece(r_, X_U, 2048).rearrange("p (c n) -> p c n", n=256), r=XOUT, w=["Ug"])
                    cbv = D["cbig"].rearrange("p (t s k) -> p t s k", t=2, k=512)
                    sbv = D["sbig"].rearrange("p (t s k) -> p t s k", t=2, k=512)
                    ld = 0
                    for kt in range(2):
                        banks = [ps_get(hold=True) for _ in range(4)]
                        for s8 in range(4):
                            tb = tabb[ld % 2]
                            key = ("tabb", ld % 2)
                            ld += 1
                            B.dma("sp", tb[:, 0, :, :], cbv[:, kt, s8 * 8:(s8 + 1) * 8, :], w=[key])
                            B.dma("sp", tb[:, 1, :, :], sbv[:, kt, s8 * 8:(s8 + 1) * 8, :], w=[key])
                            for si in range(8):
                                sc = s8 * 8 + si
                                for tab in range(2):
                                    for hc in range(2):
                                        b = banks[tab * 2 + hc]
                                        mm(psum[b][:, :], Ug[:, sc, hc * 128:(hc + 1) * 128], tb[:, tab, si, :],
                                           start=(sc == 0), stop=(sc == 31), r=["Ug", key], w=[PK(b)])
                        for tab in range(2):
                            for hc in range(2):
                                b = banks[tab * 2 + hc]
                                cp("dve" if hc else "act", PQ[:, tab, hc, :], psum[b][:, :], r=[PK(b)], w=["PQ"])
                                ps_rel(b)
                        fnet_stage2(PQ, kt * 512, 512, (4096 * 64) ** -0.5)
                    B.barrier()
                    A.release(m1)

                    ck(6)
                    m1 = A.mark()
                    Kd = A.alloc([128, 2, 4608], BF16)
                    Vd = A.alloc([128, 36, 384], BF16)
                    Om = A.alloc([128, 4, 512], F32)
                    dtmp = (A.alloc([128, 512], F32), A.alloc([128, 512], BF16), A.alloc([128, 512], F32))
                    memset("pool", Vd.rearrange("p c (a b) -> p c a b", b=192)[:, :, :, 64:128], 1.0, w=["Vdones"])

                    def vload(q, Vx, c0, nchunk, src, key, rkeys):
                        sv = src.rearrange("p (c a e n) -> p c a e n", a=2, e=2, n=64)
                        dv = Vx[:, c0:c0 + nchunk, :].rearrange("p c (a b) -> p c a b", b=192)
                        for a in range(2):
                            B.dma(q, dv[:, :, a, 0:64], sv[:, :, a, 0, :], r=rkeys, w=[key])
                            B.dma(q, dv[:, :, a, 128:192], sv[:, :, a, 1, :], r=rkeys, w=[key])
                    for r_ in range(4):
                        B.dma("sp", Kd[:, :, r_ * 1024:(r_ + 1) * 1024],
                              xo_piece(r_, X_KD, 2048).rearrange("p (g n) -> p g n", n=1024), r=XOUT, w=[("Kd", r_)])
                        vload("sp", Vd, r_ * 8, 8, xo_piece(r_, X_VD, 2048), ("Vd", r_), XOUT)
                    B.dma("pool", Kd[:, :, 4096:4608], D["ckdT"][l].rearrange("p (g n) -> p g n", n=512), w=[("Kd", 4)])
                    vload("pool", Vd, 32, 4, D["cvd"][l], ("Vd", 4), [])
                    QB = [A.alloc([128, 2, 512], BF16) for _ in range(4)]
                    for b_ in range(4):
                        memset("pool", QB[b_], 0.0, w=[("QB", b_)])
                    for tq in range(2):
                        q0 = tq * 512
                        jobs = []
                        for m in range(8):
                            g, pr = m // 4, (m % 4) * 32
                            jobs.append(dict(
                                nk=36, r=[("QB", m % 4), "Vdones"],
                                rk=lambda kc: [('Kd', kc // 8 if kc < 32 else 4), ('Vd', kc // 8 if kc < 32 else 4)],
                                pre=lambda g=g, pr=pr, q0=q0, b_=m % 4: cp("pool", QB[b_][pr:pr + 32, :, :],
                                                                           QdS[pr:pr + 32, :, g, q0:q0 + 512],
                                                                           r=["QdS"], w=[("QB", b_)]),
                                qk=lambda kc, g=g, b_=m % 4: [(Kd[:, g, kc * 128:(kc + 1) * 128],
                                                               QB[b_][:, 0 if kc < 32 else 1, :])],
                                v=lambda kc, m=m: Vd[:, kc, (m // 4) * 192 + ((m // 2) % 2) * 64:(m // 4) * 192 + ((m // 2) % 2) * 64 + 128],
                                epi=make_epi(lambda lo, hi, m=m: Om[lo:hi, (m // 4) * 2 + m % 2, :], (m // 2) % 2, 512, rcp, "Om")))
                        attention2([(jobs[2 * i], jobs[2 * i + 1]) for i in range(4)], 32 ** -0.5, ptb, 512)
                        diff_epilogue(Om, q0, 512, dtmp)
                    B.barrier()
                    A.release(m1)

                    ck(7)
                    m1 = A.mark()
                    Ck = A.alloc([128, 4608], BF16)
                    for r_ in range(4):
                        B.dma("sp", Ck[:, r_ * 1024:(r_ + 1) * 1024], xo_piece(r_, X_CKV, 1024), r=XOUT, w=[("Ck", r_)])
                    B.dma("pool", Ck[:, 4096:4608], D["cckvT"][l], w=[("Ck", 4)])
                    for hp in range(2):
                        m2 = A.mark()
                        KK = [A.alloc([128, 4608], BF16) for _ in range(2)]
                        Vm = A.alloc([128, 36, 192], BF16)
                        QMB = [A.alloc([128, 2, 512], BF16) for _ in range(4)]
                        memset("pool", Vm[:, :, 64:128], 1.0, w=["Vmones"])
                        for hh in range(2):
                            h = hp * 2 + hh
                            memset("pool", KK[hh][96:128, :], 0.0, w=[("KK", hh, "z")])
                            for tq_ in range(2):
                                memset("pool", QMB[tq_ * 2 + hh][96:128, :, :], 0.0, w=[("QMB", tq_ * 2 + hh)])
                            for r_ in range(4):
                                B.dma("sp", KK[hh][64:96, r_ * 1024:(r_ + 1) * 1024], xo_piece(r_, X_KR, 1024)[64:96, :],
                                      r=XOUT, w=[("KK", hh, "r", r_)])
                            B.dma("pool", KK[hh][64:96, 4096:4608], D["ckrT"][l][64:96, :], w=[("KK", hh, "r", 4)])
                            for kt in range(9):
                                pb = ps_get()
                                mm(psum[pb][0:64, :], wuk[:, l, h * 64:(h + 1) * 64], Ck[:, kt * 512:(kt + 1) * 512],
                                   start=True, stop=True, r=[("Ck", kt // 2), "wuk"], w=[PK(pb)])
                                cp("dve" if kt % 2 else "act", KK[hh][0:64, kt * 512:(kt + 1) * 512], psum[pb][0:64, :],
                                   r=[PK(pb)], w=[("KK", hh, "n", kt)])
                        for kc in range(36):
                            pb = ps_get()
                            mm(psum[pb][:, 0:128], Ck[:, kc * 128:(kc + 1) * 128], wuv[:, l, hp * 128:(hp + 1) * 128], start=True, stop=True,
                               r=[("Ck", kc // 8 if kc < 32 else 4), "wuv"], w=[PK(pb)])
                            cp("dve", Vm[:, kc, :].rearrange("p (a b) -> p a b", b=64)[:, 0:3:2, :],
                               psum[pb][:, 0:128].rearrange("p (h c) -> p h c", c=64), r=[PK(pb)], w=[("Vm", kc)])
                        mpairs = []
                        for tq in range(2):
                            q0 = tq * 512
                            jobs = []
                            for hh in range(2):
                                h = hp * 2 + hh
                                pr = hh * 64
                                qi = tq * 2 + hh

                                def pre(qi=qi, h=h, pr=pr, hp=hp, q0=q0):
                                    for ver in range(2):
                                        cp("dve", QMB[qi][0:64, ver, :], QnS[pr:pr + 64, hp, q0:q0 + 512], r=["QnS"], w=[("QMB", qi)])
                                        cp("dve", QMB[qi][64:96, ver, :], QrS[32 * h:32 * h + 32, ver, q0:q0 + 512],
                                           r=["QrS"], w=[("QMB", qi)])
                                jobs.append(dict(
                                    nk=36, r=[("QMB", qi), ("KK", hh, "z"), "Vmones"], pre=pre,
                                    rk=lambda kc, hh=hh: [("KK", hh, "r", kc // 8 if kc < 32 else 4), ("KK", hh, "n", kc // 4),
                                                          ("Vm", kc)],
                                    qk=lambda kc, hh=hh, qi=qi: [(KK[hh][:, kc * 128:(kc + 1) * 128], QMB[qi][:, 0 if kc < 32 else 1, :])],
                                    v=lambda kc, hh=hh: Vm[:, kc, hh * 64:hh * 64 + 128],
                                    epi=make_epi(lambda lo, hi, hp=hp, q0=q0: yT[lo:hi, 4 + hp, q0:q0 + 512], hh, 512, rcp)))
                            mpairs.append((jobs[0], jobs[1]))
                        attention2(mpairs, 96 ** -0.5, ptb, 512)
                        B.barrier()
                        A.release(m2)
                    A.release(m1)

                    ck(8)
                    m1 = A.mark()
                    Kg = A.alloc([128, 4608], BF16)
                    Vg = A.alloc([128, 36, 384], BF16)
                    memset("pool", Vg.rearrange("p c (a b) -> p c a b", b=192)[:, :, :, 64:128], 1.0, w=["Vgones"])

                    def vloadg(q, c0, nchunk, src, rkeys):
                        sv = src.rearrange("p (c a n) -> p c a n", a=2, n=64)
                        dv = Vg[:, c0:c0 + nchunk, :].rearrange("p c (a b) -> p c a b", b=192)
                        B.dma(q, dv[:, :, :, 0:64], sv, r=rkeys, w=[("Vg", c0 // 8)])
                        B.dma(q, dv[:, :, :, 128:192], sv, r=rkeys, w=[("Vg", c0 // 8)])
                    for r_ in range(4):
                        B.dma("sp", Kg[:, r_ * 1024:(r_ + 1) * 1024], xo_piece(r_, X_KG, 1024), r=XOUT, w=[("Kg", r_)])
                        vloadg("sp", r_ * 8, 8, xo_piece(r_, X_VG, 1024), XOUT)
                    B.dma("pool", Kg[:, 4096:4608], D["ckgT"][l], w=[("Kg", 4)])
                    vloadg("pool", 32, 4, D["cvg"][l], [])
                    for tq in range(2):
                        q0 = tq * 512
                        jobs = []
                        for h in range(4):
                            kv, ab = h // 2, h % 2
                            jobs.append(dict(
                                nk=36, r=["QgS", "Vgones"],
                                rk=lambda kc: [('Kg', kc // 8 if kc < 32 else 4), ('Vg', kc // 8 if kc < 32 else 4)],
                                qk=lambda kc, kv=kv, ab=ab, q0=q0: [(Kg[64 * kv:64 * kv + 64, kc * 128:(kc + 1) * 128],
                                                                     QgS[64 * kv:64 * kv + 64, 0 if kc < 32 else 1, ab, q0:q0 + 512])],
                                v=lambda kc, kv=kv, ab=ab: Vg[:, kc, kv * 192 + ab * 64:kv * 192 + ab * 64 + 128],
                                epi=make_epi(lambda lo, hi, h=h, q0=q0: yT[lo:hi, 6 + h // 2, q0:q0 + 512], h % 2, 512, rcp)))
                        attention2([(jobs[0], jobs[2]), (jobs[1], jobs[3])], 64 ** -0.5, ptb, 512)
                    B.barrier()
                    A.release(mSa)
                    ck(9)
                    outproj(TP)
                    ck(10)
                A.release(mL)

                mM = A.mark()
                hT2 = A.alloc([128, 8, T], BF16)
                m1 = A.mark()
                sq = [A.alloc([128, 8, 512], BF16) for _ in range(2)]
                ntmp = [A.alloc([128, 2, 512], F32) for _ in range(2)]
                rstd = [A.alloc([128, 512], F32) for _ in range(2)]
                for grp_t0, v in ((0, 0), (TP, 1)):
                    fm_norm(grp_t0, 1024, lambda k, o, g0=grp_t0: hT2[:, k, g0 + o:g0 + o + 512],
                            lambda k, v=v: modA_ap(l, v, 1, k), lambda k, v=v: mods_ap(l, 3, k, v), ntmp, sq, rstd)
                B.barrier()
                A.release(m1)
                hid = A.alloc([128, 4, T], BF16)
                rl = [A.alloc([128, 512], F32) for _ in range(2)]
                w1b = [A.alloc([128, 8, 512], BF16) for _ in range(2)]
                w2b = [A.alloc([128, 4, 1024], BF16) for _ in range(2)]
                adw = [A.alloc([128, 8, 1024], BF16) for _ in range(2)] if l + 1 < NL else None
                w1v = D["w1"][l].rearrange("p (k c) -> p k c", c=4096)
                w2v = D["w2"][l].rearrange("p (j o) -> p j o", o=1024)
                for e8 in range(8):
                    wb1, wb2 = w1b[e8 % 2], w2b[e8 % 2]
                    k1, k2 = ("w1b", e8 % 2), ("w2b", e8 % 2)
                    B.dma("pool", wb1, w1v[:, :, e8 * 512:(e8 + 1) * 512], w=[k1])
                    B.dma("pool", wb2, w2v[:, e8 * 4:(e8 + 1) * 4, :], w=[k2])
                    if adw is not None and e8 < 6:
                        mods_load(l + 1, e8, adw[e8 % 2], ("adw", e8 % 2))
                    for tq in range(4):
                        qsl = slice(tq * 512, (tq + 1) * 512)
                        for jj in range(4):
                            pb = ps_get()
                            for k in range(8):
                                mm(psum[pb][:, :], wb1[:, k, jj * 128:(jj + 1) * 128], hT2[:, k, qsl], start=(k == 0), stop=(k == 7),
                                   r=[k1, ("hT", k)], w=[PK(pb)])
                            rk = ("rl", (tq * 4 + jj) % 2)
                            act(rl[(tq * 4 + jj) % 2], psum[pb][:, :], AF.Relu, r=[PK(pb)], w=[rk])
                            tt("pool", hid[:, jj, qsl], rl[(tq * 4 + jj) % 2], rl[(tq * 4 + jj) % 2], ALU.mult, r=[rk],
                               w=[("hid", jj, tq)])
                    for tq in range(4):
                        qsl = slice(tq * 512, (tq + 1) * 512)
                        v = 0 if tq < 2 else 1
                        for o in range(8):
                            pb = ps_get()
                            for jj in range(4):
                                mm(psum[pb][:, :], wb2[:, jj, o * 128:(o + 1) * 128], hid[:, jj, qsl], start=(jj == 0), stop=(jj == 3),
                                   r=[k2, ("hid", jj, tq)], w=[PK(pb)])
                            stt("dve", xT[:, o, qsl], psum[pb][:, :], mods_ap(l, 5, o, v), xT[:, o, qsl], ALU.mult, ALU.add,
                                r=[PK(pb), "mods", "xT"], w=["xT"])
                    if adw is not None and e8 < 6:
                        mods_piece(l + 1, e8, adw[e8 % 2], ("adw", e8 % 2))
                if adw is not None:
                    mods_finish(l + 1)
                B.barrier()
                A.release(mM)
                ck(11)

            mF = A.mark()
            sq = [A.alloc([128, 8, 512], BF16) for _ in range(2)]
            ntmp = [A.alloc([128, 2, 512], F32) for _ in range(2)]
            rstd = [A.alloc([128, 512], F32) for _ in range(2)]
            yo = A.alloc([128, 8, 1024], F32)
            yTv = D["yT"].rearrange("p (c t) -> p c t", t=T)
            for g0 in (0, TP):
                fm_norm(g0, 1024, lambda k, o: yo[:, k, o:o + 512], lambda k: gains[:, 32 + k:33 + k], lambda k: None,
                        ntmp, sq, rstd)
                B.dma("sp", yTv[:, :, g0:g0 + 1024], yo, r=[("hT", k) for k in range(8)], w=["yout"])

        except StopBuild:
            pass
        print("arena peak words", A.peak, "of", NW, {e: len(B.ops[e]) for e in B.ENGS})
        block = es.enter_context(nc.Block())
        B.emit(block, sems, dsems)
    return nc


def _prep(inp):
    f32 = np.float32
    bf = ml_dtypes.bfloat16

    def fm(w, k):
        C = w.shape[1]
        return np.ascontiguousarray(w.reshape(k, 128, C).transpose(1, 0, 2).reshape(128, k * C))

    shared = {}
    shared["adaw"] = np.stack([fm(inp["ada_w"][l], 8) for l in range(NL)])
    shared["adab"] = np.stack([np.ascontiguousarray(inp["ada_b"][l].reshape(48, 128).T) for l in range(NL)])
    shared["gmix"] = np.stack([np.ascontiguousarray(inp["norm_mix_g"][l].reshape(8, 128).T) for l in range(NL)])
    shared["gmlp"] = np.stack([np.ascontiguousarray(inp["norm_mlp_g"][l].reshape(8, 128).T) for l in range(NL)])
    shared["gfin"] = np.ascontiguousarray(inp["final_norm_g"].reshape(8, 128).T)
    shared["win"] = np.stack([fm(inp["w_in"][l], 8) for l in range(NL)])
    wuq = np.zeros((NL, 128, 2, 384), f32)
    for l in range(NL):
        wuq[l, :, 0, :] = inp["mla_w_uq"][l][0:128]
        wuq[l, 0:64, 1, :] = inp["mla_w_uq"][l][128:192]
    shared["wuq"] = wuq.reshape(NL, 128, 768)
    ukv = inp["mla_w_ukv"].reshape(NL, 128, 4, 128)
    shared["wuk"] = np.ascontiguousarray(ukv[:, :, :, 0:64].reshape(NL, 128, 256))
    shared["wuv"] = np.ascontiguousarray(ukv[:, :, :, 64:128].reshape(NL, 128, 256))
    shared["wout"] = np.stack([fm(inp["w_out"][l], 8) for l in range(NL)])
    shared["w1"] = np.stack([fm(inp["mlp_w1"][l], 8) for l in range(NL)])
    shared["w2"] = np.stack([fm(inp["mlp_w2"][l], 32) for l in range(NL)])
    gt = np.concatenate([inp["mla_q_norm_g"], inp["mla_kv_norm_g"], inp["gqa_q_norm_g"], inp["gqa_k_norm_g"]], axis=1)
    shared["gt"] = np.ascontiguousarray(np.broadcast_to(gt[:, None, :], (NL, 128, 448))).astype(f32)
    shared["gsub"] = np.ascontiguousarray(inp["diff_subln_g"].reshape(NL, 64, 1))
    shared["lamp"] = np.ascontiguousarray(np.broadcast_to(inp["diff_lambda"].reshape(NL, 1, 128), (NL, 128, 128))).astype(f32)
    shared["ident"] = np.eye(128, dtype=f32)
    s = np.arange(256, dtype=np.float64)
    ang = 2 * np.pi * np.outer(s, s) / 256
    shared["c256"] = fm(np.cos(ang), 2).astype(bf)
    shared["s256"] = fm(np.sin(ang), 2).astype(bf)
    c = np.arange(64, dtype=np.float64)
    a64 = 2 * np.pi * np.outer(c, c) / 64
    c64 = np.zeros((128, 128)); s64 = np.zeros((128, 128))
    for g in range(2):
        c64[g * 64:(g + 1) * 64, g * 64:(g + 1) * 64] = np.cos(a64)
        s64[g * 64:(g + 1) * 64, g * 64:(g + 1) * 64] = -np.sin(a64)
    shared["c64"] = c64.astype(bf)
    shared["s64n"] = s64.astype(bf)

    def rope_tabs(pos, d, H):
        a = d // 2
        inv = 10000.0 ** (-np.arange(0, a, 2, dtype=np.float64) / a)
        row = (pos // 64).astype(np.float64); col = (pos % 64).astype(np.float64)
        cr, sr = np.cos(row[:, None] * inv), np.sin(row[:, None] * inv)
        cc, sc = np.cos(col[:, None] * inv), np.sin(col[:, None] * inv)
        cs = np.stack([cr, cc], axis=1)
        sn = np.stack([sr, sc], axis=1)
        cs = np.broadcast_to(cs[:, None], (len(pos), H, 2, a // 2)).reshape(len(pos), -1)
        sn = np.broadcast_to(sn[:, None], (len(pos), H, 2, a // 2)).reshape(len(pos), -1)

        def lay(x):
            return np.ascontiguousarray(x.reshape(8, 128, 128).transpose(1, 0, 2).reshape(128, 1024)).astype(f32)
        return lay(cs), lay(sn)

    maps = []
    kk = np.arange(1024, dtype=np.float64)
    ss_ = np.arange(4096, dtype=np.float64)
    for i in range(8):
        b, r = i // 4, i % 4
        m = dict(shared)
        xp = inp["x_prompt"][4 * i:4 * i + 4].reshape(1024, 1024)
        xs = inp["x_sample"][b, r * 1024:(r + 1) * 1024]
        x = np.concatenate([xp, xs], axis=0)
        m["xT"] = np.ascontiguousarray(x.T.reshape(8, 128, T).transpose(1, 0, 2).reshape(128, 8 * T))
        cvv = np.stack([inp["c_ctx"], inp["c"][b]], axis=1)
        m["cv"] = np.ascontiguousarray(cvv.reshape(8, 128, 2).transpose(1, 0, 2).reshape(128, 16))
        pos = np.arange(r * 1024, (r + 1) * 1024)
        m["rc32"], m["rs32"] = rope_tabs(pos, 32, 8)
        m["rc64"], m["rs64"] = rope_tabs(pos, 64, 4)
        ang = 2 * np.pi * ((np.outer(ss_, kk + r * 1024)) % 4096) / 4096
        def lay_big(x):
            return np.ascontiguousarray(x.reshape(32, 128, 2, 512).transpose(1, 2, 0, 3).reshape(128, 32 * 1024)).astype(bf)
        m["cbig"] = lay_big(np.cos(ang))
        m["sbig"] = lay_big(np.sin(ang))
        m["ckdT"] = np.stack([np.ascontiguousarray(inp["cache_diff_k"][b, l].reshape(512, 2, 128).transpose(2, 1, 0).reshape(128, 1024)) for l in range(NL)])
        m["cvd"] = np.stack([fm(inp["cache_diff_v"][b, l].reshape(512, 256), 4) for l in range(NL)])
        m["cckvT"] = np.stack([np.ascontiguousarray(inp["cache_mla_ckv"][b, l].T) for l in range(NL)])
        m["ckrT"] = np.stack([np.ascontiguousarray(np.tile(inp["cache_mla_krope"][b, l].T, (4, 1))) for l in range(NL)])
        m["ckgT"] = np.stack([np.ascontiguousarray(inp["cache_gqa_k"][b, l].reshape(512, 128).T) for l in range(NL)])
        m["cvg"] = np.stack([fm(inp["cache_gqa_v"][b, l].reshape(512, 128), 4) for l in range(NL)])
        maps.append(m)
    return maps


_NC = None


def kernel(**inputs):
    global _NC
    inp = {k: np.asarray(v) for k, v in inputs.items()}
    maps = _prep(inp)
    if _NC is None:
        _NC = build_program()
    res = run_bass_kernel_spmd(_NC, maps, core_ids=list(range(8)))
    R = res.results
    y_prompt = np.zeros((32, 256, 1024), np.float32)
    y_sample = np.zeros((2, 4096, 1024), np.float32)
    outs = {k: np.zeros(s, np.float32) for k, s in (("o_dk", (32, NL, 256, 4, 64)), ("o_dv", (32, NL, 256, 4, 64)),
                                                     ("o_ckv", (32, NL, 256, 128)), ("o_kr", (32, NL, 256, 32)),
                                                     ("o_gk", (32, NL, 256, 2, 64)), ("o_gv", (32, NL, 256, 2, 64)))}
    for i in range(8):
        b, r = i // 4, i % 4
        yT = np.asarray(R[i]["yT"]).reshape(128, 8, T)
        y = yT.transpose(2, 1, 0).reshape(T, 1024)
        y_prompt[4 * i:4 * i + 4] = y[0:TP].reshape(4, 256, 1024)
        y_sample[b, r * 1024:(r + 1) * 1024] = y[TP:]
        for k in outs:
            outs[k][4 * i:4 * i + 4] = np.asarray(R[i][k]).reshape(outs[k][4 * i:4 * i + 4].shape)
    return (y_prompt, y_sample, outs["o_dk"], outs["o_dv"], outs["o_ckv"], outs["o_kr"], outs["o_gk"], outs["o_gv"])
```

```python
import contextlib
import math
import numpy as np
import ml_dtypes
import concourse.bass as bass
import concourse.mybir as mybir
from concourse.bass_utils import run_bass_kernel_spmd

F32 = mybir.dt.float32
BF16 = mybir.dt.bfloat16
AF = mybir.ActivationFunctionType
ALU = mybir.AluOpType

NL = 2
EPS = 1e-6
TP = 1024
TS = 1024
T = TP + TS
XW = 10240
X_KD, X_KG, X_CKV, X_KR, X_VD, X_VG, X_U = 0, 2048, 3072, 4096, 5120, 7168, 8192
SAME_ENG_SYNC = True
RAW_ONLY = True
DEBUG = {}
NDMASEM = 12


class Op:
    __slots__ = ("eng", "fn", "deps", "dma", "sem", "cnt", "need", "idx", "cc", "raw")

    def __init__(self, eng, fn, dma):
        self.eng, self.fn, self.dma = eng, fn, dma
        self.cc = False
        self.deps = set()
        self.sem = None
        self.cnt = 0
        self.need = False


class StopBuild(Exception):
    pass


def ck(n):
    if DEBUG.get('stop') == n:
        raise StopBuild()


class Builder:
    ENGS = ("pe", "act", "dve", "pool", "sp")

    def __init__(self, nc):
        self.nc = nc
        self.ops = {e: [] for e in self.ENGS}
        self.lastw = {}
        self.readers = {}
        self.pending = {e: set() for e in self.ENGS}
        self.dma_ring = {"sp": [None] * NDMASEM, "pool": [None] * NDMASEM}
        self.dma_rr = {"sp": 0, "pool": 0}
        self.dmas_since_bar = []

    def op(self, eng, fn, r=(), w=(), dma=False, slot=None, cc=False):
        o = Op(eng, fn, dma)
        o.cc = cc
        deps = set()
        if dma:
            i = self.dma_rr[eng]
            self.dma_rr[eng] = (i + 1) % NDMASEM
            prev = self.dma_ring[eng][i]
            if prev is not None:
                deps.add(prev)
            self.dma_ring[eng][i] = o
            o.sem = (eng, i)
        slot = o.sem if dma else (("cc", len(self.ops[eng])) if cc else eng)
        raw = set()
        for k in r:
            raw.update(self.lastw.get(k, {}).values())
        deps.update(raw)
        for k in w:
            deps.update(self.lastw.get(k, {}).values())
            deps.update(self.readers.get(k, {}).values())
        for k in r:
            self.readers.setdefault(k, {})[slot] = o
        for k in w:
            self.lastw.setdefault(k, {})[slot] = o
            self.readers[k] = {}
        deps.update(self.pending[eng])
        raw.update(self.pending[eng])
        self.pending[eng] = set()
        o.raw = raw
        deps.discard(o)
        o.deps = deps
        self.ops[eng].append(o)
        return o

    def dma(self, eng, out, in_, r=(), w=()):
        return self.op(eng, lambda e: e.dma_start(out=out, in_=in_), r=r, w=w, dma=True)

    def barrier(self):
        lasts = set()
        for e in self.ENGS:
            for o in reversed(self.ops[e]):
                if not o.dma and not o.cc:
                    lasts.add(o)
                    break
        for e in ("sp", "pool"):
            for o in self.dma_ring[e]:
                if o is not None:
                    lasts.add(o)
        for e in self.ENGS:
            self.pending[e] |= lasts

    def emit(self, block, sems, dsems):
        nc = self.nc
        for e in self.ENGS:
            for o in self.ops[e]:
                for d in o.deps:
                    if d.dma or d.cc:
                        continue
                    if d.eng == o.eng and (d.eng == "pe" or not SAME_ENG_SYNC or (RAW_ONLY and d not in o.raw)):
                        continue
                    d.need = True
        for e in self.ENGS:
            c = 0
            cd = {}
            for o in self.ops[e]:
                if o.dma:
                    cd[o.sem] = cd.get(o.sem, 0) + 16
                    o.cnt = cd[o.sem]
                elif o.cc:
                    cd["cc"] = cd.get("cc", 0) + 1
                    o.cnt = cd["cc"]
                elif o.need:
                    c += 1
                    o.cnt = c

        def run(engname, engobj):
            waited = {}
            for o in self.ops[engname]:
                for d in sorted(o.deps, key=lambda x: (x.eng, x.cnt)):
                    if d.dma:
                        key, s, v = d.sem, dsems[d.sem], d.cnt
                    elif d.cc:
                        key, s, v = "cc", dsems["cc"], d.cnt
                    else:
                        if d.eng == o.eng and (d.eng == "pe" or not SAME_ENG_SYNC or (RAW_ONLY and d not in o.raw)):
                            continue
                        key, s, v = d.eng, sems[d.eng], d.cnt
                    if waited.get(key, 0) >= v:
                        continue
                    waited[key] = v
                    engobj.wait_ge(s, v)
                ins = o.fn(engobj)
                if o.dma:
                    ins.then_inc(dsems[o.sem], 16)
                elif o.cc:
                    ins.then_inc(dsems["cc"], 1)
                elif o.need:
                    ins.then_inc(sems[o.eng], 1)

        @block.tensor
        def _(t):
            run("pe", t)

        @block.scalar
        def _(t):
            run("act", t)

        @block.vector
        def _(t):
            run("dve", t)

        @block.gpsimd
        def _(t):
            run("pool", t)

        @block.sync
        def _(t):
            run("sp", t)
            for q in ("sp", "pool"):
                for o in self.dma_ring[q]:
                    if o is not None:
                        t.wait_ge(dsems[o.sem], o.cnt)


class Arena:
    def __init__(self, tensor, nwords):
        self.t = tensor
        self.n = nwords
        self.top = 0
        self.peak = 0

    def alloc(self, shape, dt):
        free = 1
        for s in shape[1:]:
            free *= s
        words = free if dt == F32 else (free + 1) // 2
        words = (words + 7) // 8 * 8
        off = self.top
        self.top += words
        self.peak = max(self.peak, self.top)
        assert self.top <= self.n, f"arena overflow {self.top} > {self.n}"
        ap = self.t[0:shape[0], off:off + words]
        if dt != F32:
            ap = ap.bitcast(dt)
        ap = ap[:, 0:free]
        if len(shape) == 3:
            ap = ap.rearrange("p (a b) -> p a b", b=shape[2])
        elif len(shape) == 4:
            ap = ap.rearrange("p (a b c) -> p a b c", b=shape[2], c=shape[3])
        return ap

    def mark(self):
        return self.top

    def release(self, m):
        self.top = m


def build_program():
    nc = bass.Bass("TRN2", target_bir_lowering=False)
    D = {}

    def din(name, shape, dt=F32):
        D[name] = nc.dram_tensor(name, list(shape), dt, kind="ExternalInput").ap()
        return D[name]

    def dout(name, shape, dt=F32):
        D[name] = nc.dram_tensor(name, list(shape), dt, kind="ExternalOutput").ap()
        return D[name]

    din("xT", [128, 8 * T])
    din("cv", [128, 16])
    din("adaw", [NL, 128, 8 * 6144])
    din("adab", [NL, 128, 48])
    din("gmix", [NL, 128, 8])
    din("gmlp", [NL, 128, 8])
    din("gfin", [128, 8])
    din("win", [NL, 128, 8 * 1888])
    din("wuq", [NL, 128, 2 * 384])
    din("wuk", [NL, 128, 256])
    din("wuv", [NL, 128, 256])
    din("wout", [NL, 128, 8 * 1024])
    din("w1", [NL, 128, 8 * 4096])
    din("w2", [NL, 128, 32 * 1024])
    din("gt", [NL, 128, 448])
    din("gsub", [NL, 64, 1])
    din("lamp", [NL, 128, 128])
    for nm in ("rc32", "rs32", "rc64", "rs64"):
        din(nm, [128, 8 * 128])
    din("c256", [128, 2 * 256], BF16)
    din("s256", [128, 2 * 256], BF16)
    din("cbig", [128, 32 * 1024], BF16)
    din("sbig", [128, 32 * 1024], BF16)
    din("c64", [128, 128], BF16)
    din("s64n", [128, 128], BF16)
    din("ident", [128, 128])
    din("ckdT", [NL, 128, 2 * 512])
    din("cvd", [NL, 128, 4 * 256])
    din("cckvT", [NL, 128, 512])
    din("ckrT", [NL, 128, 512])
    din("ckgT", [NL, 128, 512])
    din("cvg", [NL, 128, 4 * 128])
    dout("yT", [128, 8 * T])
    dout("o_dk", [4, NL, 256, 256])
    dout("o_dv", [4, NL, 256, 256])
    dout("o_ckv", [4, NL, 256, 128])
    dout("o_kr", [4, NL, 256, 32])
    dout("o_gk", [4, NL, 256, 128])
    dout("o_gv", [4, NL, 256, 128])
    SEGW = (4096, 4096, 2048)
    xin = [[nc.dram_tensor(f"xin{l}_{j}", [128, SEGW[j]], BF16) for j in range(3)] for l in range(NL)]
    xout = [[nc.dram_tensor(f"xout{l}_{j}", [4 * 128, SEGW[j]], BF16) for j in range(3)] for l in range(NL)]

    es = contextlib.ExitStack()
    with es:
        NW = 52400
        arena_t = es.enter_context(nc.sbuf_tensor("arena", [128, NW], F32))
        A = Arena(arena_t, NW)
        psall = es.enter_context(nc.psum_tensor("psall", [128, 4096], F32))
        psum = [psall[:, i * 512:(i + 1) * 512] for i in range(8)]
        sems = {e: es.enter_context(nc.semaphore(f"s_{e}")) for e in Builder.ENGS}
        dsems = {(q, i): es.enter_context(nc.semaphore(f"d_{q}{i}")) for q in ("sp", "pool") for i in range(NDMASEM)}
        s_cc = es.enter_context(nc.semaphore("s_cc"))
        dsems["cc"] = s_cc
        B = Builder(nc)

        ps_state = {"rr": 0, "held": set(), "lo": 0}

        def ps_get(hold=False):
            for _ in range(8):
                i = ps_state["rr"]
                ps_state["rr"] = (i + 1) % 8
                if i < ps_state["lo"]:
                    continue
                if i not in ps_state["held"]:
                    if hold:
                        ps_state["held"].add(i)
                    return i
            raise RuntimeError("no psum bank")

        def ps_rel(i):
            ps_state["held"].discard(i)

        def PK(i):
            return ("ps", i)

        uid = [0]

        def U():
            uid[0] += 1
            return ("u", uid[0])

        def mm(out, lhsT, rhs, start, stop, r, w):
            try:
                bp = lhsT.base_partition()
            except AssertionError:
                bp = 96
            kw = {"tile_position": (96, 0)} if bp == 96 else {}
            return B.op("pe", lambda e: e.matmul(out, lhsT, rhs, start=start, stop=stop, **kw), r=r, w=w)

        def transpose(out, in_, ident, r, w):
            return B.op("pe", lambda e: e.transpose(out, in_, ident), r=r, w=w)

        def act(out, in_, func, r, w, bias=None, scale=None, accum_out=None):
            kw = {}
            if bias is not None:
                kw["bias"] = bias
            if scale is not None:
                kw["scale"] = scale
            if accum_out is not None:
                kw["accum_out"] = accum_out
            return B.op("act", lambda e: e.activation(out=out, in_=in_, func=func, **kw), r=r, w=w)

        def tt(eng, out, in0, in1, op, r, w):
            return B.op(eng, lambda e: e.tensor_tensor(out=out, in0=in0, in1=in1, op=op), r=r, w=w)

        def stt(eng, out, in0, scalar, in1, op0, op1, r, w):
            return B.op(eng, lambda e: e.scalar_tensor_tensor(out=out, in0=in0, scalar=scalar, in1=in1,
                                                              op0=op0, op1=op1), r=r, w=w)

        def ts(eng, out, in0, s1, s2, op0, op1, r, w):
            return B.op(eng, lambda e: e.tensor_scalar(out=out, in0=in0, scalar1=s1, scalar2=s2,
                                                       op0=op0, op1=op1), r=r, w=w)

        def cp(eng, out, in_, r, w):
            if eng == "act":
                return act(out, in_, AF.Copy, r, w)
            return B.op(eng, lambda e: e.tensor_copy(out=out, in_=in_), r=r, w=w)

        def recip(out, in_, r, w):
            return B.op("dve", lambda e: e.reciprocal(out=out, in_=in_), r=r, w=w)

        def memset(eng, ap, val, w):
            return B.op(eng, lambda e: e.memset(ap, val), w=w)

        xT = A.alloc([128, 8, T], F32)
        ident = A.alloc([128, 128], F32)
        ones_bf = A.alloc([128, 128], BF16)
        epsc = A.alloc([128, 1], F32)
        mods = A.alloc([128, NL * 48 * 2], F32)
        modA = A.alloc([128, NL * 2 * 2 * 8], F32)
        gains = A.alloc([128, NL * 2 * 8 + 8], F32)
        gtab = A.alloc([128, NL * 448], F32)
        gsub = A.alloc([128, NL], F32)
        onesbd = A.alloc([128, 128], BF16)
        lam = A.alloc([128, NL * 4], F32)
        c64 = A.alloc([128, 128], BF16)
        s64n = A.alloc([128, 128], BF16)
        c256 = A.alloc([128, 2, 256], BF16)
        s256 = A.alloc([128, 2, 256], BF16)
        wuq = A.alloc([128, NL * 2, 384], BF16)
        wuk = A.alloc([128, NL, 256], BF16)
        wuv = A.alloc([128, NL, 256], BF16)
        cvb = A.alloc([128, 8, 2], BF16)
        adab = A.alloc([128, NL, 48], F32)
        mrow = A.alloc([128, 1024], F32)

        def mods_ap(l, m, k, v):
            i = (l * 48 + m * 8 + k) * 2 + v
            return mods[:, i:i + 1]

        def modA_ap(l, v, which, k):
            i = ((l * 2 + v) * 2 + which) * 8 + k
            return modA[:, i:i + 1]

        try:
            B.dma("sp", xT, D["xT"].rearrange("p (c t) -> p c t", t=T), w=["xT"])
            B.dma("sp", ident, D["ident"], w=["ident"])
            memset("dve", ones_bf, 1.0, w=["ones"])
            memset("dve", epsc, EPS, w=["eps"])
            memset("dve", onesbd, 0.0, w=["onesbd"])
            memset("dve", onesbd[0:64, 0:64], 1.0, w=["onesbd"])
            memset("dve", onesbd[64:128, 64:128], 1.0, w=["onesbd"])
            B.dma("sp", gains[:, 0:16].rearrange("p (l k) -> p l k", k=8), D["gmix"].rearrange("l p k -> p l k"), w=["gains"])
            B.dma("sp", gains[:, 16:32].rearrange("p (l k) -> p l k", k=8), D["gmlp"].rearrange("l p k -> p l k"), w=["gains"])
            B.dma("sp", gains[:, 32:40], D["gfin"], w=["gains"])
            B.dma("sp", gtab.rearrange("p (l c) -> p l c", c=448), D["gt"].rearrange("l p c -> p l c"), w=["gtab"])
            for l_ in range(NL):
                B.dma("sp", gsub[0:64, l_:l_ + 1], D["gsub"][l_], w=["gsub"])
                B.dma("sp", gsub[64:128, l_:l_ + 1], D["gsub"][l_], w=["gsub"])
            B.dma("pool", c64, D["c64"], w=["c64"])
            B.dma("pool", s64n, D["s64n"], w=["c64"])
            B.dma("pool", c256, D["c256"].rearrange("p (a b) -> p a b", b=256), w=["c256"])
            B.dma("pool", s256, D["s256"].rearrange("p (a b) -> p a b", b=256), w=["c256"])
            for l_ in range(NL):
                B.dma("pool", wuq[:, l_ * 2:l_ * 2 + 2, :], D["wuq"][l_].rearrange("p (a b) -> p a b", b=384), w=["wuq"])
            B.dma("pool", wuk, D["wuk"].rearrange("l p c -> p l c"), w=["wuk"])
            B.dma("pool", wuv, D["wuv"].rearrange("l p c -> p l c"), w=["wuv"])

            m0 = A.mark()
            cvf = A.alloc([128, 16], F32)
            lamp = A.alloc([128, NL, 128], F32)
            lprod = A.alloc([128, NL, 64], F32)
            lsum = A.alloc([128, NL * 2], F32)
            adw = [A.alloc([128, 8, 1024], BF16) for _ in range(2)]
            B.dma("sp", cvf, D["cv"], w=["cvf"])
            B.dma("sp", adab, D["adab"].rearrange("l p j -> p l j"), w=["adab"])
            B.dma("sp", lamp, D["lamp"].rearrange("l p c -> p l c"), w=["lamp"])
            act(cvb.rearrange("p a b -> p (a b)"), cvf, AF.Silu, r=["cvf"], w=["cvb"])

            def mods_load(l, m, buf, key):
                B.dma("pool", buf, D["adaw"][l].rearrange("p (k c) -> p k c", c=6144)[:, :, m * 1024:(m + 1) * 1024],
                      w=[key])

            def mods_piece(l, m, buf, key):
                for hf in range(2):
                    pb = ps_get()
                    for k in range(8):
                        mm(psum[pb][0:2, :], cvb[:, k, :], buf[:, k, hf * 512:(hf + 1) * 512], start=(k == 0), stop=(k == 7),
                           r=[key, "cvb"], w=[PK(pb)])
                    cp("dve", mrow[0:2, hf * 512:(hf + 1) * 512], psum[pb][0:2, :], r=[PK(pb)], w=[("mrow", hf)])
                pb = ps_get()
                for j in range(8):
                    transpose(psum[pb][:, 2 * j:2 * j + 2], mrow[0:2, j * 128:(j + 1) * 128], ident[0:2, 0:2],
                              r=[("mrow", j // 4), "ident"], w=[PK(pb)])
                base = (l * 48 + m * 8) * 2
                for v in range(2):
                    dst = mods[:, base:base + 16].rearrange("p (j v) -> p j v", v=2)[:, :, v]
                    src = psum[pb][:, 0:16].rearrange("p (j v) -> p j v", v=2)[:, :, v]
                    tt("dve", dst, src, adab[:, l, m * 8:(m + 1) * 8], ALU.add, r=[PK(pb), "adab"], w=["mods"])

            def mods_finish(l):
                for v in range(2):
                    for which, (msc, goff) in enumerate(((1, 0), (4, 16))):
                        for k in range(8):
                            ts("dve", modA_ap(l, v, which, k), mods_ap(l, msc, k, v), 1.0,
                               gains[:, goff + l * 8 + k:goff + l * 8 + k + 1], ALU.add, ALU.mult,
                               r=["mods", "gains"], w=["modA"])

            for m in range(6):
                mods_load(0, m, adw[m % 2], ("adw", m % 2))
                mods_piece(0, m, adw[m % 2], ("adw", m % 2))
            mods_finish(0)
            for l in range(NL):
                tt("dve", lprod[:, l, :].rearrange("p (a b) -> p a b", b=32),
                   lamp[:, l, :].rearrange("p (a t b) -> p a t b", t=2, b=32)[:, :, 0, :],
                   lamp[:, l, :].rearrange("p (a t b) -> p a t b", t=2, b=32)[:, :, 1, :], ALU.mult,
                   r=["lamp"], w=["lprod"])
                for j in range(2):
                    B.op("dve", lambda e, l=l, j=j: e.reduce_sum(out=lsum[:, l * 2 + j:l * 2 + j + 1],
                                                                 in_=lprod[:, l, j * 32:(j + 1) * 32],
                                                                 axis=mybir.AxisListType.X), r=["lprod"], w=["lsum"])
                act(lsum[:, l * 2:l * 2 + 2], lsum[:, l * 2:l * 2 + 2], AF.Exp, r=["lsum"], w=["lsum"])
                lam_init = 0.8 - 0.6 * math.exp(-0.3 * l)
                tt("dve", lam[:, l * 4:l * 4 + 1], lsum[:, l * 2 + 1:l * 2 + 2], lsum[:, l * 2:l * 2 + 1], ALU.subtract,
                   r=["lsum"], w=["lam"])
                B.op("dve", lambda e, l=l, li=lam_init: e.tensor_scalar_add(out=lam[:, l * 4:l * 4 + 1], in0=lam[:, l * 4:l * 4 + 1],
                                                                      scalar1=-li), r=["lam"], w=["lam"])
            B.barrier()
            A.release(m0)
            ck(1)

            def fm_norm(t0, ntok, hT_dst, scale_ap_fn, bias_ap_fn, tmp, sq, rstd, out_f32=None):
                tmps = tmp if isinstance(tmp, list) else [tmp]
                sqs = sq if isinstance(sq, list) else [sq]
                rstds = rstd if isinstance(rstd, list) else [rstd]
                for it, tt0 in enumerate(range(t0, t0 + ntok, 512)):
                    p = it % len(sqs)
                    tmp_, sq_, rstd_ = tmps[p], sqs[p], rstds[p]
                    sl = slice(tt0, tt0 + 512)
                    pb = ps_get()
                    for k in range(8):
                        if k % 2 == 0:
                            act(sq_[:, k, :], xT[:, k, sl], AF.Square, r=["xT"], w=[("sq", p, k)])
                        else:
                            tt("dve", sq_[:, k, :], xT[:, k, sl], xT[:, k, sl], ALU.mult, r=["xT"], w=[("sq", p, k)])
                    for k in range(8):
                        mm(psum[pb][:, :], ones_bf, sq_[:, k, :], start=(k == 0), stop=(k == 7),
                           r=[("sq", p, k), "ones"], w=[PK(pb)])
                    act(rstd_, psum[pb][:, :], AF.Ln, r=[PK(pb), "eps"], w=[("rstd", p)], bias=epsc[:, 0:1], scale=1.0 / 1024)
                    act(rstd_, rstd_, AF.Exp, r=[("rstd", p)], w=[("rstd", p)], scale=-0.5)
                    for k in range(8):
                        tt("dve", tmp_[:, k % 2, :], xT[:, k, sl], rstd_, ALU.mult, r=["xT", ("rstd", p)], w=[("ntmp", p, k % 2)])
                        b = bias_ap_fn(k)
                        act(hT_dst(k, tt0 - t0), tmp_[:, k % 2, :], AF.Identity, r=[("ntmp", p, k % 2), "mods", "modA", "gains"],
                            w=[("hT", k)], scale=scale_ap_fn(k), **({"bias": b} if b is not None else {}))

            def attention2(pairs, scale, ptb, nq):
                ps_state["lo"] = 4
                merge = (nq == 512)
                pre_done = set()

                def run_pre(ix):
                    if ix < len(pairs) and ix not in pre_done:
                        pre_done.add(ix)
                        for job in pairs[ix]:
                            if "pre" in job:
                                job["pre"]()

                for pidx, pair in enumerate(pairs):
                    nk = pair[0]["nk"]
                    run_pre(pidx)
                    obs = [ps_get(hold=True), ps_get(hold=True)]
                    rr0 = pair[0]["r"] + pair[1]["r"]

                    def rkeys(kc):
                        out = list(rr0)
                        for job in pair:
                            if "rk" in job:
                                out += job["rk"](kc)
                        return out

                    def qk(kc):
                        sbase = (kc % 2) * 2
                        for i, job in enumerate(pair):
                            spec = job["qk"](kc)
                            for j, (kT, qT) in enumerate(spec):
                                mm(psum[sbase + i][:, 0:nq], kT, qT, start=(j == 0), stop=(j == len(spec) - 1),
                                   r=rkeys(kc), w=[PK(sbase + i)])

                    def pv_exp(kc):
                        sbase = (kc % 2) * 2
                        pi_ = kc % len(ptb)
                        pt = ptb[pi_]
                        if merge:
                            act(pt[:, 0:1024], psall[:, sbase * 512:(sbase + 2) * 512], AF.Exp,
                                r=[PK(sbase), PK(sbase + 1)], w=[("pt", pi_)], scale=scale)
                        else:
                            for i in range(2):
                                act(pt[:, i * 512:i * 512 + nq], psum[sbase + i][:, 0:nq], AF.Exp,
                                    r=[PK(sbase + i)], w=[("pt", pi_)], scale=scale)

                    def pv_mm(kc):
                        pi_ = kc % len(ptb)
                        pt = ptb[pi_]
                        for i, job in enumerate(pair):
                            mm(psum[obs[i]][:, 0:nq], job["v"](kc), pt[:, i * 512:i * 512 + nq], start=(kc == 0),
                               stop=(kc == nk - 1), r=rkeys(kc) + [("pt", pi_)], w=[PK(obs[i])])

                    qk(0)
                    if nk > 1:
                        qk(1)
                    run_pre(pidx + 1)
                    for kc in range(nk):
                        pv_exp(kc)
                        if kc + 2 < nk:
                            qk(kc + 2)
                        pv_mm(kc)
                    for i, job in enumerate(pair):
                        job["epi"](obs[i])
                        ps_rel(obs[i])
                ps_state["lo"] = 0

            for l in range(NL):
                lam_init = 0.8 - 0.6 * math.exp(-0.3 * l)
                neglam = lam[0:64, l * 4:l * 4 + 1]
                mL = A.mark()
                QdS = A.alloc([128, 2, 2, TS], BF16)
                QnS = A.alloc([128, 2, TS], BF16)
                QrS = A.alloc([128, 2, TS], BF16)
                QgS = A.alloc([128, 2, 2, TS], BF16)

                def drive(gens):
                    live = list(gens)
                    first = True
                    while live:
                        for g in list(live):
                            try:
                                next(g)
                                if first:
                                    next(g)
                                    next(g)
                                    first = False
                            except StopIteration:
                                live.remove(g)

                def inproj(grp):
                    t0 = TP if grp == "S" else 0
                    v = 1 if grp == "S" else 0
                    G = gtab[:, l * 448:(l + 1) * 448]
                    winv = D["win"][l].rearrange("p (k c) -> p k c", c=1888)
                    mI = A.mark()
                    hT = A.alloc([128, 8, 512], BF16)
                    win0 = A.alloc([128, 8, 512], BF16)
                    win1 = A.alloc([128, 8, 512], BF16)

                    def load_winA():
                        B.dma("pool", win0, winv[:, :, 0:512], w=["win0"])
                        B.dma("pool", win1, winv[:, :, 512:1024], w=["win1"])

                    def load_winB():
                        B.dma("pool", win0[:, :, 0:352], winv[:, :, 1024:1376], w=["win0"])
                        B.dma("pool", win1, winv[:, :, 1376:1888], w=["win1"])
                    load_winA()
                    for half in range(2):
                        m1 = A.mark()
                        sq = A.alloc([128, 8, 512], BF16)
                        ntmp = A.alloc([128, 2, 512], F32)
                        rstd = A.alloc([128, 512], F32)
                        fm_norm(t0 + half * 512, 512, lambda k, o: hT[:, k, :], lambda k: modA_ap(l, v, 0, k),
                                lambda k: mods_ap(l, 0, k, v), ntmp, sq, rstd)
                        B.barrier()
                        A.release(m1)
                        ck(20)

                        def tr(dst, src, ncols, rk, wk, eng="dve"):
                            pb = ps_get()
                            transpose(psum[pb][0:ncols, 0:128], src, ident, r=rk + ["ident"], w=[PK(pb)])
                            cp(eng, dst, psum[pb][0:ncols, 0:128], r=[PK(pb)], w=wk)

                        def rope(dst, src, nh, d, ci, si, rk, wk, rp, rtab, p):
                            q = d // 4
                            n = nh * d
                            sv = src.rearrange("p (a t q) -> p a t q", t=2, q=q)
                            dv = dst.rearrange("p (a t q) -> p a t q", t=2, q=q)
                            cs = rtab[:, ci, 0:n // 2].rearrange("p (a q) -> p a q", q=q)
                            sn = rtab[:, si, 0:n // 2].rearrange("p (a q) -> p a q", q=q)
                            t = [rp[:, i, 0:n // 2].rearrange("p (a q) -> p a q", q=q) for i in range(4)]
                            RK = rk + [("rope", p)]
                            tt("pool", t[0], sv[:, :, 0, :], cs, ALU.mult, r=RK, w=[("rp", p, 0)])
                            tt("pool", t[1], sv[:, :, 1, :], sn, ALU.mult, r=RK, w=[("rp", p, 1)])
                            tt("dve", t[2], sv[:, :, 0, :], sn, ALU.mult, r=RK, w=[("rp", p, 2)])
                            tt("dve", t[3], sv[:, :, 1, :], cs, ALU.mult, r=RK, w=[("rp", p, 3)])
                            tt("pool", dv[:, :, 0, :], t[0], t[1], ALU.subtract, r=[("rp", p, 0), ("rp", p, 1)], w=wk)
                            tt("dve", dv[:, :, 1, :], t[2], t[3], ALU.add, r=[("rp", p, 2), ("rp", p, 3)], w=wk)

                        def load_rope(ti, rtab, p):
                            for j, nm in enumerate(("rc32", "rs32", "rc64", "rs64")):
                                B.dma("sp", rtab[:, j, :], D[nm][:, ti * 128:(ti + 1) * 128], w=[("rope", p)])

                        mA = A.mark()
                        ztA = [A.alloc([128, 1024], F32) for _ in range(2)]
                        zrA = [A.alloc([128, 512], F32) for _ in range(2)] if grp == "S" else [None, None]
                        rpA = [A.alloc([128, 4, 128], F32) for _ in range(2)] if grp == "S" else [None, None]
                        rtA = [A.alloc([128, 4, 128], F32) for _ in range(2)] if grp == "S" else [None, None]
                        def projA(tl):
                            ti = half * 4 + tl
                            p = tl % 2
                            zt, zr, rp, rtab = ztA[p], zrA[p], rpA[p], rtA[p]
                            tsl = slice(ti * 128, (ti + 1) * 128)
                            for (c0, c1, eng, wb, wk) in ((0, 512, "act", win0, "win0"), (512, 1024, "dve", win1, "win1")):
                                pb = ps_get()
                                for k in range(8):
                                    mm(psum[pb][:, 0:c1 - c0], hT[:, k, tl * 128:(tl + 1) * 128], wb[:, k, 0:c1 - c0],
                                       start=(k == 0), stop=(k == 7), r=[("hT", k), wk], w=[PK(pb)])
                                cp(eng, zt[:, c0:c1], psum[pb][:, 0:c1 - c0], r=[PK(pb)], w=[("zt", p, c0)])

                        def postA(tl):
                            ti = half * 4 + tl
                            p = tl % 2
                            zt, zr, rp, rtab = ztA[p], zrA[p], rpA[p], rtA[p]
                            tsl = slice(ti * 128, (ti + 1) * 128)
                            ZA, ZB = [("zt", p, 0)], [("zt", p, 512)]
                            if grp == "S":
                                load_rope(ti, rtab, p)
                                rope(zr[:, 0:256], zt[:, 0:256], 8, 32, 0, 1, ZA, [("zr", p, 0)], rp, rtab, p)
                                yield
                                rope(zr[:, 256:512], zt[:, 256:512], 8, 32, 0, 1, ZA, [("zr", p, 1)], rp, rtab, p)
                                yield
                                for g in range(2):
                                    tr(QdS[:, 0, g, tsl], zr[:, g * 128:(g + 1) * 128], 128, [("zr", p, 0)], ["QdS"])
                                    tr(QdS[:, 1, g, tsl], zt[:, g * 128:(g + 1) * 128], 128, ZA, ["QdS"], eng="act")
                                    tr(XS[:, X_KD + g * 1024 + ti * 128:X_KD + g * 1024 + (ti + 1) * 128],
                                       zr[:, 256 + g * 128:256 + (g + 1) * 128], 128, [("zr", p, 1)], ["XS"])
                                yield
                                cp("pool", XS[:, X_VD + ti * 256:X_VD + (ti + 1) * 256], zt[:, 512:768], r=ZB, w=["XS"])
                                cp("pool", XS[:, X_U + ti * 256:X_U + (ti + 1) * 256], zt[:, 768:1024], r=ZB, w=["XS"])
                            else:
                                sq_i, pos0 = ti // 2, (ti % 2) * 128
                                B.dma("sp", D["o_dk"][sq_i, l, pos0:pos0 + 128, :], zt[:, 256:512], r=ZA)
                                B.dma("sp", D["o_dv"][sq_i, l, pos0:pos0 + 128, :], zt[:, 512:768], r=ZB)
                                for g in range(2):
                                    tr(QdP[:, g, tsl], zt[:, g * 128:(g + 1) * 128], 128, ZA, ["QdP"], eng="act")
                                    tr(KdP[:, g, tsl], zt[:, 256 + g * 128:256 + (g + 1) * 128], 128, ZA, ["KdP"])
                                yield
                                vsrc = zt[:, 512:768].rearrange("p (a e c) -> p a e c", e=2, c=64)
                                vdst = VdP[:, ti, :].rearrange("p (a b) -> p a b", b=192)
                                cp("pool", vdst[:, :, 0:64], vsrc[:, :, 0, :], r=ZB, w=["VdP"])
                                cp("pool", vdst[:, :, 128:192], vsrc[:, :, 1, :], r=ZB, w=["VdP"])
                                cp("pool", UP[:, ti, :], zt[:, 768:1024], r=ZB, w=["UP"])
                            yield

                        def laneA(tiles):
                            for tl in tiles:
                                projA(tl)
                                yield
                                yield from postA(tl)
                        drive([laneA([0, 2]), laneA([1, 3])])
                        load_winB()
                        B.barrier()
                        A.release(mA)
                        ck(23)

                        ztB = [A.alloc([128, 864], F32) for _ in range(2)]
                        znB = [A.alloc([128, 768], F32) for _ in range(2)]
                        ssB = [A.alloc([128, 8], F32) for _ in range(2)]
                        jkB = [A.alloc([128, 736], F32) for _ in range(2)]
                        rpB = [A.alloc([128, 4, 128], F32) for _ in range(2)] if grp == "S" else [None, None]
                        zrB = [A.alloc([128, 384], F32) for _ in range(2)] if grp == "S" else [None, None]
                        qmB = [A.alloc([128, 640], F32) for _ in range(2)]
                        cqB = [A.alloc([128, 2, 128], BF16) for _ in range(2)]
                        rtB = [A.alloc([128, 4, 128], F32) for _ in range(2)] if grp == "S" else [None, None]
                        def projB(tl):
                            ti = half * 4 + tl
                            p = tl % 2
                            zt, zn, ss, junk, rp, zr, qm, cqT, rtab = ztB[p], znB[p], ssB[p], jkB[p], rpB[p], zrB[p], qmB[p], cqB[p], rtB[p]
                            qnope, qrope, qrope_r, kr4 = qm[:, 0:256], qm[:, 256:384], qm[:, 384:512], qm[:, 512:640]
                            tsl = slice(ti * 128, (ti + 1) * 128)
                            for (c0, c1, eng, wb, wk) in ((0, 352, "act", win0, "win0"), (352, 864, "dve", win1, "win1")):
                                pb = ps_get()
                                for k in range(8):
                                    mm(psum[pb][:, 0:c1 - c0], hT[:, k, tl * 128:(tl + 1) * 128], wb[:, k, 0:c1 - c0],
                                       start=(k == 0), stop=(k == 7), r=[("hT", k), wk], w=[PK(pb)])
                                cp(eng, zt[:, c0:c1], psum[pb][:, 0:c1 - c0], r=[PK(pb)], w=[("ztb", p, c0)])

                        def postB(tl):
                            ti = half * 4 + tl
                            p = tl % 2
                            zt, zn, ss, junk, rp, zr, qm, cqT, rtab = ztB[p], znB[p], ssB[p], jkB[p], rpB[p], zrB[p], qmB[p], cqB[p], rtB[p]
                            qnope, qrope, qrope_r, kr4 = qm[:, 0:256], qm[:, 256:384], qm[:, 384:512], qm[:, 512:640]
                            tsl = slice(ti * 128, (ti + 1) * 128)
                            ZR = [("ztb", p, 0), ("ztb", p, 352)]
                            slices = [(0, 192), (192, 128)] + [(352 + 64 * h, 64) for h in range(4)] + \
                                     [(608 + 64 * h, 64) for h in range(2)]
                            SS = [("ss", p, j) for j in range(8)]
                            for (c0, c1, n, j0, j1) in ((0, 192, 192, 0, 1), (192, 320, 128, 1, 2), (352, 736, 64, 2, 8)):
                                stt("dve", junk[:, c0:c1], zt[:, c0:c1], 1.0 / n, zt[:, c0:c1], ALU.mult, ALU.mult,
                                    r=ZR, w=[("junk", p, c0)])
                                B.op("dve", lambda e, c0=c0, c1=c1, n=n, j0=j0, j1=j1, junk=junk, ss=ss: e.reduce_sum(
                                    out=ss[:, j0:j1], in_=junk[:, c0:c1].rearrange("p (a b) -> p a b", b=n),
                                    axis=mybir.AxisListType.X), r=[("junk", p, c0)], w=[("ss", p, j) for j in range(j0, j1)])
                            yield
                            act(ss, ss, AF.Ln, r=SS + ["eps"], w=[("ssr", p)], bias=epsc[:, 0:1], scale=1.0)
                            act(ss, ss, AF.Exp, r=[("ssr", p)], w=[("ssr", p)], scale=-0.5)
                            zdst = [0, 192, 320, 320 + 128, 320 + 64, 320 + 192, 576, 640]
                            gsrc = [0, 192, 320, 320, 320, 320, 384, 384]
                            for j, (c0, n) in enumerate(slices):
                                stt("dve", zn[:, zdst[j]:zdst[j] + n], zt[:, c0:c0 + n], ss[:, j:j + 1], G[:, gsrc[j]:gsrc[j] + n],
                                    ALU.mult, ALU.mult, r=ZR + [("ssr", p), "gtab"], w=[("zn", p, j)])
                            ZN = [("zn", p, j) for j in range(8)]
                            ck(24)
                            yield
                            tr(cqT[:, 0, :], zn[:, 0:128], 128, ZN, [("cqT", p, 0)])
                            tr(cqT[0:64, 1, :], zn[:, 128:192], 64, ZN, [("cqT", p, 1)])
                            pb = ps_get()
                            mm(psum[pb][:, 0:384], cqT[:, 0, :], wuq[:, l * 2, :], start=True, stop=False,
                               r=[("cqT", p, 0), "wuq"], w=[PK(pb)])
                            mm(psum[pb][:, 0:384], cqT[0:64, 1, :], wuq[0:64, l * 2 + 1, :], start=False, stop=True,
                               r=[("cqT", p, 1), "wuq"], w=[PK(pb)])
                            qraw = psum[pb][:, 0:384].rearrange("p (h c) -> p h c", c=96)
                            cp("dve", qnope.rearrange("p (h c) -> p h c", c=64), qraw[:, :, 0:64], r=[PK(pb)], w=[("qnope", p)])
                            cp("dve", qrope.rearrange("p (h c) -> p h c", c=32), qraw[:, :, 64:96], r=[PK(pb)], w=[("qrope", p)])
                            ck(25)
                            yield
                            KR = zt[:, 320:352]
                            DV = zt[:, 736:864]
                            if grp == "S":
                                load_rope(ti, rtab, p)
                                rope(zr[:, 0:256], zn[:, 320:576], 4, 64, 2, 3, ZN, [("zr", p, 2)], rp, rtab, p)
                                yield
                                rope(zr[:, 256:384], zn[:, 576:704], 2, 64, 2, 3, ZN, [("zr", p, 3)], rp, rtab, p)
                                yield
                                rope(qrope_r, qrope, 4, 32, 0, 1, [("qrope", p)], [("qrope_r", p)], rp, rtab, p)
                                yield
                                rope(kr4[:, 0:32], KR, 1, 32, 0, 1, ZR, [("kr4a", p)], rp, rtab, p)
                                for h in range(1, 4):
                                    cp("pool", kr4[:, 32 * h:32 * h + 32], kr4[:, 0:32], r=[("kr4a", p)], w=[("kr4", p)])
                                ck(26)
                                yield
                                for g in range(2):
                                    tr(QnS[:, g, tsl], qnope[:, g * 128:(g + 1) * 128], 128, [("qnope", p)], ["QnS"])
                                    tr(QgS[:, 0, g, tsl], zr[:, g * 128:(g + 1) * 128], 128, [("zr", p, 2)], ["QgS"])
                                    tr(QgS[:, 1, g, tsl], zn[:, 320 + g * 128:320 + (g + 1) * 128], 128, ZN, ["QgS"], eng="act")
                                yield
                                tr(QrS[:, 0, tsl], qrope_r, 128, [("qrope_r", p)], ["QrS"])
                                tr(QrS[:, 1, tsl], qrope, 128, [("qrope", p)], ["QrS"], eng="act")
                                tr(XS[:, X_KG + ti * 128:X_KG + (ti + 1) * 128], zr[:, 256:384], 128, [("zr", p, 3)], ["XS"])
                                tr(XS[:, X_CKV + ti * 128:X_CKV + (ti + 1) * 128], zn[:, 192:320], 128, ZN, ["XS"], eng="act")
                                tr(XS[:, X_KR + ti * 128:X_KR + (ti + 1) * 128], kr4, 128, [("kr4", p), ("kr4a", p)], ["XS"])
                                cp("pool", XS[:, X_VG + ti * 128:X_VG + (ti + 1) * 128], DV, r=ZR, w=["XS"])
                            else:
                                sq_i, pos0 = ti // 2, (ti % 2) * 128
                                B.dma("sp", D["o_kr"][sq_i, l, pos0:pos0 + 128, :], KR, r=ZR)
                                B.dma("sp", D["o_gv"][sq_i, l, pos0:pos0 + 128, :], DV, r=ZR)
                                B.dma("sp", D["o_ckv"][sq_i, l, pos0:pos0 + 128, :], zn[:, 192:320], r=ZN)
                                B.dma("sp", D["o_gk"][sq_i, l, pos0:pos0 + 128, :], zn[:, 576:704], r=ZN)
                                for h in range(4):
                                    cp("pool", kr4[:, 32 * h:32 * h + 32], KR, r=ZR, w=[("kr4", p)])
                                for g in range(2):
                                    tr(QnP[:, g, tsl], qnope[:, g * 128:(g + 1) * 128], 128, [("qnope", p)], ["QnP"])
                                    tr(QgP[:, g, tsl], zn[:, 320 + g * 128:320 + (g + 1) * 128], 128, ZN, ["QgP"], eng="act")
                                yield
                                tr(QrP[:, tsl], qrope, 128, [("qrope", p)], ["QrP"])
                                tr(KgP[:, tsl], zn[:, 576:704], 128, ZN, ["KgP"])
                                tr(KrP[:, tsl], kr4, 128, [("kr4", p)], ["KrP"], eng="act")
                                tr(CkP[:, tsl], zn[:, 192:320], 128, ZN, [("CkP", ti)])
                                vsrc = DV.rearrange("p (a c) -> p a c", c=64)
                                vdst = VgP[:, ti, :].rearrange("p (a b) -> p a b", b=192)
                                cp("pool", vdst[:, :, 0:64], vsrc, r=ZR, w=["VgP"])
                                cp("pool", vdst[:, :, 128:192], vsrc, r=ZR, w=["VgP"])
                                yield
                                pb2 = ps_get()
                                mm(psum[pb2][:, 0:256], CkP[:, tsl], wuv[:, l, :], start=True, stop=True,
                                   r=[("CkP", ti), "wuv"], w=[PK(pb2)])
                                vsrc = psum[pb2][:, 0:256].rearrange("p (a e c) -> p a e c", e=2, c=64)
                                vdst = VmP[:, ti, :].rearrange("p (a b) -> p a b", b=192)
                                cp("dve", vdst[:, :, 0:64], vsrc[:, :, 0, :], r=[PK(pb2)], w=["VmP"])
                                cp("dve", vdst[:, :, 128:192], vsrc[:, :, 1, :], r=[PK(pb2)], w=["VmP"])
                                for g in range(2):
                                    pb3 = ps_get()
                                    mm(psum[pb3][:, 0:128], wuk[:, l, g * 128:(g + 1) * 128], CkP[:, tsl], start=True, stop=True,
                                       r=[("CkP", ti), "wuk"], w=[PK(pb3)])
                                    cp("act", KnP[:, g, tsl], psum[pb3][:, 0:128], r=[PK(pb3)], w=["KnP"])
                            yield

                        def laneB(tiles):
                            for tl in tiles:
                                projB(tl)
                                yield
                                yield from postB(tl)
                        drive([laneB([0, 2]), laneB([1, 3])])
                        if half == 0:
                            load_winA()
                        B.barrier()
                        A.release(m1)
                    A.release(mI)

                def epi_plain(dst_fn, rcp):
                    def epi(ob, nq=None):
                        pass
                    return epi

                def make_epi(dst_fn, eo, nq, rcp, wkey="yT"):
                    def epi(ob):
                        o0, d0 = (0, 64) if eo == 0 else (64, 0)
                        if nq == 256:
                            act(rcp[d0:d0 + 64, 0:nq], psum[ob][d0:d0 + 64, 0:nq], AF.Ln, r=[PK(ob)], w=["rcp"])
                            act(rcp[d0:d0 + 64, 0:nq], rcp[d0:d0 + 64, 0:nq], AF.Exp, r=["rcp"], w=["rcp"], scale=-1.0)
                        else:
                            recip(rcp[d0:d0 + 64, 0:nq], psum[ob][d0:d0 + 64, 0:nq], r=[PK(ob)], w=["rcp"])
                        tt("dve", dst_fn(o0, o0 + 64), psum[ob][o0:o0 + 64, 0:nq], rcp[d0:d0 + 64, 0:nq], ALU.mult,
                           r=[PK(ob), "rcp"], w=[wkey])
                    return epi

                def diff_epilogue(Omaps, q0, nq, dtmp):
                    y, ysq, rs = dtmp
                    for pp in range(2):
                        stt("dve", y[:, 0:nq], Omaps[:, 2 * pp + 1, 0:nq], lam[:, l * 4:l * 4 + 1], Omaps[:, 2 * pp, 0:nq],
                            ALU.mult, ALU.add, r=["Om", "lam"], w=["dy"])
                        act(ysq[:, 0:nq], y[:, 0:nq], AF.Square, r=["dy"], w=["dysq"])
                        pb = ps_get()
                        mm(psum[pb][:, 0:nq], onesbd, ysq[:, 0:nq], start=True, stop=True,
                           r=["dysq", "onesbd"], w=[PK(pb)])
                        act(rs[:, 0:nq], psum[pb][:, 0:nq], AF.Ln, r=[PK(pb), "eps"], w=["drs"],
                            bias=epsc[:, 0:1], scale=1.0 / 64)
                        act(rs[:, 0:nq], rs[:, 0:nq], AF.Exp, r=["drs"], w=["drs"], scale=-0.5)
                        stt("dve", y[:, 0:nq], y[:, 0:nq], gsub[:, l:l + 1], rs[:, 0:nq], ALU.mult, ALU.mult,
                            r=["dy", "drs", "gsub"], w=["dy"])
                        act(yT[:, pp, q0:q0 + nq], y[:, 0:nq], AF.Identity, r=["dy"], w=["yT"], scale=1.0 - lam_init)

                def fnet_stage2(PQ, q0, nq, scl):
                    for hc in range(2):
                        pb = ps_get()
                        mm(psum[pb][:, 0:nq], c64, PQ[:, 0, hc, 0:nq], start=True, stop=False, r=["PQ", "c64"], w=[PK(pb)])
                        mm(psum[pb][:, 0:nq], s64n, PQ[:, 1, hc, 0:nq], start=False, stop=True, r=["PQ", "c64"], w=[PK(pb)])
                        act(yT[:, 2 + hc, q0:q0 + nq], psum[pb][:, 0:nq], AF.Identity, r=[PK(pb)], w=["yT"], scale=scl)

                def outproj(t0):
                    v = 1 if t0 >= TP else 0
                    m1 = A.mark()
                    wo = A.alloc([128, 8, 1024], BF16)
                    B.dma("pool", wo, D["wout"][l].rearrange("p (h o) -> p h o", o=1024), w=["wo"])
                    for tq in range(2):
                        for o in range(8):
                            pb = ps_get()
                            osl = slice(o * 128, (o + 1) * 128)
                            qsl = slice(tq * 512, (tq + 1) * 512)
                            for c in range(8):
                                mm(psum[pb][:, :], wo[:, c, osl], yT[:, c, qsl], start=(c == 0), stop=(c == 7),
                                   r=["wo", "yT"], w=[PK(pb)])
                            xs = xT[:, o, t0 + tq * 512:t0 + (tq + 1) * 512]
                            stt("dve", xs, psum[pb][:, :], mods_ap(l, 2, o, v), xs, ALU.mult, ALU.add,
                                r=[PK(pb), "mods", "xT"], w=["xT"])
                    B.barrier()
                    A.release(m1)

                if not DEBUG.get("skipS"):
                    mX = A.mark()
                    XS = A.alloc([128, XW], BF16)
                    inproj("S")
                    ck(2)
                    for j in range(3):
                        B.dma("sp", xin[l][j].ap(), XS[:, j * 4096:j * 4096 + SEGW[j]], r=["XS"], w=[("xin", j)])
                    B.barrier()
                    A.release(mX)
                    for j in range(3):
                        if not DEBUG.get("nocc"):
                            B.op("pool", lambda e, l=l, j=j: e.collective_compute(
                                "AllGather", ALU.bypass, replica_groups=[[0, 1, 2, 3], [4, 5, 6, 7]],
                                ins=[xin[l][j].ap().opt()], outs=[xout[l][j].ap().opt()]),
                                r=[("xin", j)], w=[("xout", j)], cc=True)
                XOUT = [("xout", 0), ("xout", 1), ("xout", 2)]
                xov = [xout[l][j].ap().rearrange("(r p) w -> p r w", p=128) for j in range(3)]

                def xo_piece(r_, off, n):
                    j = off // 4096
                    return xov[j][:, r_, off - j * 4096:off - j * 4096 + n]

                mP = A.mark()
                QdP = A.alloc([128, 2, TP], BF16)
                QnP = A.alloc([128, 2, TP], BF16)
                QrP = A.alloc([128, TP], BF16)
                QgP = A.alloc([128, 2, TP], BF16)
                KdP = A.alloc([128, 2, TP], BF16)
                KnP = A.alloc([128, 2, TP], BF16)
                KrP = A.alloc([128, TP], BF16)
                KgP = A.alloc([128, TP], BF16)
                CkP = A.alloc([128, TP], BF16)
                VdP = A.alloc([128, 8, 384], BF16)
                VmP = A.alloc([128, 8, 384], BF16)
                VgP = A.alloc([128, 8, 384], BF16)
                UP = A.alloc([128, 8, 256], BF16)
                for Vx, kx in ((VdP, "VdP"), (VmP, "VmP"), (VgP, "VgP")):
                    memset("pool", Vx.rearrange("p c (a b) -> p c a b", b=192)[:, :, :, 64:128], 1.0, w=[kx])
                inproj("P")
                ck(3)
                yT = A.alloc([128, 8, 1024], BF16)
                mPa = A.mark()
                ptb = [A.alloc([128, 1024], BF16) for _ in range(3)]
                rcp = A.alloc([128, 512], F32)
                Om = A.alloc([128, 4, 256], F32)
                dtmp = (A.alloc([128, 512], F32), A.alloc([128, 512], BF16), A.alloc([128, 512], F32))
                PQ = A.alloc([128, 2, 2, 512], BF16)

                for s in range(4):
                    q0 = s * 256
                    qs = slice(q0, q0 + 256)
                    jobs = []
                    for m in range(8):
                        g, pr = m // 4, (m % 4) * 32
                        jobs.append(dict(
                            nk=2, r=["QdP", "KdP", "VdP"],
                            qk=lambda kc, g=g, pr=pr, q0=q0: [(KdP[pr:pr + 32, g, q0 + kc * 128:q0 + (kc + 1) * 128],
                                                               QdP[pr:pr + 32, g, q0:q0 + 256])],
                            v=lambda kc, m=m, s=s: VdP[:, s * 2 + kc, (m // 4) * 192 + ((m // 2) % 2) * 64:(m // 4) * 192 + ((m // 2) % 2) * 64 + 128],
                            epi=make_epi(lambda lo, hi, m=m: Om[lo:hi, (m // 4) * 2 + m % 2, :], (m // 2) % 2, 256, rcp, "Om")))
                    attention2([(jobs[2 * i], jobs[2 * i + 1]) for i in range(4)], 32 ** -0.5, ptb, 256)
                    diff_epilogue(Om, q0, 256, dtmp)
                    jobs = []
                    for h in range(4):
                        g, pr = h // 2, (h % 2) * 64
                        jobs.append(dict(
                            nk=2, r=["QnP", "KnP", "QrP", "KrP", "VmP"],
                            qk=lambda kc, g=g, pr=pr, h=h, q0=q0: [
                                (KnP[pr:pr + 64, g, q0 + kc * 128:q0 + (kc + 1) * 128], QnP[pr:pr + 64, g, q0:q0 + 256]),
                                (KrP[32 * h:32 * h + 32, q0 + kc * 128:q0 + (kc + 1) * 128], QrP[32 * h:32 * h + 32, q0:q0 + 256])],
                            v=lambda kc, h=h, s=s: VmP[:, s * 2 + kc, (h // 2) * 192 + (h % 2) * 64:(h // 2) * 192 + (h % 2) * 64 + 128],
                            epi=make_epi(lambda lo, hi, h=h, qs=qs: yT[lo:hi, 4 + h // 2, qs], h % 2, 256, rcp)))
                    attention2([(jobs[0], jobs[1]), (jobs[2], jobs[3])], 96 ** -0.5, ptb, 256)
                    jobs = []
                    for h in range(4):
                        kv, ab = h // 2, h % 2
                        jobs.append(dict(
                            nk=2, r=["QgP", "KgP", "VgP"],
                            qk=lambda kc, kv=kv, ab=ab, q0=q0: [(KgP[64 * kv:64 * kv + 64, q0 + kc * 128:q0 + (kc + 1) * 128],
                                                                 QgP[64 * kv:64 * kv + 64, ab, q0:q0 + 256])],
                            v=lambda kc, kv=kv, ab=ab, s=s: VgP[:, s * 2 + kc, kv * 192 + ab * 64:kv * 192 + ab * 64 + 128],
                            epi=make_epi(lambda lo, hi, h=h, qs=qs: yT[lo:hi, 6 + h // 2, qs], h % 2, 256, rcp)))
                    attention2([(jobs[0], jobs[2]), (jobs[1], jobs[3])], 64 ** -0.5, ptb, 256)
                    for tab, tabt in enumerate((c256, s256)):
                        for hc in range(2):
                            pb = ps_get()
                            for sc in range(2):
                                mm(psum[pb][:, 0:256], UP[:, s * 2 + sc, hc * 128:(hc + 1) * 128], tabt[:, sc, :],
                                   start=(sc == 0), stop=(sc == 1), r=["UP", "c256"], w=[PK(pb)])
                            cp("dve", PQ[:, tab, hc, 0:256], psum[pb][:, 0:256], r=[PK(pb)], w=["PQ"])
                    fnet_stage2(PQ, q0, 256, (256 * 64) ** -0.5)
                B.barrier()
                A.release(mPa)
                ck(4)
                outproj(0)
                ck(5)
                A.release(mP)

                if not DEBUG.get("skipS"):
                    mS = A.mark()
                    yT = A.alloc([128, 8, 1024], BF16)
                    mSa = A.mark()
                    ptb = [A.alloc([128, 1024], BF16) for _ in range(3)]
                    rcp = A.alloc([128, 512], F32)
                    m1 = A.mark()
                    Ug = A.alloc([128, 32, 256], BF16)
                    PQ = A.alloc([128, 2, 2, 512], BF16)
                    tabb = [A.alloc([128, 2, 8, 512], BF16) for _ in range(2)]
                    for r_ in range(4):
                        B.dma("sp", Ug[:, r_ * 8:(r_ + 1) * 8, :],
                              xo_piece(r_, X_U, 2048).rearrange("p (c n) -> p c n", n=256), r=XOUT, w=["Ug"])
                    cbv = D["cbig"].rearrange("p (t s k) -> p t s k", t=2, k=512)
                    sbv = D["sbig"].rearrange("p (t s k) -> p t s k", t=2, k=512)
                    ld = 0
                    for kt in range(2):
                        banks = [ps_get(hold=True) for _ in range(4)]
                        for s8 in range(4):
                            tb = tabb[ld % 2]
                            key = ("tabb", ld % 2)
                            ld += 1
                            B.dma("sp", tb[:, 0, :, :], cbv[:, kt, s8 * 8:(s8 + 1) * 8, :], w=[key])
                            B.dma("sp", tb[:, 1, :, :], sbv[:, kt, s8 * 8:(s8 + 1) * 8, :], w=[key])
                            for si in range(8):
                                sc = s8 * 8 + si
                                for tab in range(2):
                                    for hc in range(2):
                                        b = banks[tab * 2 + hc]
                                        mm(psum[b][:, :], Ug[:, sc, hc * 128:(hc + 1) * 128], tb[:, tab, si, :],
                                           start=(sc == 0), stop=(sc == 31), r=["Ug", key], w=[PK(b)])
                        for tab in range(2):
                            for hc in range(2):
                                b = banks[tab * 2 + hc]
                                cp("dve" if hc else "act", PQ[:, tab, hc, :], psum[b][:, :], r=[PK(b)], w=["PQ"])
                                ps_rel(b)
                        fnet_stage2(PQ, kt * 512, 512, (4096 * 64) ** -0.5)
                    B.barrier()
                    A.release(m1)

                    ck(6)
                    m1 = A.mark()
                    Kd = A.alloc([128, 2, 4608], BF16)
                    Vd = A.alloc([128, 36, 384], BF16)
                    Om = A.alloc([128, 4, 512], F32)
                    dtmp = (A.alloc([128, 512], F32), A.alloc([128, 512], BF16), A.alloc([128, 512], F32))
                    memset("pool", Vd.rearrange("p c (a b) -> p c a b", b=192)[:, :, :, 64:128], 1.0, w=["Vdones"])

                    def vload(q, Vx, c0, nchunk, src, key, rkeys):
                        sv = src.rearrange("p (c a e n) -> p c a e n", a=2, e=2, n=64)
                        dv = Vx[:, c0:c0 + nchunk, :].rearrange("p c (a b) -> p c a b", b=192)
                        for a in range(2):
                            B.dma(q, dv[:, :, a, 0:64], sv[:, :, a, 0, :], r=rkeys, w=[key])
                            B.dma(q, dv[:, :, a, 128:192], sv[:, :, a, 1, :], r=rkeys, w=[key])
                    for r_ in range(4):
                        B.dma("sp", Kd[:, :, r_ * 1024:(r_ + 1) * 1024],
                              xo_piece(r_, X_KD, 2048).rearrange("p (g n) -> p g n", n=1024), r=XOUT, w=[("Kd", r_)])
                        vload("sp", Vd, r_ * 8, 8, xo_piece(r_, X_VD, 2048), ("Vd", r_), XOUT)
                    B.dma("pool", Kd[:, :, 4096:4608], D["ckdT"][l].rearrange("p (g n) -> p g n", n=512), w=[("Kd", 4)])
                    vload("pool", Vd, 32, 4, D["cvd"][l], ("Vd", 4), [])
                    QB = [A.alloc([128, 2, 512], BF16) for _ in range(4)]
                    for b_ in range(4):
                        memset("pool", QB[b_], 0.0, w=[("QB", b_)])
                    for tq in range(2):
                        q0 = tq * 512
                        jobs = []
                        for m in range(8):
                            g, pr = m // 4, (m % 4) * 32
                            jobs.append(dict(
                                nk=36, r=[("QB", m % 4), "Vdones"],
                                rk=lambda kc: [('Kd', kc // 8 if kc < 32 else 4), ('Vd', kc // 8 if kc < 32 else 4)],
                                pre=lambda g=g, pr=pr, q0=q0, b_=m % 4: cp("pool", QB[b_][pr:pr + 32, :, :],
                                                                           QdS[pr:pr + 32, :, g, q0:q0 + 512],
                                                                           r=["QdS"], w=[("QB", b_)]),
                                qk=lambda kc, g=g, b_=m % 4: [(Kd[:, g, kc * 128:(kc + 1) * 128],
                                                               QB[b_][:, 0 if kc < 32 else 1, :])],
                                v=lambda kc, m=m: Vd[:, kc, (m // 4) * 192 + ((m // 2) % 2) * 64:(m // 4) * 192 + ((m // 2) % 2) * 64 + 128],
                                epi=make_epi(lambda lo, hi, m=m: Om[lo:hi, (m // 4) * 2 + m % 2, :], (m // 2) % 2, 512, rcp, "Om")))
                        attention2([(jobs[2 * i], jobs[2 * i + 1]) for i in range(4)], 32 ** -0.5, ptb, 512)
                        diff_epilogue(Om, q0, 512, dtmp)
                    B.barrier()
                    A.release(m1)

                    ck(7)
                    m1 = A.mark()
                    Ck = A.alloc([128, 4608], BF16)
                    for r_ in range(4):
                        B.dma("sp", Ck[:, r_ * 1024:(r_ + 1) * 1024], xo_piece(r_, X_CKV, 1024), r=XOUT, w=[("Ck", r_)])
                    B.dma("pool", Ck[:, 4096:4608], D["cckvT"][l], w=[("Ck", 4)])
                    for hp in range(2):
                        m2 = A.mark()
                        KK = [A.alloc([128, 4608], BF16) for _ in range(2)]
                        Vm = A.alloc([128, 36, 192], BF16)
                        QMB = [A.alloc([128, 2, 512], BF16) for _ in range(4)]
                        memset("pool", Vm[:, :, 64:128], 1.0, w=["Vmones"])
                        for hh in range(2):
                            h = hp * 2 + hh
                            memset("pool", KK[hh][96:128, :], 0.0, w=[("KK", hh, "z")])
                            for tq_ in range(2):
                                memset("pool", QMB[tq_ * 2 + hh][96:128, :, :], 0.0, w=[("QMB", tq_ * 2 + hh)])
                            for r_ in range(4):
                                B.dma("sp", KK[hh][64:96, r_ * 1024:(r_ + 1) * 1024], xo_piece(r_, X_KR, 1024)[64:96, :],
                                      r=XOUT, w=[("KK", hh, "r", r_)])
                            B.dma("pool", KK[hh][64:96, 4096:4608], D["ckrT"][l][64:96, :], w=[("KK", hh, "r", 4)])
                            for kt in range(9):
                                pb = ps_get()
                                mm(psum[pb][0:64, :], wuk[:, l, h * 64:(h + 1) * 64], Ck[:, kt * 512:(kt + 1) * 512],
                                   start=True, stop=True, r=[("Ck", kt // 2), "wuk"], w=[PK(pb)])
                                cp("dve" if kt % 2 else "act", KK[hh][0:64, kt * 512:(kt + 1) * 512], psum[pb][0:64, :],
                                   r=[PK(pb)], w=[("KK", hh, "n", kt)])
                        for kc in range(36):
                            pb = ps_get()
                            mm(psum[pb][:, 0:128], Ck[:, kc * 128:(kc + 1) * 128], wuv[:, l, hp * 128:(hp + 1) * 128], start=True, stop=True,
                               r=[("Ck", kc // 8 if kc < 32 else 4), "wuv"], w=[PK(pb)])
                            cp("dve", Vm[:, kc, :].rearrange("p (a b) -> p a b", b=64)[:, 0:3:2, :],
                               psum[pb][:, 0:128].rearrange("p (h c) -> p h c", c=64), r=[PK(pb)], w=[("Vm", kc)])
                        mpairs = []
                        for tq in range(2):
                            q0 = tq * 512
                            jobs = []
                            for hh in range(2):
                                h = hp * 2 + hh
                                pr = hh * 64
                                qi = tq * 2 + hh

                                def pre(qi=qi, h=h, pr=pr, hp=hp, q0=q0):
                                    for ver in range(2):
                                        cp("dve", QMB[qi][0:64, ver, :], QnS[pr:pr + 64, hp, q0:q0 + 512], r=["QnS"], w=[("QMB", qi)])
                                        cp("dve", QMB[qi][64:96, ver, :], QrS[32 * h:32 * h + 32, ver, q0:q0 + 512],
                                           r=["QrS"], w=[("QMB", qi)])
                                jobs.append(dict(
                                    nk=36, r=[("QMB", qi), ("KK", hh, "z"), "Vmones"], pre=pre,
                                    rk=lambda kc, hh=hh: [("KK", hh, "r", kc // 8 if kc < 32 else 4), ("KK", hh, "n", kc // 4),
                                                          ("Vm", kc)],
                                    qk=lambda kc, hh=hh, qi=qi: [(KK[hh][:, kc * 128:(kc + 1) * 128], QMB[qi][:, 0 if kc < 32 else 1, :])],
                                    v=lambda kc, hh=hh: Vm[:, kc, hh * 64:hh * 64 + 128],
                                    epi=make_epi(lambda lo, hi, hp=hp, q0=q0: yT[lo:hi, 4 + hp, q0:q0 + 512], hh, 512, rcp)))
                            mpairs.append((jobs[0], jobs[1]))
                        attention2(mpairs, 96 ** -0.5, ptb, 512)
                        B.barrier()
                        A.release(m2)
                    A.release(m1)

                    ck(8)
                    m1 = A.mark()
                    Kg = A.alloc([128, 4608], BF16)
                    Vg = A.alloc([128, 36, 384], BF16)
                    memset("pool", Vg.rearrange("p c (a b) -> p c a b", b=192)[:, :, :, 64:128], 1.0, w=["Vgones"])

                    def vloadg(q, c0, nchunk, src, rkeys):
                        sv = src.rearrange("p (c a n) -> p c a n", a=2, n=64)
                        dv = Vg[:, c0:c0 + nchunk, :].rearrange("p c (a b) -> p c a b", b=192)
                        B.dma(q, dv[:, :, :, 0:64], sv, r=rkeys, w=[("Vg", c0 // 8)])
                        B.dma(q, dv[:, :, :, 128:192], sv, r=rkeys, w=[("Vg", c0 // 8)])
                    for r_ in range(4):
                        B.dma("sp", Kg[:, r_ * 1024:(r_ + 1) * 1024], xo_piece(r_, X_KG, 1024), r=XOUT, w=[("Kg", r_)])
                        vloadg("sp", r_ * 8, 8, xo_piece(r_, X_VG, 1024), XOUT)
                    B.dma("pool", Kg[:, 4096:4608], D["ckgT"][l], w=[("Kg", 4)])
                    vloadg("pool", 32, 4, D["cvg"][l], [])
                    for tq in range(2):
                        q0 = tq * 512
                        jobs = []
                        for h in range(4):
                            kv, ab = h // 2, h % 2
                            jobs.append(dict(
                                nk=36, r=["QgS", "Vgones"],
                                rk=lambda kc: [('Kg', kc // 8 if kc < 32 else 4), ('Vg', kc // 8 if kc < 32 else 4)],
                                qk=lambda kc, kv=kv, ab=ab, q0=q0: [(Kg[64 * kv:64 * kv + 64, kc * 128:(kc + 1) * 128],
                                                                     QgS[64 * kv:64 * kv + 64, 0 if kc < 32 else 1, ab, q0:q0 + 512])],
                                v=lambda kc, kv=kv, ab=ab: Vg[:, kc, kv * 192 + ab * 64:kv * 192 + ab * 64 + 128],
                                epi=make_epi(lambda lo, hi, h=h, q0=q0: yT[lo:hi, 6 + h // 2, q0:q0 + 512], h % 2, 512, rcp)))
                        attention2([(jobs[0], jobs[2]), (jobs[1], jobs[3])], 64 ** -0.5, ptb, 512)
                    B.barrier()
                    A.release(mSa)
                    ck(9)
                    outproj(TP)
                    ck(10)
                A.release(mL)

                mM = A.mark()
                hT2 = A.alloc([128, 8, T], BF16)
                m1 = A.mark()
                sq = [A.alloc([128, 8, 512], BF16) for _ in range(2)]
                ntmp = [A.alloc([128, 2, 512], F32) for _ in range(2)]
                rstd = [A.alloc([128, 512], F32) for _ in range(2)]
                for grp_t0, v in ((0, 0), (TP, 1)):
                    fm_norm(grp_t0, 1024, lambda k, o, g0=grp_t0: hT2[:, k, g0 + o:g0 + o + 512],
                            lambda k, v=v: modA_ap(l, v, 1, k), lambda k, v=v: mods_ap(l, 3, k, v), ntmp, sq, rstd)
                B.barrier()
                A.release(m1)
                hid = A.alloc([128, 4, T], BF16)
                rl = [A.alloc([128, 512], F32) for _ in range(2)]
                w1b = [A.alloc([128, 8, 512], BF16) for _ in range(2)]
                w2b = [A.alloc([128, 4, 1024], BF16) for _ in range(2)]
                adw = [A.alloc([128, 8, 1024], BF16) for _ in range(2)] if l + 1 < NL else None
                w1v = D["w1"][l].rearrange("p (k c) -> p k c", c=4096)
                w2v = D["w2"][l].rearrange("p (j o) -> p j o", o=1024)
                for e8 in range(8):
                    wb1, wb2 = w1b[e8 % 2], w2b[e8 % 2]
                    k1, k2 = ("w1b", e8 % 2), ("w2b", e8 % 2)
                    B.dma("pool", wb1, w1v[:, :, e8 * 512:(e8 + 1) * 512], w=[k1])
                    B.dma("pool", wb2, w2v[:, e8 * 4:(e8 + 1) * 4, :], w=[k2])
                    if adw is not None and e8 < 6:
                        mods_load(l + 1, e8, adw[e8 % 2], ("adw", e8 % 2))
                    for tq in range(4):
                        qsl = slice(tq * 512, (tq + 1) * 512)
                        for jj in range(4):
                            pb = ps_get()
                            for k in range(8):
                                mm(psum[pb][:, :], wb1[:, k, jj * 128:(jj + 1) * 128], hT2[:, k, qsl], start=(k == 0), stop=(k == 7),
                                   r=[k1, ("hT", k)], w=[PK(pb)])
                            rk = ("rl", (tq * 4 + jj) % 2)
                            act(rl[(tq * 4 + jj) % 2], psum[pb][:, :], AF.Relu, r=[PK(pb)], w=[rk])
                            tt("pool", hid[:, jj, qsl], rl[(tq * 4 + jj) % 2], rl[(tq * 4 + jj) % 2], ALU.mult, r=[rk],
                               w=[("hid", jj, tq)])
                    for tq in range(4):
                        qsl = slice(tq * 512, (tq + 1) * 512)
                        v = 0 if tq < 2 else 1
                        for o in range(8):
                            pb = ps_get()
                            for jj in range(4):
                                mm(psum[pb][:, :], wb2[:, jj, o * 128:(o + 1) * 128], hid[:, jj, qsl], start=(jj == 0), stop=(jj == 3),
                                   r=[k2, ("hid", jj, tq)], w=[PK(pb)])
                            stt("dve", xT[:, o, qsl], psum[pb][:, :], mods_ap(l, 5, o, v), xT[:, o, qsl], ALU.mult, ALU.add,
                                r=[PK(pb), "mods", "xT"], w=["xT"])
                    if adw is not None and e8 < 6:
                        mods_piece(l + 1, e8, adw[e8 % 2], ("adw", e8 % 2))
                if adw is not None:
                    mods_finish(l + 1)
                B.barrier()
                A.release(mM)
                ck(11)

            mF = A.mark()
            sq = [A.alloc([128, 8, 512], BF16) for _ in range(2)]
            ntmp = [A.alloc([128, 2, 512], F32) for _ in range(2)]
            rstd = [A.alloc([128, 512], F32) for _ in range(2)]
            yo = A.alloc([128, 8, 1024], F32)
            yTv = D["yT"].rearrange("p (c t) -> p c t", t=T)
            for g0 in (0, TP):
                fm_norm(g0, 1024, lambda k, o: yo[:, k, o:o + 512], lambda k: gains[:, 32 + k:33 + k], lambda k: None,
                        ntmp, sq, rstd)
                B.dma("sp", yTv[:, :, g0:g0 + 1024], yo, r=[("hT", k) for k in range(8)], w=["yout"])

        except StopBuild:
            pass
        print("arena peak words", A.peak, "of", NW, {e: len(B.ops[e]) for e in B.ENGS})
        block = es.enter_context(nc.Block())
        B.emit(block, sems, dsems)
    return nc


def _prep(inp):
    f32 = np.float32
    bf = ml_dtypes.bfloat16

    def fm(w, k):
        C = w.shape[1]
        return np.ascontiguousarray(w.reshape(k, 128, C).transpose(1, 0, 2).reshape(128, k * C))

    shared = {}
    shared["adaw"] = np.stack([fm(inp["ada_w"][l], 8) for l in range(NL)])
    shared["adab"] = np.stack([np.ascontiguousarray(inp["ada_b"][l].reshape(48, 128).T) for l in range(NL)])
    shared["gmix"] = np.stack([np.ascontiguousarray(inp["norm_mix_g"][l].reshape(8, 128).T) for l in range(NL)])
    shared["gmlp"] = np.stack([np.ascontiguousarray(inp["norm_mlp_g"][l].reshape(8, 128).T) for l in range(NL)])
    shared["gfin"] = np.ascontiguousarray(inp["final_norm_g"].reshape(8, 128).T)
    shared["win"] = np.stack([fm(inp["w_in"][l], 8) for l in range(NL)])
    wuq = np.zeros((NL, 128, 2, 384), f32)
    for l in range(NL):
        wuq[l, :, 0, :] = inp["mla_w_uq"][l][0:128]
        wuq[l, 0:64, 1, :] = inp["mla_w_uq"][l][128:192]
    shared["wuq"] = wuq.reshape(NL, 128, 768)
    ukv = inp["mla_w_ukv"].reshape(NL, 128, 4, 128)
    shared["wuk"] = np.ascontiguousarray(ukv[:, :, :, 0:64].reshape(NL, 128, 256))
    shared["wuv"] = np.ascontiguousarray(ukv[:, :, :, 64:128].reshape(NL, 128, 256))
    shared["wout"] = np.stack([fm(inp["w_out"][l], 8) for l in range(NL)])
    shared["w1"] = np.stack([fm(inp["mlp_w1"][l], 8) for l in range(NL)])
    shared["w2"] = np.stack([fm(inp["mlp_w2"][l], 32) for l in range(NL)])
    gt = np.concatenate([inp["mla_q_norm_g"], inp["mla_kv_norm_g"], inp["gqa_q_norm_g"], inp["gqa_k_norm_g"]], axis=1)
    shared["gt"] = np.ascontiguousarray(np.broadcast_to(gt[:, None, :], (NL, 128, 448))).astype(f32)
    shared["gsub"] = np.ascontiguousarray(inp["diff_subln_g"].reshape(NL, 64, 1))
    shared["lamp"] = np.ascontiguousarray(np.broadcast_to(inp["diff_lambda"].reshape(NL, 1, 128), (NL, 128, 128))).astype(f32)
    shared["ident"] = np.eye(128, dtype=f32)
    s = np.arange(256, dtype=np.float64)
    ang = 2 * np.pi * np.outer(s, s) / 256
    shared["c256"] = fm(np.cos(ang), 2).astype(bf)
    shared["s256"] = fm(np.sin(ang), 2).astype(bf)
    c = np.arange(64, dtype=np.float64)
    a64 = 2 * np.pi * np.outer(c, c) / 64
    c64 = np.zeros((128, 128)); s64 = np.zeros((128, 128))
    for g in range(2):
        c64[g * 64:(g + 1) * 64, g * 64:(g + 1) * 64] = np.cos(a64)
        s64[g * 64:(g + 1) * 64, g * 64:(g + 1) * 64] = -np.sin(a64)
    shared["c64"] = c64.astype(bf)
    shared["s64n"] = s64.astype(bf)

    def rope_tabs(pos, d, H):
        a = d // 2
        inv = 10000.0 ** (-np.arange(0, a, 2, dtype=np.float64) / a)
        row = (pos // 64).astype(np.float64); col = (pos % 64).astype(np.float64)
        cr, sr = np.cos(row[:, None] * inv), np.sin(row[:, None] * inv)
        cc, sc = np.cos(col[:, None] * inv), np.sin(col[:, None] * inv)
        cs = np.stack([cr, cc], axis=1)
        sn = np.stack([sr, sc], axis=1)
        cs = np.broadcast_to(cs[:, None], (len(pos), H, 2, a // 2)).reshape(len(pos), -1)
        sn = np.broadcast_to(sn[:, None], (len(pos), H, 2, a // 2)).reshape(len(pos), -1)

        def lay(x):
            return np.ascontiguousarray(x.reshape(8, 128, 128).transpose(1, 0, 2).reshape(128, 1024)).astype(f32)
        return lay(cs), lay(sn)

    maps = []
    kk = np.arange(1024, dtype=np.float64)
    ss_ = np.arange(4096, dtype=np.float64)
    for i in range(8):
        b, r = i // 4, i % 4
        m = dict(shared)
        xp = inp["x_prompt"][4 * i:4 * i + 4].reshape(1024, 1024)
        xs = inp["x_sample"][b, r * 1024:(r + 1) * 1024]
        x = np.concatenate([xp, xs], axis=0)
        m["xT"] = np.ascontiguousarray(x.T.reshape(8, 128, T).transpose(1, 0, 2).reshape(128, 8 * T))
        cvv = np.stack([inp["c_ctx"], inp["c"][b]], axis=1)
        m["cv"] = np.ascontiguousarray(cvv.reshape(8, 128, 2).transpose(1, 0, 2).reshape(128, 16))
        pos = np.arange(r * 1024, (r + 1) * 1024)
        m["rc32"], m["rs32"] = rope_tabs(pos, 32, 8)
        m["rc64"], m["rs64"] = rope_tabs(pos, 64, 4)
        ang = 2 * np.pi * ((np.outer(ss_, kk + r * 1024)) % 4096) / 4096
        def lay_big(x):
            return np.ascontiguousarray(x.reshape(32, 128, 2, 512).transpose(1, 2, 0, 3).reshape(128, 32 * 1024)).astype(bf)
        m["cbig"] = lay_big(np.cos(ang))
        m["sbig"] = lay_big(np.sin(ang))
        m["ckdT"] = np.stack([np.ascontiguousarray(inp["cache_diff_k"][b, l].reshape(512, 2, 128).transpose(2, 1, 0).reshape(128, 1024)) for l in range(NL)])
        m["cvd"] = np.stack([fm(inp["cache_diff_v"][b, l].reshape(512, 256), 4) for l in range(NL)])
        m["cckvT"] = np.stack([np.ascontiguousarray(inp["cache_mla_ckv"][b, l].T) for l in range(NL)])
        m["ckrT"] = np.stack([np.ascontiguousarray(np.tile(inp["cache_mla_krope"][b, l].T, (4, 1))) for l in range(NL)])
        m["ckgT"] = np.stack([np.ascontiguousarray(inp["cache_gqa_k"][b, l].reshape(512, 128).T) for l in range(NL)])
        m["cvg"] = np.stack([fm(inp["cache_gqa_v"][b, l].reshape(512, 128), 4) for l in range(NL)])
        maps.append(m)
    return maps


_NC = None


def kernel(**inputs):
    global _NC
    inp = {k: np.asarray(v) for k, v in inputs.items()}
    maps = _prep(inp)
    if _NC is None:
        _NC = build_program()
    res = run_bass_kernel_spmd(_NC, maps, core_ids=list(range(8)))
    R = res.results
    y_prompt = np.zeros((32, 256, 1024), np.float32)
    y_sample = np.zeros((2, 4096, 1024), np.float32)
    outs = {k: np.zeros(s, np.float32) for k, s in (("o_dk", (32, NL, 256, 4, 64)), ("o_dv", (32, NL, 256, 4, 64)),
                                                     ("o_ckv", (32, NL, 256, 128)), ("o_kr", (32, NL, 256, 32)),
                                                     ("o_gk", (32, NL, 256, 2, 64)), ("o_gv", (32, NL, 256, 2, 64)))}
    for i in range(8):
        b, r = i // 4, i % 4
        yT = np.asarray(R[i]["yT"]).reshape(128, 8, T)
        y = yT.transpose(2, 1, 0).reshape(T, 1024)
        y_prompt[4 * i:4 * i + 4] = y[0:TP].reshape(4, 256, 1024)
        y_sample[b, r * 1024:(r + 1) * 1024] = y[TP:]
        for k in outs:
            outs[k][4 * i:4 * i + 4] = np.asarray(R[i][k]).reshape(outs[k][4 * i:4 * i + 4].shape)
    return (y_prompt, y_sample, outs["o_dk"], outs["o_dv"], outs["o_ckv"], outs["o_kr"], outs["o_gk"], outs["o_gv"])
```
